# Optimizing a Trainium2 kernel written in Bass

```python
import math
import jax, jax.numpy as jnp
from jax import lax
import numpy as np

D_MODEL = 1024
BATCH = 4
SEQ = 4096
DEPTH = 4

MEM_LEN = 256
EPS = 1e-6

GLA_HEADS = 4
GLA_DV = 96
GLA_DK = 48
GLA_QK = GLA_HEADS * GLA_DK
GLA_WIDTH = GLA_HEADS * GLA_DV
GLA_GATE_RANK = 16
GLA_TAU = 16.0
GLA_CHUNK = 64

CONV_WIDTH = 256
CONV_K = 3

MLA_HEADS = 6
MLA_NOPE = 64
MLA_ROPE = 32
MLA_V = 64
MLA_Q_RANK = 256
MLA_KV_RANK = 256
MLA_WIDTH = MLA_HEADS * MLA_V
ROPE_BASE = 10000.0
Q_BLOCK = 128

D_MIX = GLA_WIDTH + CONV_WIDTH + MLA_WIDTH

MEM_HEADS = 4
MEM_HEAD_DIM = 128
MEM_INNER = MEM_HEADS * MEM_HEAD_DIM

IN_SPLITS = (GLA_QK, GLA_QK, GLA_WIDTH, GLA_GATE_RANK, GLA_WIDTH,
             CONV_WIDTH, CONV_WIDTH, CONV_WIDTH, CONV_WIDTH,
             MLA_Q_RANK, MLA_KV_RANK, MLA_ROPE, MLA_WIDTH)
IN_WIDTH = sum(IN_SPLITS)

kernel_name = 'hybrid_gla_shortconv_mla_memory_trunk'


def _split_points():
    pts, acc = [], 0
    for s in IN_SPLITS[:-1]:
        acc += s
        pts.append(acc)
    return pts


def rms_norm(x, g):
    xf = x.astype(jnp.float32)
    y = xf * lax.rsqrt(jnp.mean(xf * xf, axis=-1, keepdims=True) + EPS)
    return (y * g.astype(jnp.float32)).astype(x.dtype)


def rope_tables(positions):
    inv_freq = 1.0 / (ROPE_BASE ** (jnp.arange(0, MLA_ROPE, 2, dtype=jnp.float32) / MLA_ROPE))
    ang = positions.astype(jnp.float32)[..., None] * inv_freq
    return jnp.cos(ang), jnp.sin(ang)


def apply_rope(x, cos, sin):
    half = MLA_ROPE // 2
    xf = x.astype(jnp.float32)
    x1, x2 = xf[..., :half], xf[..., half:]
    c, s = cos[:, :, None, :], sin[:, :, None, :]
    return jnp.concatenate([x1 * c - x2 * s, x2 * c + x1 * s], axis=-1).astype(x.dtype)


def gla_chunked(q, k, v, log_a):
    B, S, H, dk = q.shape
    dv = v.shape[-1]
    C = GLA_CHUNK
    N = S // C

    def chunks(t):
        return t.astype(jnp.float32).reshape(B, N, C, H, t.shape[-1]).transpose(0, 3, 1, 2, 4)

    qf = chunks(q) * (dk ** -0.5)
    kf, vf, la = chunks(k), chunks(v), chunks(log_a)
    b = jnp.cumsum(la, axis=3)
    b_last = b[:, :, :, -1:, :]
    q_dec = qf * jnp.exp(b)
    k_inv = kf * jnp.exp(-b)
    k_end = kf * jnp.exp(b_last - b)
    mask = jnp.tril(jnp.ones((C, C), dtype=bool))
    A = jnp.einsum('bhncd,bhnjd->bhncj', q_dec, k_inv)
    A = jnp.where(mask, A, 0.0)
    o_intra = jnp.einsum('bhncj,bhnjv->bhncv', A, vf)
    U = jnp.einsum('bhncd,bhncv->bhndv', k_end, vf)
    decay = jnp.exp(b_last[:, :, :, 0, :])

    def step(state, inp):
        u, d = inp
        return d[..., None] * state + u, state

    init = jnp.zeros((B, H, dk, dv), jnp.float32)
    _, s_prev = lax.scan(step, init, (U.transpose(2, 0, 1, 3, 4), decay.transpose(2, 0, 1, 3)))
    s_prev = s_prev.transpose(1, 2, 0, 3, 4)
    o_inter = jnp.einsum('bhncd,bhndv->bhncv', q_dec, s_prev)
    o = o_intra + o_inter
    return o.transpose(0, 2, 3, 1, 4).reshape(B, S, H, dv)


def causal_attention_blocks(q, k, v, scale):
    B, S, H, D = q.shape
    Dv = v.shape[-1]
    nb = S // Q_BLOCK
    qb = q.reshape(B, nb, Q_BLOCK, H, D).transpose(1, 0, 2, 3, 4)
    key_pos = jnp.arange(S)

    def one_block(args):
        qblk, start = args
        s = jnp.einsum('bqhd,bkhd->bhqk', qblk, k).astype(jnp.float32) * scale
        qpos = start + jnp.arange(Q_BLOCK)
        allowed = key_pos[None, :] <= qpos[:, None]
        s = jnp.where(allowed, s, -1e30)
        p = jax.nn.softmax(s, axis=-1).astype(v.dtype)
        return jnp.einsum('bhqk,bkhd->bqhd', p, v)

    out = lax.map(one_block, (qb, jnp.arange(nb) * Q_BLOCK))
    return out.transpose(1, 0, 2, 3, 4).reshape(B, S, H, Dv)


def hybrid_mixer(xn, cos, sin, w_in, gla_w_gate, gla_b_gate, gla_norm, conv_w,
                 mla_q_norm, mla_w_uq, mla_kv_norm, mla_w_ukv, w_out):
    B, S, _ = xn.shape
    proj = xn @ w_in
    (gq, gk, gv, glr, ggate, cc, cb, ch, cgate,
     cq, ckv, kr, mgate) = jnp.split(proj, _split_points(), axis=-1)

    z = glr @ gla_w_gate + gla_b_gate
    log_a = (jax.nn.log_sigmoid(z.astype(jnp.float32)) / GLA_TAU).reshape(B, S, GLA_HEADS, GLA_DK)
    o_gla = gla_chunked(gq.reshape(B, S, GLA_HEADS, GLA_DK),
                        gk.reshape(B, S, GLA_HEADS, GLA_DK),
                        gv.reshape(B, S, GLA_HEADS, GLA_DV), log_a)
    o_gla = rms_norm(o_gla, gla_norm).astype(xn.dtype).reshape(B, S, GLA_WIDTH)
    o_gla = o_gla * jax.nn.silu(ggate)

    u = cc * ch
    conv = lax.conv_general_dilated(u, conv_w[:, None, :].astype(u.dtype), window_strides=(1,),
                                    padding=[(CONV_K - 1, 0)],
                                    dimension_numbers=('NWC', 'WIO', 'NWC'),
                                    feature_group_count=CONV_WIDTH)
    o_conv = cb * conv * jax.nn.silu(cgate)

    qh = (rms_norm(cq, mla_q_norm) @ mla_w_uq).reshape(B, S, MLA_HEADS, MLA_NOPE + MLA_ROPE)
    q_full = jnp.concatenate([qh[..., :MLA_NOPE], apply_rope(qh[..., MLA_NOPE:], cos, sin)], axis=-1)
    kvh = (rms_norm(ckv, mla_kv_norm) @ mla_w_ukv).reshape(B, S, MLA_HEADS, MLA_NOPE + MLA_V)
    k_rope = apply_rope(kr.reshape(B, S, 1, MLA_ROPE), cos, sin)
    k_full = jnp.concatenate([kvh[..., :MLA_NOPE],
                              jnp.broadcast_to(k_rope, (B, S, MLA_HEADS, MLA_ROPE))], axis=-1)
    v_mla = kvh[..., MLA_NOPE:]
    o_mla = causal_attention_blocks(q_full, k_full, v_mla, 1.0 / math.sqrt(MLA_NOPE + MLA_ROPE))
    o_mla = o_mla.reshape(B, S, MLA_WIDTH) * jax.nn.silu(mgate)

    return jnp.concatenate([o_gla, o_conv, o_mla], axis=-1) @ w_out


def memory_cross_attention(hn, mem, norm_mem, wq, wk, wv, wo):
    B, S, _ = hn.shape
    M = mem.shape[1]
    memn = rms_norm(mem, norm_mem)
    q = (hn @ wq).reshape(B, S, MEM_HEADS, MEM_HEAD_DIM)
    k = (memn @ wk).reshape(B, M, MEM_HEADS, MEM_HEAD_DIM)
    v = (memn @ wv).reshape(B, M, MEM_HEADS, MEM_HEAD_DIM)
    s = jnp.einsum('bshd,bmhd->bhsm', q, k).astype(jnp.float32) / math.sqrt(MEM_HEAD_DIM)
    p = jax.nn.softmax(s, axis=-1).astype(v.dtype)
    o = jnp.einsum('bhsm,bmhd->bshd', p, v).reshape(B, S, MEM_INNER)
    return o @ wo


def setup_inputs(seed: int = 0) -> dict:
    key = jax.random.key(seed)
    ks = jax.random.split(key, 24)
    f32 = jnp.float32

    def dense(k, fan_in, fan_out):
        return jax.random.normal(k, (DEPTH, fan_in, fan_out), f32) * (fan_in ** -0.5)

    def gain(k, n):
        return 1.0 + 0.02 * jax.random.normal(k, (DEPTH, n), f32)

    x = jax.random.normal(ks[0], (BATCH, SEQ, D_MODEL), f32)
    mem = jax.random.normal(ks[1], (BATCH, MEM_LEN, D_MODEL), f32)
    offset = jax.random.randint(ks[2], (BATCH, 1), 0, 1024, dtype=jnp.int32)
    positions = jnp.arange(SEQ, dtype=jnp.int32)[None, :] + offset
    return {
        'x': x,
        'mem': mem,
        'positions': positions,
        'norm_mix': gain(ks[3], D_MODEL),
        'w_in': dense(ks[4], D_MODEL, IN_WIDTH),
        'gla_w_gate': dense(ks[5], GLA_GATE_RANK, GLA_QK),
        'gla_b_gate': 0.02 * jax.random.normal(ks[6], (DEPTH, GLA_QK), f32),
        'gla_norm': gain(ks[7], GLA_DV),
        'conv_w': jax.random.normal(ks[8], (DEPTH, CONV_K, CONV_WIDTH), f32) * (CONV_K ** -0.5),
        'mla_q_norm': gain(ks[9], MLA_Q_RANK),
        'mla_w_uq': dense(ks[10], MLA_Q_RANK, MLA_HEADS * (MLA_NOPE + MLA_ROPE)),
        'mla_kv_norm': gain(ks[11], MLA_KV_RANK),
        'mla_w_ukv': dense(ks[12], MLA_KV_RANK, MLA_HEADS * (MLA_NOPE + MLA_V)),
        'w_out': dense(ks[13], D_MIX, D_MODEL),
        'norm_xattn': gain(ks[14], D_MODEL),
        'norm_mem': gain(ks[15], D_MODEL),
        'mem_wq': dense(ks[16], D_MODEL, MEM_INNER),
        'mem_wk': dense(ks[17], D_MODEL, MEM_INNER),
        'mem_wv': dense(ks[18], D_MODEL, MEM_INNER),
        'mem_wo': dense(ks[19], MEM_INNER, D_MODEL),
        'norm_final': 1.0 + 0.02 * jax.random.normal(ks[20], (D_MODEL,), f32),
    }


def reference(x, mem, positions, norm_mix, w_in, gla_w_gate, gla_b_gate, gla_norm, conv_w,
              mla_q_norm, mla_w_uq, mla_kv_norm, mla_w_ukv, w_out, norm_xattn, norm_mem,
              mem_wq, mem_wk, mem_wv, mem_wo, norm_final):
    cos, sin = rope_tables(positions)
    h = x
    for l in range(DEPTH):
        h = h + hybrid_mixer(rms_norm(h, norm_mix[l]), cos, sin, w_in[l], gla_w_gate[l],
                             gla_b_gate[l], gla_norm[l], conv_w[l], mla_q_norm[l], mla_w_uq[l],
                             mla_kv_norm[l], mla_w_ukv[l], w_out[l])
        h = h + memory_cross_attention(rms_norm(h, norm_xattn[l]), mem, norm_mem[l],
                                       mem_wq[l], mem_wk[l], mem_wv[l], mem_wo[l])
    return rms_norm(h, norm_final)
```

```python
import math
from contextlib import ExitStack

import numpy as np
import concourse.bass as bass
import concourse.mybir as mybir
from concourse.bass_utils import run_bass_kernel_spmd

F32 = mybir.dt.float32
BF16 = mybir.dt.bfloat16
I32 = mybir.dt.int32
AF = mybir.ActivationFunctionType
ALU = mybir.AluOpType

D_MODEL = 1024
DEPTH = 4
MEM_LEN = 256
EPS = 1e-6
GLA_H, GLA_DV, GLA_DK, GLA_RANK, GLA_TAU = 4, 96, 48, 16, 16.0
CONV_W = 256
MLA_H, MLA_NOPE, MLA_ROPE, MLA_V = 6, 64, 32, 64
MEM_H, MEM_HD = 4, 128
O_GQ, O_GK, O_GV, O_GLR, O_GG = 0, 192, 384, 768, 784
O_CC, O_CB, O_CH, O_CG = 1168, 1424, 1680, 1936
O_CQ, O_CKV, O_KR, O_MG = 2192, 2448, 2704, 2736

COMPUTE = ("pe", "act", "dve", "pool")
ALL_ENG = ("pe", "act", "dve", "pool", "sp")
N_DMA_SEMS = 24


class Op:
    __slots__ = ("eng", "fn", "reads", "writes", "dma", "waits", "count", "marked", "slot", "pre_wait", "cost", "lat", "cc")

    def __init__(self, eng, fn, reads, writes, dma, cost, lat):
        self.eng, self.fn, self.reads, self.writes, self.dma = eng, fn, reads, writes, dma
        self.waits = []
        self.count = 0
        self.marked = False
        self.slot = None
        self.pre_wait = None
        self.cost = cost
        self.lat = lat
        self.cc = False


DEF_COST = {"pe": 0.3, "act": 0.5, "dve": 0.6, "pool": 0.7, "sp": 0.05}
WINDOW = {"pe": 256, "act": 192, "dve": 192, "pool": 64, "sp": 48}
REORDER = True
STRICT_SAME_ENGINE = True


class Sched:
    def __init__(self, nc):
        self.nc = nc
        self.ops = []

    def add(self, eng, fn, reads=(), writes=(), cost=None):
        self.ops.append(Op(eng, fn, tuple(reads), tuple(writes), False, DEF_COST[eng] if cost is None else cost, 0.0))

    def dma(self, eng, fn, reads=(), writes=(), lat=3.0):
        self.ops.append(Op(eng, fn, tuple(reads), tuple(writes), True, 0.08 if eng in ("sp", "act") else 0.5, lat))

    def coll(self, fn, reads=(), writes=(), lat=40.0):
        op = Op("pool", fn, tuple(reads), tuple(writes), True, 1.0, lat)
        op.cc = True
        self.ops.append(op)

    def analyze(self):
        ops = self.ops
        n = len(ops)
        last_writer, readers, bank_last = {}, {}, {}
        deps_all = []
        for i, op in enumerate(ops):
            deps = set()
            for k in op.reads:
                j = last_writer.get(k)
                if j is not None:
                    deps.add(j)
            for k in op.writes:
                j = last_writer.get(k)
                if j is not None:
                    deps.add(j)
                for r in readers.get(k, ()):
                    deps.add(r)
            banks = set()
            for k in op.reads + op.writes:
                if isinstance(k, tuple) and k[0] == "ps":
                    banks.add(k[1])
            for b in banks:
                d = bank_last.setdefault(b, {})
                for e, j in d.items():
                    if e != op.eng:
                        deps.add(j)
                d[op.eng] = i
            for k in op.reads:
                readers.setdefault(k, []).append(i)
            for k in op.writes:
                last_writer[k] = i
                readers[k] = []
            deps.discard(i)
            deps_all.append(deps)
        per_eng = {e: [] for e in ALL_ENG}
        for i, op in enumerate(ops):
            per_eng[op.eng].append(i)
        order = {e: [] for e in ALL_ENG}
        if not REORDER:
            order = per_eng
        else:
            users = [[] for _ in range(n)]
            ndep = [0] * n
            for i in range(n):
                ndep[i] = len(deps_all[i])
                for j in deps_all[i]:
                    users[j].append(i)
            ready = [0.0] * n
            done = [None] * n
            pend = {e: list(per_eng[e]) for e in ALL_ENG}
            tfree = {e: 0.0 for e in ALL_ENG}
            remaining = n
            while remaining:
                best = None
                for e in ALL_ENG:
                    pl = pend[e]
                    if not pl:
                        continue
                    te = tfree[e]
                    w = WINDOW[e]
                    cand = None
                    for pos in range(min(w, len(pl))):
                        i = pl[pos]
                        if ndep[i]:
                            continue
                        st = ready[i] if ready[i] > te else te
                        if cand is None or st < cand[0] - 1e-9:
                            cand = (st, pos, i)
                            if st <= te:
                                break
                    if cand is not None and (best is None or cand[0] < best[0] - 1e-9):
                        best = (cand[0], e, cand[1], cand[2])
                st, e, pos, i = best
                op = ops[i]
                pend[e].pop(pos)
                order[e].append(i)
                tfree[e] = st + op.cost
                done[i] = st + op.cost + op.lat + 0.2
                for u in users[i]:
                    ndep[u] -= 1
                    if done[i] > ready[u]:
                        ready[u] = done[i]
                remaining -= 1
            self.sim_time = max(d for d in done if d is not None)
        self.order = order
        pos_of = [0] * n
        for e in ALL_ENG:
            for p, i in enumerate(order[e]):
                pos_of[i] = p
        known = {e: {p: -1 for p in COMPUTE} for e in ALL_ENG}
        known_dma = {e: set() for e in ALL_ENG}
        for e in ALL_ENG:
            for i in order[e]:
                op = ops[i]
                best, dmas = {}, []
                for j in deps_all[i]:
                    pj = ops[j]
                    if pj.dma:
                        dmas.append(j)
                        continue
                    if pj.eng == op.eng and not op.dma:
                        if op.eng == "pe":
                            continue
                        if not STRICT_SAME_ENGINE and not any(k in pj.writes for k in op.reads):
                            continue
                    if pj.eng not in best or pos_of[best[pj.eng]] < pos_of[j]:
                        best[pj.eng] = j
                w = []
                for pe_, j in sorted(best.items()):
                    if known[e][pe_] >= pos_of[j]:
                        continue
                    known[e][pe_] = pos_of[j]
                    ops[j].marked = True
                    w.append(j)
                for j in sorted(dmas):
                    if j in known_dma[e]:
                        continue
                    known_dma[e].add(j)
                    w.append(j)
                op.waits = w
        cnt = {e: 0 for e in COMPUTE}
        dcnt = {e: 0 for e in ALL_ENG}
        ccn = [0]
        for e in ALL_ENG:
            for i in order[e]:
                op = ops[i]
                if op.cc:
                    ccn[0] += 1
                    op.count = ccn[0]
                elif op.dma:
                    d = dcnt[e]
                    dcnt[e] += 1
                    op.slot = d % N_DMA_SEMS
                    op.count = 16 * (d // N_DMA_SEMS + 1)
                    if d >= N_DMA_SEMS:
                        op.pre_wait = (op.slot, 16 * (d // N_DMA_SEMS))
                elif op.marked:
                    cnt[e] += 1
                    op.count = cnt[e]
        self.stats = dict(n_ops=len(ops), marked=dict(cnt), dmas=dict(dcnt),
                          waits=sum(len(o.waits) for o in ops), sim_us=getattr(self, "sim_time", None))

    def emit(self, ctx):
        nc = self.nc
        self.analyze()
        ops = self.ops
        order = self.order
        sems = {e: ctx.enter_context(nc.semaphore("s_" + e)) for e in COMPUTE}
        dma_engs = sorted({op.eng for op in ops if op.dma})
        dsems = {e: [ctx.enter_context(nc.semaphore("d_%s_%d" % (e, s))) for s in range(N_DMA_SEMS)]
                 for e in dma_engs}
        cc_sem = ctx.enter_context(nc.semaphore("cc_sem"))
        block = ctx.enter_context(nc.Block())

        def run(eng_name, eng):
            for i in order[eng_name]:
                op = ops[i]
                for j in op.waits:
                    pj = ops[j]
                    if pj.cc:
                        eng.wait_ge(cc_sem, pj.count)
                    elif pj.dma:
                        eng.wait_ge(dsems[pj.eng][pj.slot], pj.count)
                    else:
                        eng.wait_ge(sems[pj.eng], pj.count)
                if op.cc:
                    op.fn(eng).then_inc(cc_sem, 1)
                elif op.dma:
                    if op.pre_wait is not None:
                        eng.wait_ge(dsems[op.eng][op.pre_wait[0]], op.pre_wait[1])
                    op.fn(eng).then_inc(dsems[op.eng][op.slot], 16)
                elif op.fn is not None:
                    ins = op.fn(eng)
                    if op.marked:
                        ins.then_inc(sems[op.eng], 1)

        deco = {"pe": block.tensor, "act": block.scalar, "dve": block.vector,
                "pool": block.gpsimd, "sp": block.sync}
        for e in ALL_ENG:
            if order[e]:
                deco[e](lambda eng, e=e: run(e, eng))


def _slab_table():
    t = [("glrkr", 8), ("krrot", 8)]
    for i in range(5):
        t.append(("kvtok%d" % i, 8))
    for h in range(4):
        t.append(("gq%d" % h, 8))
    for h in range(4):
        t.append(("gk%d" % h, 8))
    for h in range(4):
        t.append(("gg%d" % h, 8))
    for i in range(2):
        for nm in ("ch", "cc", "cg", "cb"):
            t.append(("%s%d" % (nm, i), 8))
    for nm in ("cq", "ckv"):
        for i in range(2):
            t.append(("%s%d" % (nm, i), 8))
    for i in range(3):
        t.append(("mg%d" % i, 8))
    for i in range(8):
        t.append(("wout%d" % i, 9))
    for i in range(4):
        t.append(("wq%d" % i, 8))
    for i in range(8):
        t.append(("wo%d" % i, 4))
    for i in range(4):
        t.append(("wk%d" % i, 8))
    for i in range(4):
        t.append(("wv%d" % i, 8))
    return t


SLABS = _slab_table()
SLAB_OFF = {}
_o = 0
for _n, _k in SLABS:
    SLAB_OFF[_n] = (_o, _k)
    _o += _k
TOTK = _o

VC_PER = 35
V_NMIX, V_NX, V_NMEM, V_CONV, V_QN, V_KVN, V_GN = 0, 8, 16, 24, 30, 32, 34


def _in_cols():
    c = {}
    ar = np.arange
    for h in range(4):
        c["gq%d" % h] = ar(O_GQ + 48 * h, O_GQ + 48 * h + 48)
        c["gk%d" % h] = ar(O_GK + 48 * h, O_GK + 48 * h + 48)
        c["gg%d" % h] = ar(O_GG + 96 * h, O_GG + 96 * h + 96)
    c["glrkr"] = np.concatenate([ar(O_GLR, O_GLR + 16), ar(0, 48), ar(O_KR, O_KR + 32)])
    c["krrot"] = np.concatenate([ar(0, 64), ar(O_KR + 16, O_KR + 32), ar(O_KR, O_KR + 16)])
    for i in range(2):
        c["cc%d" % i] = ar(O_CC + 128 * i, O_CC + 128 * i + 128)
        c["cb%d" % i] = ar(O_CB + 128 * i, O_CB + 128 * i + 128)
        c["ch%d" % i] = ar(O_CH + 128 * i, O_CH + 128 * i + 128)
        c["cg%d" % i] = ar(O_CG + 128 * i, O_CG + 128 * i + 128)
        c["cq%d" % i] = ar(O_CQ + 128 * i, O_CQ + 128 * i + 128)
        c["ckv%d" % i] = ar(O_CKV + 128 * i, O_CKV + 128 * i + 128)
    for i in range(3):
        c["mg%d" % i] = ar(O_MG + 128 * i, O_MG + 128 * i + 128)
    kv = np.concatenate([ar(O_GK, O_GK + 192), ar(O_GV, O_GV + 384), ar(0, 64)])
    for i in range(5):
        c["kvtok%d" % i] = kv[128 * i:128 * i + 128]
    return c


def _pad128(a):
    n = a.shape[-1]
    if n == 128:
        return a
    pad = np.take(a, np.arange(128 - n) % n, axis=-1)
    return np.concatenate([a, pad], axis=-1)


def _kchunks(w, nk):
    return np.ascontiguousarray(w.reshape(nk, 128, 128).transpose(1, 0, 2))


def prep_weights(inp, layers, rank, NB):
    f = np.float32
    depth = len(layers)
    wsl = np.empty((depth, 128, TOTK, 128), f)
    cols = _in_cols()
    for l, gl in enumerate(layers):
        w_in = np.asarray(inp["w_in"][gl], f)
        for name, idx in cols.items():
            off, nk = SLAB_OFF[name]
            wsl[l, :, off:off + nk, :] = _kchunks(_pad128(w_in[:, idx]), 8)
        wk, wv, wq = (np.asarray(inp[k][gl], f) for k in ("mem_wk", "mem_wv", "mem_wq"))
        for i in range(4):
            for nm, w in (("wk", wk), ("wv", wv), ("wq", wq)):
                off, nk = SLAB_OFF["%s%d" % (nm, i)]
                wsl[l, :, off:off + nk, :] = _kchunks(w[:, 128 * i:128 * i + 128], 8)
        wo = np.asarray(inp["mem_wo"][gl], f)
        wout = np.asarray(inp["w_out"][gl], f)
        rows = []
        for h in range(4):
            r = np.arange(96 * h, 96 * h + 96)
            rows.append(np.concatenate([r, r[:32]]))
        for i in range(5):
            rows.append(np.arange(384 + 128 * i, 384 + 128 * i + 128))
        rows = np.concatenate(rows)
        woutr = wout[rows, :]
        for i in range(8):
            off, nk = SLAB_OFF["wout%d" % i]
            wsl[l, :, off:off + nk, :] = _kchunks(woutr[:, 128 * i:128 * i + 128], 9)
            off, nk = SLAB_OFF["wo%d" % i]
            wsl[l, :, off:off + nk, :] = _kchunks(wo[:, 128 * i:128 * i + 128], 4)
    wsl = wsl.reshape(depth, 128, TOTK * 128)
    wuq = np.empty((depth, 128, 2, 2, 576), f)
    wukv = np.empty((depth, 128, 2, 768), f)
    rot = np.concatenate([np.concatenate([np.arange(96 * h, 96 * h + 64), np.arange(96 * h + 80, 96 * h + 96),
                                          np.arange(96 * h + 64, 96 * h + 80)]) for h in range(6)])
    kvc = np.concatenate([np.concatenate([np.arange(128 * h, 128 * h + 64) for h in range(6)]),
                          np.concatenate([np.arange(128 * h + 64, 128 * h + 128) for h in range(6)])])
    wgate = np.empty((depth, 17, 192), f)
    for l, gl in enumerate(layers):
        u = np.asarray(inp["mla_w_uq"][gl], f)
        wuq[l, :, 0] = u.reshape(2, 128, 576).transpose(1, 0, 2)
        wuq[l, :, 1] = u[:, rot].reshape(2, 128, 576).transpose(1, 0, 2)
        wukv[l] = np.asarray(inp["mla_w_ukv"][gl], f)[:, kvc].reshape(2, 128, 768).transpose(1, 0, 2)
        wgate[l, :16] = np.asarray(inp["gla_w_gate"][gl], f)
        wgate[l, 16] = np.asarray(inp["gla_b_gate"][gl], f)
    nv = VC_PER * depth + 14 + NB
    vecs = np.zeros((128, nv), f)

    def colmaj(v, n):
        return np.asarray(v, f).reshape(n, 128).T

    for l, gl in enumerate(layers):
        b = VC_PER * l
        vecs[:, b + V_NMIX:b + V_NMIX + 8] = colmaj(inp["norm_mix"][gl], 8)
        vecs[:, b + V_NX:b + V_NX + 8] = colmaj(inp["norm_xattn"][gl], 8)
        vecs[:, b + V_NMEM:b + V_NMEM + 8] = colmaj(inp["norm_mem"][gl], 8)
        cw = np.asarray(inp["conv_w"][gl], f)
        for i in range(2):
            vecs[:, b + V_CONV + 3 * i:b + V_CONV + 3 * i + 3] = cw[:, 128 * i:128 * i + 128].T
        vecs[:, b + V_QN:b + V_QN + 2] = colmaj(inp["mla_q_norm"][gl], 2)
        vecs[:, b + V_KVN:b + V_KVN + 2] = colmaj(inp["mla_kv_norm"][gl], 2)
        vecs[:96, b + V_GN] = np.asarray(inp["gla_norm"][gl], f)
    b = VC_PER * depth
    last = (rank == 1)
    vecs[:, b:b + 8] = colmaj(inp["norm_final"], 8) if last else 1.0
    vecs[:, b + 10] = 0.0 if last else 1.0
    vecs[:, b + 11] = 1.0 if last else 0.0
    vecs[:, b + 12] = 1.0 if last else 0.0
    vecs[:, b + 13] = 0.0 if last else 1.0
    vecs[:, b + 14:b + 14 + NB] = 1.0
    if last:
        vecs[:, b + 14] = 0.0
    inv_freq = (1.0 / (10000.0 ** (np.arange(0, 32, 2, dtype=np.float32) / np.float32(32)))).astype(f)
    invf2 = np.concatenate([inv_freq, inv_freq])
    sgn = np.concatenate([-np.ones(16, f), np.ones(16, f)])
    vecs[64:96, b + 8] = invf2
    vecs[64:96, b + 9] = invf2 * sgn
    return dict(wsl=wsl, wuq=wuq.reshape(depth, 128, 2 * 2 * 576), wukv=wukv.reshape(depth, 128, 2 * 768),
                wgate=wgate, vecs=vecs)


def build_program(T, depth, TB=512, debug=False, ncores=8):
    NT = TB // 128
    NB = T // TB + 1
    T = NB * TB
    NTT = T // 128
    nc = bass.Bass("TRN2", target_bir_lowering=False)
    NV = VC_PER * depth + 14 + NB
    x_d = nc.dram_tensor("x", [T, 1024], F32, kind="ExternalInput").ap()
    mem_d = nc.dram_tensor("mem", [MEM_LEN, 1024], F32, kind="ExternalInput").ap()
    pos_d = nc.dram_tensor("pos", [1, T], I32, kind="ExternalInput").ap()
    wsl_d = nc.dram_tensor("wsl", [depth, 128, TOTK * 128], F32, kind="ExternalInput").ap()
    wuq_d = nc.dram_tensor("wuq", [depth, 128, 2 * 2 * 576], F32, kind="ExternalInput").ap()
    wukv_d = nc.dram_tensor("wukv", [depth, 128, 2 * 768], F32, kind="ExternalInput").ap()
    wgate_d = nc.dram_tensor("wgate", [depth, 17, 192], F32, kind="ExternalInput").ap()
    vecs_d = nc.dram_tensor("vecs", [128, NV], F32, kind="ExternalInput").ap()
    out_d = nc.dram_tensor("out", [T, 1024], F32, kind="ExternalOutput").ap()
    dbg_d = nc.dram_tensor("dbg", [128, 9 * TB], BF16, kind="ExternalOutput").ap() if debug else None
    wsc_d = nc.dram_tensor("wsc", [depth, 128, TOTK * 128], BF16).ap()
    xsend_d = nc.dram_tensor("xsend", [NB * 1024, TB], F32).ap()
    xrecv_d = nc.dram_tensor("xrecv", [NB * 2 * 1024, TB], F32).ap()
    kscr_d = nc.dram_tensor("kscr", [depth, 6, 96, T], BF16).ap()
    vscr_d = nc.dram_tensor("vscr", [depth, 6, 128, NTT * 65], BF16).ap()
    rope_d = nc.dram_tensor("ropetab", [2, 32, T], F32).ap()

    ctx = ExitStack()
    with ctx:
        def sb(name, shape, dt):
            return ctx.enter_context(nc.sbuf_tensor("sb_" + name, shape, dt))

        S = Sched(nc)
        A = S.add
        banks = [ctx.enter_context(nc.psum_tensor("bank%d" % i, [128, 512], F32)) for i in range(8)]

        def PK(b):
            return ("ps", b)

        rot = {"mm": [0, 1, 2], "st": [3, 4, 7], "o": [5, 6]}
        rot_i = {"mm": 0, "st": 0, "o": 0}

        def nextb(kind):
            b = rot[kind][rot_i[kind] % len(rot[kind])]
            rot_i[kind] += 1
            return b

        ident_f = sb("ident_f", [128, 128], F32)
        ident_b = sb("ident_b", [128, 128], BF16)
        ones_b = sb("ones_b", [128, 128], BF16)
        triSL = sb("triSL", [128, 128], F32)
        triU = sb("triU", [128, 128], F32)
        maskA = sb("maskA", [128, 128], BF16)
        vecs = sb("vecs", [128, NV], F32)
        A("pool", lambda e: e.memset(ident_f[:], 0.0), writes=["ident_f"])
        A("pool", lambda e: e.affine_select(out=ident_f[:], in_=ident_f[:], pattern=[[-1, 128]],
                                            compare_op=ALU.not_equal, fill=1.0, base=0, channel_multiplier=1),
          reads=["ident_f"], writes=["ident_f"])
        A("pool", lambda e: e.tensor_copy(out=ident_b[:], in_=ident_f[:]), reads=["ident_f"], writes=["ident_b"])
        A("pool", lambda e: e.memset(ones_b[:], 1.0), writes=["ones_b"])
        A("pool", lambda e: e.memset(ones_f[:], 1.0), writes=["ones_f"])
        A("pool", lambda e: e.memset(triU[:], 1.0), writes=["triU"])
        A("pool", lambda e: e.affine_select(out=triU[:], in_=triU[:], pattern=[[1, 128]], compare_op=ALU.is_ge,
                                            fill=0.0, base=0, channel_multiplier=-1),
          reads=["triU"], writes=["triU"])
        A("pool", lambda e: e.tensor_copy(out=maskA[:], in_=triU[:]), reads=["triU"], writes=["maskA"])
        A("pool", lambda e: e.memset(triSL[:], 1.0), writes=["triSL"])
        A("pool", lambda e: e.affine_select(out=triSL[:], in_=triSL[:], pattern=[[-1, 128]], compare_op=ALU.is_gt,
                                            fill=0.0, base=0, channel_multiplier=1),
          reads=["triSL"], writes=["triSL"])
        S.dma("sp", lambda e: e.dma_start(out=vecs[:], in_=vecs_d[:, :]), writes=["vecs"])

        def vcol(l, off, n=1, rows=slice(0, 128)):
            b = VC_PER * l + off
            return vecs[rows, b:b + n]

        VB = VC_PER * depth

        NRING = 6
        ring = sb("ring", [128, NRING, 9 * 128], BF16)
        ring_i = [0]

        def load_slab(l, name):
            off, nk = SLAB_OFF[name]
            s = ring_i[0] % NRING
            ring_i[0] += 1
            pcs = sorted({min(kk // CSTEP, NPC - 1) for kk in (off, off + nk - 1)})
            S.dma("sp", lambda e: e.dma_start(out=ring[:, s, 0:nk * 128],
                                              in_=wsc_d[l, :, off * 128:(off + nk) * 128]),
                  reads=[("wsc", l, i) for i in range(pcs[0], pcs[-1] + 1)], writes=[("ring", s)])
            return s

        def slab(s, k, m0=0, m1=128, rows=128):
            return ring[0:rows, s, k * 128 + m0:k * 128 + m1]

        NPC = 16
        CSTEP = TOTK // NPC

        def cast_weights(l):
            n = TOTK * 128
            step = CSTEP * 128
            for i in range(NPC):
                a, b = i * step, (n if i == NPC - 1 else (i + 1) * step)
                gi = l * NPC + i
                S.dma("pool", lambda e, a=a, b=b: e.dma_start(out=wsc_d[l, :, a:b], in_=wsl_d[l, :, a:b]),
                      reads=([("castchain", gi - 3)] if gi >= 3 else []), writes=[("wsc", l, i), ("castchain", gi)], lat=30.0)

        wuqs = [sb("wuq%d" % i, [128, 2, 2, 576], BF16) for i in range(depth)]
        wukvs = [sb("wukv%d" % i, [128, 2, 768], BF16) for i in range(depth)]
        wgates = [sb("wgate%d" % i, [17, 192], BF16) for i in range(depth)]
        hTs = [sb("hT0", [128, 8, TB], F32), sb("hT1", [128, 8, TB], F32)]
        xnT = sb("xnT", [128, 8, TB], BF16)
        hnT = xnT
        ones_f = sb("ones_f", [128, 64], F32)
        otn = sb("otn", [128, TB], BF16)
        sqr = sb("sqr", [128, 3, TB], BF16)
        rstd = sb("rstd", [128, TB], F32)
        catT = sb("catT", [128, 9, TB], BF16)
        tok32 = sb("tok32", [128, 2, 1024], F32)
        memnT = catT[:, 0:4, :].rearrange("p k t -> p (k t)")[:, 0:8 * MEM_LEN].rearrange("p (k t) -> p k t", k=8)
        kmTs = [sb("kmT%d" % i, [128, 4, MEM_LEN], BF16) for i in range(depth)]
        vms = [sb("vm%d" % i, [128, 2, 512], BF16) for i in range(depth)]
        ucar = sb("ucar", [128, depth, 2, 2], F32)
        small = sb("small", [128, 16], F32)
        glrT = sb("glrT", [17, TB], BF16)
        ktok = sb("ktok", [128, NT, 192], F32)
        vtok = sb("vtok", [128, NT, 384], BF16)
        g_e = sb("g_e", [128, 2, 192], F32)
        g_l = sb("g_l", [128, 2, 192], F32)
        g_ek = sb("g_ek", [128, 192], F32)
        kend = sb("kend", [128, NT, 192], BF16)
        e1 = sb("e1", [48, 4, TB], BF16)
        e2 = sb("e2", [48, 4, TB], BF16)
        dec = sb("dec", [48, NT, 4], F32)
        qdec = sb("qdec", [48, 4, TB], BF16)
        kinv = sb("kinv", [48, 4, TB], BF16)
        AT = sb("AT", [128, 2, 4, 128], BF16)
        Ssts = [sb("Sst%d" % i, [48, 4, 96], F32) for i in range(depth)]
        Sbs = [sb("Sb%d" % i, [48, 4, 96], BF16) for i in range(depth)]
        gateG = sb("gateG", [96, 4, TB], BF16)
        chs = sb("chs", [128, TB], F32)
        ubuf = sb("ubuf", [128, 2, TB + 2], F32)
        cva = sb("cva", [128, TB], F32)
        cvb = sb("cvb", [128, TB], F32)
        sg = sb("sg", [128, TB], F32)
        osq = otn[0:96, :].rearrange("p (h c) -> p h c", h=4)
        otmp = cva[0:96, :].rearrange("p (h c) -> p h c", h=4)
        orstd = sg[0:96, :].rearrange("p (h c) -> p h c", h=4)
        raw = sb("raw", [128, 2, TB], F32)
        cqn = sb("cqn", [128, 2, TB], BF16)
        ckvn = sb("ckvn", [128, 2, TB], BF16)
        QT = sb("QT", [128, 6 * TB], BF16)
        KT = sb("KT", [128, 6 * TB], BF16)
        QTv = QT[:, :].rearrange("p (h t) -> p h t", h=6)
        KTv = KT[:, :].rearrange("p (h t) -> p h t", h=6)
        qxT = QT[:, 0:4 * TB].rearrange("p (h t) -> p h t", h=4)
        oxT = KT[:, 0:4 * TB].rearrange("p (h t) -> p h t", h=4)
        Vc = sb("Vc", [128, NT, 6, 65], BF16)
        rtab = sb("rtab", [128, 2, TB], F32)
        rt1 = sb("rt1", [128, TB], F32)
        rt2 = sb("rt2", [128, TB], F32)
        gateM = sb("gateM", [128, 3, TB], BF16)
        NKR = 3
        Kr = sb("Kr", [128, NKR, TB], BF16)
        Vr = sb("Vr", [128, NKR, NT * 65], BF16)
        NPT = 4
        PT = sb("PT", [128, NPT, 512], BF16)
        rsum = sb("rsum", [128, TB], F32)
        posi = rsum[:, :].bitcast(I32)

        pt_i = [0]

        def next_pt():
            i = pt_i[0] % NPT
            pt_i[0] += 1
            return i

        TWO_PI = 2.0 * math.pi
        A("pool", lambda e: e.memset(Vc[:], 1.0), writes=["Vc"])
        A("pool", lambda e: e.memset(glrT[:], 1.0), writes=["glrT"])
        def gen_rope(j):
            tsl = slice(j * TB, (j + 1) * TB)
            R = slice(64, 96)
            S.dma("sp", lambda e, tsl=tsl: e.dma_start(out=posi[64:96, :], in_=pos_d[0:1, tsl].to_broadcast([32, TB])),
                  writes=["rsum"])
            A("dve", lambda e: e.tensor_copy(out=rt1[R, :], in_=posi[R, :]), reads=["rsum"], writes=["rt1"])
            for which in range(2):
                col = VB + 8 + which
                A("dve", lambda e, col=col, which=which: e.tensor_scalar(
                    out=rt2[R, :], in0=rt1[R, :], scalar1=vecs[R, col:col + 1],
                    scalar2=(math.pi / 2 if which == 0 else 0.0), op0=ALU.mult, op1=ALU.add),
                  reads=["rt1", "vecs"], writes=["rt2"])
                A("dve", lambda e: e.tensor_scalar(out=posi[R, :], in0=rt2[R, :], scalar1=1.0 / TWO_PI, scalar2=None,
                                                   op0=ALU.mult), reads=["rt2"], writes=["rsum"])
                A("dve", lambda e: e.tensor_copy(out=cva[R, :], in_=posi[R, :]), reads=["rsum"], writes=["cva"])
                A("dve", lambda e: e.scalar_tensor_tensor(out=rt2[R, :], in0=cva[R, :], scalar=-TWO_PI, in1=rt2[R, :],
                                                          op0=ALU.mult, op1=ALU.add),
                  reads=["cva", "rt2"], writes=["rt2"])
                for thr, sgn_ in ((math.pi, -TWO_PI), (-math.pi, TWO_PI)):
                    op = ALU.is_gt if thr > 0 else ALU.is_lt
                    A("dve", lambda e, thr=thr, sgn_=sgn_, op=op: e.tensor_scalar(
                        out=cva[R, :], in0=rt2[R, :], scalar1=thr, scalar2=sgn_, op0=op, op1=ALU.mult),
                      reads=["rt2"], writes=["cva"])
                    A("dve", lambda e: e.tensor_tensor(out=rt2[R, :], in0=rt2[R, :], in1=cva[R, :], op=ALU.add),
                      reads=["rt2", "cva"], writes=["rt2"])
                A("dve", lambda e: e.tensor_scalar(out=rt2[R, :], in0=rt2[R, :], scalar1=math.pi, scalar2=-math.pi,
                                                   op0=ALU.min, op1=ALU.max), reads=["rt2"], writes=["rt2"])
                A("act", lambda e, which=which: e.activation(out=rtab[R, which, :], in_=rt2[R, :], func=AF.Sin),
                  reads=["rt2"], writes=["rtab"])

        def rmsnorm_feat(src_key, src, nchunk, gcol, dst, dst_key, ndim, extra_reads=(), chunk_keys=False):
            b = 7
            for k in range(nchunk):
                s = k % 3
                A("act", lambda e, k=k, s=s: e.activation(out=sqr[:, s, :], in_=src[:, k, :], func=AF.Square),
                  reads=[(src_key, k) if chunk_keys else src_key] + list(extra_reads), writes=[("sqr", s)])
                A("pe", lambda e, k=k, s=s: e.matmul(banks[b][:, 0:TB], lhsT=ones_b[:, :], rhs=sqr[:, s, :],
                                                    start=(k == 0), stop=(k == nchunk - 1)),
                  reads=[("sqr", s), "ones_b"], writes=[PK(b)])
            A("act", lambda e: e.activation(out=rstd[:, :], in_=banks[b][:, 0:TB], func=AF.Ln, scale=1.0 / ndim,
                                            bias=EPS), reads=[PK(b)], writes=["rstd"])
            A("act", lambda e: e.activation(out=rstd[:, :], in_=rstd[:, :], func=AF.Exp, scale=-0.5),
              reads=["rstd"], writes=["rstd"])
            for k in range(nchunk):
                A("dve",
                  lambda e, k=k: e.scalar_tensor_tensor(out=dst[:, k, :], in0=src[:, k, :], scalar=gcol(k),
                                                        in1=rstd[:, :], op0=ALU.mult, op1=ALU.mult),
                  reads=[(src_key, k) if chunk_keys else src_key, "rstd", "vecs"], writes=[dst_key, (dst_key, "c", k)], cost=0.8)

        def group_mm(l, name, M, rhs_fn, rhs_keys, nk=8, rows=128, ncols=None, perk=None):
            s = load_slab(l, name)
            b = nextb("mm")
            ncols = TB if ncols is None else ncols

            def fn(e):
                ins = None
                for k in range(nk):
                    ins = e.matmul(banks[b][0:M, 0:ncols], lhsT=slab(s, k, 0, M, rows), rhs=rhs_fn(k),
                                   start=(k == 0), stop=(k == nk - 1))
                return ins
            if perk is not None:
                for k in range(nk):
                    A("pe", lambda e, k=k: e.matmul(banks[b][0:M, 0:ncols], lhsT=slab(s, k, 0, M, rows), rhs=rhs_fn(k),
                                                    start=(k == 0), stop=(k == nk - 1)),
                      reads=[("ring", s), (perk, "c", k)], writes=[PK(b)], cost=0.29)
                return b
            A("pe", fn, reads=[("ring", s)] + list(rhs_keys), writes=[PK(b)], cost=nk * (0.29 if ncols >= 512 else 0.17))
            return b

        def mem_kv(l):
            kmT, vm = kmTs[l], vms[l]
            for mt in range(2):
                S.dma("sp", lambda e, mt=mt: e.dma_start(out=tok32[:, mt, :], in_=mem_d[mt * 128:(mt + 1) * 128, :]),
                      writes=[("tok32", mt), ("tok32b", mt)])
                A("act", lambda e, mt=mt: e.activation(out=catT[:, 4:6, :].rearrange("p k t -> p (k t)")[:, 0:1024],
                                                       in_=tok32[:, mt, :], func=AF.Square,
                                                       accum_out=small[:, mt:mt + 1]),
                  reads=[("tok32", mt), ("tok32b", mt)], writes=[("small", mt), ("catT", "gla"), ("catT", "conv"), ("catT", "mla")])
                A("act", lambda e, mt=mt: e.activation(out=small[:, mt:mt + 1], in_=small[:, mt:mt + 1], func=AF.Sqrt,
                                                       scale=1.0 / 1024, bias=EPS),
                  reads=[("small", mt)], writes=[("small", mt)])
                A("dve", lambda e, mt=mt: e.reciprocal(out=small[:, mt:mt + 1], in_=small[:, mt:mt + 1]),
                  reads=[("small", mt)], writes=[("small", mt)])
                A("dve", lambda e, mt=mt: e.tensor_scalar(out=tok32[:, mt, :], in0=tok32[:, mt, :],
                                                          scalar1=small[:, mt:mt + 1], scalar2=None, op0=ALU.mult),
                  reads=[("tok32", mt), ("tok32b", mt), ("small", mt)], writes=[("tok32", mt), ("tok32b", mt)])
                for half in range(2):
                    b = nextb("mm")

                    def fn(e, mt=mt, half=half, b=b):
                        ins = None
                        for kk in range(4):
                            k = half * 4 + kk
                            ins = e.transpose(out=banks[b][:, kk * 128:(kk + 1) * 128],
                                              in_=tok32[:, mt, k * 128:(k + 1) * 128], identity=ident_f[:, :])
                        return ins
                    A("pe", fn, reads=[("tok32", mt), ("tok32b", mt), "ident_f"], writes=[PK(b)])
                    for kk in range(4):
                        k = half * 4 + kk
                        A("dve", lambda e, mt=mt, k=k, kk=kk, b=b: e.tensor_scalar(
                            out=memnT[:, k, mt * 128:(mt + 1) * 128], in0=banks[b][:, kk * 128:(kk + 1) * 128],
                            scalar1=vcol(l, V_NMEM + k), scalar2=None, op0=ALU.mult),
                          reads=[PK(b), "vecs"], writes=[("catT", "gla")])
            for h in range(4):
                b = group_mm(l, "wk%d" % h, 128, lambda k: memnT[:, k, :], [("catT", "gla")], ncols=MEM_LEN)
                A("act", lambda e, h=h, b=b: e.copy(out=kmT[:, h, :], in_=banks[b][:, 0:MEM_LEN]),
                  reads=[PK(b)], writes=[("kmT", l)])
            for i in range(4):
                s = load_slab(l, "wv%d" % i)
                b = nextb("mm")

                def fn(e, s=s, b=b):
                    ins = None
                    for mt in range(2):
                        for k in range(8):
                            ins = e.matmul(banks[b][:, mt * 128:(mt + 1) * 128],
                                           lhsT=memnT[:, k, mt * 128:(mt + 1) * 128], rhs=slab(s, k),
                                           start=(k == 0), stop=(k == 7))
                    return ins
                A("pe", fn, reads=[("ring", s), ("catT", "gla")], writes=[PK(b)])
                A("act", lambda e, i=i, b=b: e.copy(
                    out=vm[:, :, i * 128:(i + 1) * 128],
                    in_=banks[b][:, 0:256].rearrange("p (m c) -> p m c", m=2)),
                  reads=[PK(b)], writes=[("vm", l)])


        def do_layer(l):
            wuq, wukv, wgate, kmT, vm, Sst, Sb = wuqs[l], wukvs[l], wgates[l], kmTs[l], vms[l], Ssts[l], Sbs[l]
            S.dma("pool", lambda e: e.dma_start(out=wuq[:].rearrange("p a k c -> p (a k c)"), in_=wuq_d[l, :, :]),
                  writes=[("wuq", l)])
            S.dma("pool", lambda e: e.dma_start(out=wukv[:].rearrange("p k c -> p (k c)"), in_=wukv_d[l, :, :]),
                  writes=[("wukv", l)])
            S.dma("pool", lambda e: e.dma_start(out=wgate[:], in_=wgate_d[l, :, :]), writes=[("wgate", l)])

            A("pool", lambda e: e.memset(Sst[:], 0.0), writes=[("Sst", l)])
            A("pool", lambda e: e.memset(Sb[:], 0.0), writes=[("Sb", l)])
            A("pool", lambda e: e.memset(ucar[:, l, :, :], 0.0), writes=[("ucar", l)])

        if True:
            def do_block(l, j):
                wuq, wukv, wgate, kmT, vm, Sst, Sb = wuqs[l], wukvs[l], wgates[l], kmTs[l], vms[l], Ssts[l], Sbs[l]
                tsl = slice(j * TB, (j + 1) * TB)
                hpar = j % 2
                hT = hTs[hpar]
                HK = ("hT", hpar)
                fcol = vecs[:, VB + 14 + j:VB + 15 + j]

                rmsnorm_feat(HK, hT, 8, lambda k: vcol(l, V_NMIX + k), xnT, "xnT", 1024)

                def xk(k):
                    return xnT[:, k, :]

                b = group_mm(l, "glrkr", 96, xk, ["xnT"], perk="xnT")
                A("act", lambda e, b=b: e.copy(out=glrT[0:16, :], in_=banks[b][0:16, 0:TB]), reads=[PK(b)], writes=["glrT"])
                A("act", lambda e, b=b: e.copy(out=rt1[64:96, :], in_=banks[b][64:96, 0:TB]), reads=[PK(b)], writes=["rt1"])
                b = group_mm(l, "krrot", 96, xk, ["xnT"])
                A("dve", lambda e: e.tensor_tensor(out=rt1[64:96, :], in0=rt1[64:96, :], in1=rtab[64:96, 0, :], op=ALU.mult),
                  reads=["rt1", "rtab"], writes=["rt1"])
                A("dve", lambda e, b=b: e.tensor_tensor(out=rt2[64:96, :], in0=banks[b][64:96, 0:TB], in1=rtab[64:96, 1, :],
                                                        op=ALU.mult), reads=[PK(b), "rtab"], writes=["rt2"])
                A("dve", lambda e: e.tensor_tensor(out=rt1[64:96, :], in0=rt1[64:96, :], in1=rt2[64:96, :], op=ALU.add),
                  reads=["rt1", "rt2"], writes=["rt1"])
                for h in range(6):
                    A("pool", lambda e, h=h: e.tensor_copy(out=KTv[64:96, h, :], in_=rt1[64:96, :]),
                      reads=["rt1"], writes=["KT"])
                for i in range(5):
                    s = load_slab(l, "kvtok%d" % i)
                    b = nextb("mm")

                    def fn(e, s=s, b=b):
                        ins = None
                        for n in range(NT):
                            for k in range(8):
                                ins = e.matmul(banks[b][:, n * 128:(n + 1) * 128],
                                               lhsT=xnT[:, k, n * 128:(n + 1) * 128], rhs=slab(s, k),
                                               start=(k == 0), stop=(k == 7))
                        return ins
                    A("pe", fn, reads=[("ring", s), "xnT"], writes=[PK(b)], cost=NT * 8 * 0.1)
                    bv = banks[b][:, 0:NT * 128].rearrange("p (n c) -> p n c", n=NT)
                    if i == 0:
                        A("act", lambda e, bv=bv: e.copy(out=ktok[:, :, 0:128], in_=bv), reads=[PK(b)], writes=["ktok"])
                    elif i == 1:
                        A("act", lambda e, bv=bv: e.copy(out=ktok[:, :, 128:192], in_=bv[:, :, 0:64]),
                          reads=[PK(b)], writes=["ktok"])
                        A("dve", lambda e, bv=bv: e.tensor_copy(out=vtok[:, :, 0:64], in_=bv[:, :, 64:128]),
                          reads=[PK(b)], writes=["vtok"])
                    elif i < 4:
                        c0 = 64 + (i - 2) * 128
                        A("act", lambda e, bv=bv, c0=c0: e.copy(out=vtok[:, :, c0:c0 + 128], in_=bv),
                          reads=[PK(b)], writes=["vtok"])
                    else:
                        A("dve", lambda e, bv=bv: e.tensor_copy(out=vtok[:, :, 320:384], in_=bv[:, :, 0:64]),
                          reads=[PK(b)], writes=["vtok"])
                for n in range(NT):
                    nsl = slice(n * 128, (n + 1) * 128)
                    p = n % 2
                    bz = 7
                    A("pe", lambda e, nsl=nsl: e.matmul(banks[bz][:, 0:192], lhsT=glrT[0:17, nsl], rhs=wgate[0:17, :],
                                                        start=True, stop=True),
                      reads=["glrT", ("wgate", l)], writes=[PK(bz)])
                    A("act", lambda e, p=p: e.activation(out=g_e[:, p, :], in_=banks[bz][:, 0:192], func=AF.Exp, scale=-1.0),
                      reads=[PK(bz)], writes=[("g_e", p)])
                    A("act", lambda e, p=p: e.activation(out=g_l[:, p, :], in_=g_e[:, p, :], func=AF.Ln, bias=1.0),
                      reads=[("g_e", p)], writes=[("g_l", p)])
                    A("pe", lambda e, p=p: e.matmul(banks[bz][:, 256:448], lhsT=triSL[:, :], rhs=g_l[:, p, :],
                                                    start=True, stop=True),
                      reads=[("g_l", p), "triSL"], writes=[PK(bz)], cost=0.6)
                    A("act", lambda e: e.activation(out=g_ek[:, :], in_=banks[bz][:, 256:448], func=AF.Exp,
                                                    scale=-1.0 / GLA_TAU), reads=[PK(bz)], writes=["g_ek"])
                    A("dve", lambda e, n=n: e.tensor_tensor(out=kend[:, n, :], in0=ktok[:, n, :], in1=g_ek[:, :], op=ALU.mult),
                      reads=["ktok", "g_ek"], writes=["kend"])
                    bb = nextb("mm")

                    def fn(e, p=p, bb=bb):
                        ins = None
                        for h in range(4):
                            ins = e.matmul(banks[bb][0:48, h * 128:(h + 1) * 128], lhsT=g_l[:, p, h * 48:(h + 1) * 48],
                                           rhs=triU[:, :], start=True, stop=True)
                        return ins
                    A("pe", fn, reads=[("g_l", p), "triU"], writes=[PK(bb)], cost=1.6)
                    bbv = banks[bb][0:48, 0:512].rearrange("p (h t) -> p h t", h=4)
                    A("act", lambda e, bbv=bbv, nsl=nsl: e.activation(out=e1[:, :, nsl], in_=bbv, func=AF.Exp,
                                                                       scale=-1.0 / GLA_TAU),
                      reads=[PK(bb)], writes=["e1"])
                    A("act", lambda e, bbv=bbv, nsl=nsl: e.activation(out=e2[:, :, nsl], in_=bbv, func=AF.Exp,
                                                                       scale=1.0 / GLA_TAU),
                      reads=[PK(bb)], writes=["e2"])
                    A("act", lambda e, bbv=bbv, n=n: e.activation(out=dec[:, n, :], in_=bbv[:, :, 127], func=AF.Exp,
                                                                   scale=-1.0 / GLA_TAU),
                      reads=[PK(bb)], writes=["dec"])
                for h in range(4):
                    b = group_mm(l, "gq%d" % h, 48, xk, ["xnT"])
                    A("dve", lambda e, h=h, b=b: e.scalar_tensor_tensor(
                        out=qdec[:, h, :], in0=banks[b][0:48, 0:TB], scalar=GLA_DK ** -0.5, in1=e1[:, h, :],
                        op0=ALU.mult, op1=ALU.mult), reads=[PK(b), "e1"], writes=["qdec"])
                for h in range(4):
                    b = group_mm(l, "gk%d" % h, 48, xk, ["xnT"])
                    A("dve", lambda e, h=h, b=b: e.tensor_tensor(out=kinv[:, h, :], in0=banks[b][0:48, 0:TB], in1=e2[:, h, :],
                                                                 op=ALU.mult), reads=[PK(b), "e2"], writes=["kinv"])
                for h in range(4):
                    b = group_mm(l, "gg%d" % h, 96, xk, ["xnT"])
                    A("act", lambda e, h=h, b=b: e.activation(out=gateG[:, h, :], in_=banks[b][0:96, 0:TB], func=AF.Silu),
                      reads=[PK(b)], writes=["gateG"])

                for n in range(NT):
                    nsl = slice(n * 128, (n + 1) * 128)
                    p = n % 2
                    bA, bO, bU = 5, 6, 7

                    def fnA(e, nsl=nsl):
                        ins = None
                        for h in range(4):
                            ins = e.matmul(banks[bA][:, h * 128:(h + 1) * 128], lhsT=kinv[:, h, nsl], rhs=qdec[:, h, nsl],
                                           start=True, stop=True)
                        return ins
                    A("pe", fnA, reads=["kinv", "qdec"], writes=[PK(bA)], cost=0.45)
                    A("dve", lambda e, p=p: e.tensor_tensor(
                        out=AT[:, p, :, :], in0=banks[bA][:, :].rearrange("p (h c) -> p h c", h=4),
                        in1=maskA[:, :].unsqueeze(1).to_broadcast([128, 4, 128]), op=ALU.mult),
                      reads=[PK(bA), "maskA"], writes=[("AT", p)])

                    def fnO(e, n=n, nsl=nsl, p=p):
                        ins = None
                        for h in range(4):
                            e.matmul(banks[bO][0:96, h * 128:(h + 1) * 128], lhsT=vtok[:, n, h * 96:(h + 1) * 96],
                                     rhs=AT[:, p, h, :], start=True, stop=False)
                            ins = e.matmul(banks[bO][0:96, h * 128:(h + 1) * 128], lhsT=Sb[:, h, :], rhs=qdec[:, h, nsl],
                                           start=False, stop=True)
                        return ins
                    A("pe", fnO, reads=["vtok", ("AT", p), ("Sb", l), "qdec"], writes=[PK(bO)], cost=0.9)

                    def fnU(e, n=n):
                        ins = None
                        for h in range(4):
                            ins = e.matmul(banks[bU][0:48, h * 96:(h + 1) * 96], lhsT=kend[:, n, h * 48:(h + 1) * 48],
                                           rhs=vtok[:, n, h * 96:(h + 1) * 96], start=True, stop=True)
                        return ins
                    A("pe", fnU, reads=["kend", "vtok"], writes=[PK(bU)], cost=0.45)
                    A("dve", lambda e, n=n: e.tensor_tensor(out=Sst[:, :, :], in0=Sst[:, :, :],
                                                            in1=dec[:, n, :].unsqueeze(2).to_broadcast([48, 4, 96]),
                                                            op=ALU.mult), reads=[("Sst", l), "dec"], writes=[("Sst", l)])
                    A("dve", lambda e: e.tensor_tensor(out=Sst[:, :, :], in0=Sst[:, :, :],
                                                       in1=banks[bU][0:48, 0:384].rearrange("p (h c) -> p h c", h=4),
                                                       op=ALU.add), reads=[("Sst", l), PK(bU)], writes=[("Sst", l)])
                    A("act", lambda e: e.copy(out=Sb[:, :, :], in_=Sst[:, :, :]), reads=[("Sst", l)], writes=[("Sb", l)])
                    bOv = banks[bO][0:96, :].rearrange("p (h c) -> p h c", h=4)
                    A("act", lambda e, bOv=bOv: e.activation(out=osq[:, :, :], in_=bOv, func=AF.Square),
                      reads=[PK(bO)], writes=["otn"])
                    A("act", lambda e, bOv=bOv: e.copy(out=otmp[:, :, :], in_=bOv), reads=[PK(bO)], writes=["cva"])
                    A("pe", lambda e: e.matmul(banks[bA][0:96, :], lhsT=ones_b[0:96, 0:96],
                                               rhs=osq[:, :, :].rearrange("p h c -> p (h c)"), start=True, stop=True),
                      reads=["otn", "ones_b"], writes=[PK(bA)])
                    A("act", lambda e: e.activation(out=orstd[:, :, :].rearrange("p h c -> p (h c)"), in_=banks[bA][0:96, :],
                                                    func=AF.Ln, scale=1.0 / GLA_DV, bias=EPS),
                      reads=[PK(bA)], writes=["sg"])
                    A("act", lambda e: e.activation(out=orstd[:, :, :], in_=orstd[:, :, :], func=AF.Exp, scale=-0.5),
                      reads=["sg"], writes=["sg"])
                    A("dve", lambda e: e.scalar_tensor_tensor(out=otmp[:, :, :], in0=otmp[:, :, :],
                                                              scalar=vcol(l, V_GN, 1, slice(0, 96)), in1=orstd[:, :, :],
                                                              op0=ALU.mult, op1=ALU.mult),
                      reads=["cva", "sg", "vecs"], writes=["cva"])
                    A("dve", lambda e, nsl=nsl: e.tensor_tensor(out=catT[0:96, 0:4, nsl], in0=otmp[:, :, :],
                                                                in1=gateG[:, :, nsl], op=ALU.mult),
                      reads=["cva", "gateG"], writes=[("catT", "gla")])

                for i in range(2):
                    b = group_mm(l, "ch%d" % i, 128, xk, ["xnT"])
                    A("act", lambda e, b=b: e.copy(out=chs[:, :], in_=banks[b][:, 0:TB]), reads=[PK(b)], writes=["chs"])
                    A("pool", lambda e, i=i: e.tensor_copy(out=ubuf[:, i, 0:2], in_=ucar[:, l, i, :]),
                      reads=[("ucar", l)], writes=["ubuf"])
                    b = group_mm(l, "cc%d" % i, 128, xk, ["xnT"])
                    A("dve", lambda e, b=b, i=i: e.tensor_tensor(out=ubuf[:, i, 2:TB + 2], in0=banks[b][:, 0:TB], in1=chs[:, :],
                                                                 op=ALU.mult), reads=[PK(b), "chs"], writes=["ubuf"])
                    cw = V_CONV + 3 * i
                    A("dve", lambda e, i=i, cw=cw: e.tensor_scalar(out=cva[:, :], in0=ubuf[:, i, 2:TB + 2],
                                                                   scalar1=vcol(l, cw + 2), scalar2=None, op0=ALU.mult),
                      reads=["ubuf", "vecs"], writes=["cva"])
                    A("dve", lambda e, i=i, cw=cw: e.scalar_tensor_tensor(out=cva[:, :], in0=ubuf[:, i, 1:TB + 1],
                                                                          scalar=vcol(l, cw + 1), in1=cva[:, :],
                                                                          op0=ALU.mult, op1=ALU.add),
                      reads=["ubuf", "cva", "vecs"], writes=["cva"])
                    A("dve", lambda e, i=i, cw=cw: e.scalar_tensor_tensor(out=cva[:, :], in0=ubuf[:, i, 0:TB],
                                                                          scalar=vcol(l, cw + 0), in1=cva[:, :],
                                                                          op0=ALU.mult, op1=ALU.add),
                      reads=["ubuf", "cva", "vecs"], writes=["cva"])
                    A("dve", lambda e, i=i: e.tensor_scalar(out=ucar[:, l, i, :], in0=ubuf[:, i, TB:TB + 2], scalar1=fcol,
                                                            scalar2=None, op0=ALU.mult),
                      reads=["ubuf", "vecs"], writes=[("ucar", l)])
                    b = group_mm(l, "cg%d" % i, 128, xk, ["xnT"])
                    A("act", lambda e, b=b: e.activation(out=sg[:, :], in_=banks[b][:, 0:TB], func=AF.Silu),
                      reads=[PK(b)], writes=["sg"])
                    A("pool", lambda e: e.tensor_tensor(out=sg[:, :], in0=sg[:, :], in1=cva[:, :], op=ALU.mult),
                      reads=["sg", "cva"], writes=["sg"])
                    b = group_mm(l, "cb%d" % i, 128, xk, ["xnT"])
                    A("dve", lambda e, b=b, i=i: e.tensor_tensor(out=catT[:, 4 + i, :], in0=banks[b][:, 0:TB], in1=sg[:, :],
                                                                 op=ALU.mult), reads=[PK(b), "sg"], writes=[("catT", "conv")])

                for nm, dstn, dkey, vq in (("cq", cqn, "cqn", V_QN), ("ckv", ckvn, "ckvn", V_KVN)):
                    for i in range(2):
                        b = group_mm(l, "%s%d" % (nm, i), 128, xk, ["xnT"])
                        A("act", lambda e, b=b, i=i: e.copy(out=raw[:, i, :], in_=banks[b][:, 0:TB]), reads=[PK(b)], writes=[("raw", i)])
                    rmsnorm_feat("raw", raw, 2, lambda k, vq=vq: vcol(l, vq + k), dstn, dkey, 256, chunk_keys=True)
                for i in range(3):
                    b = group_mm(l, "mg%d" % i, 128, xk, ["xnT"])
                    A("act", lambda e, b=b, i=i: e.activation(out=gateM[:, i, :], in_=banks[b][:, 0:TB], func=AF.Silu),
                      reads=[PK(b)], writes=["gateM"])
                for h in range(6):
                    b1, b2 = nextb("mm"), nextb("mm")

                    def fnq(e, h=h, b1=b1, b2=b2):
                        ins = None
                        for a, b in ((0, b1), (1, b2)):
                            for k in range(2):
                                ins = e.matmul(banks[b][0:96, 0:TB], lhsT=wuq[:, a, k, h * 96:(h + 1) * 96], rhs=cqn[:, k, :],
                                               start=(k == 0), stop=(k == 1))
                        return ins
                    A("pe", fnq, reads=[("wuq", l), "cqn"], writes=[PK(b1), PK(b2)], cost=1.16)
                    A("act", lambda e, h=h, b1=b1: e.copy(out=QTv[0:64, h, :], in_=banks[b1][0:64, 0:TB]),
                      reads=[PK(b1)], writes=["QT"])
                    A("dve", lambda e, b1=b1: e.tensor_tensor(out=rt1[64:96, :], in0=banks[b1][64:96, 0:TB], in1=rtab[64:96, 0, :],
                                                              op=ALU.mult), reads=[PK(b1), "rtab"], writes=["rt1"])
                    A("dve", lambda e, b2=b2: e.tensor_tensor(out=rt2[64:96, :], in0=banks[b2][64:96, 0:TB], in1=rtab[64:96, 1, :],
                                                              op=ALU.mult), reads=[PK(b2), "rtab"], writes=["rt2"])
                    A("dve", lambda e, h=h: e.tensor_tensor(out=QTv[64:96, h, :], in0=rt1[64:96, :], in1=rt2[64:96, :],
                                                            op=ALU.add), reads=["rt1", "rt2"], writes=["QT"])
                for h in range(6):
                    b = nextb("mm")

                    def fnk(e, h=h, b=b):
                        ins = None
                        for k in range(2):
                            ins = e.matmul(banks[b][0:64, 0:TB], lhsT=wukv[:, k, h * 64:(h + 1) * 64], rhs=ckvn[:, k, :],
                                           start=(k == 0), stop=(k == 1))
                        return ins
                    A("pe", fnk, reads=[("wukv", l), "ckvn"], writes=[PK(b)], cost=0.58)
                    A("act", lambda e, h=h, b=b: e.copy(out=KTv[0:64, h, :], in_=banks[b][0:64, 0:TB]),
                      reads=[PK(b)], writes=["KT"])
                for n in range(NT):
                    b = nextb("mm")

                    def fnv(e, n=n, b=b):
                        ins = None
                        for k in range(2):
                            ins = e.matmul(banks[b][:, 0:384], lhsT=ckvn[:, k, n * 128:(n + 1) * 128], rhs=wukv[:, k, 384:768],
                                           start=(k == 0), stop=(k == 1))
                        return ins
                    A("pe", fnv, reads=[("wukv", l), "ckvn"], writes=[PK(b)], cost=0.45)
                    A("act", lambda e, n=n, b=b: e.copy(out=Vc[:, n, :, 0:64],
                                                        in_=banks[b][:, 0:384].rearrange("p (h c) -> p h c", h=6)),
                      reads=[PK(b)], writes=["Vc"])
                sc = 1.0 / math.sqrt(MLA_NOPE + MLA_ROPE)
                for h in range(6):
                    bO = nextb("o")
                    first = [True]
                    for kb in range(j + 1):
                        if kb < j:
                            rs = (h * (j + 1) + kb) % NKR
                            S.dma("sp", lambda e, h=h, kb=kb, rs=rs: e.dma_start(
                                out=Kr[0:96, rs, :], in_=kscr_d[l, h, :, kb * TB:(kb + 1) * TB]),
                                reads=[("kscr", l, kb)], writes=[("Kr", rs)])
                            S.dma("sp", lambda e, h=h, kb=kb, rs=rs: e.dma_start(
                                out=Vr[:, rs, :], in_=vscr_d[l, h, :, kb * NT * 65:(kb + 1) * NT * 65]),
                                reads=[("vscr", l, kb, h)], writes=[("Vr", rs)])
                        for kt in range(NT):
                            q0 = kt * 128 if kb == j else 0
                            nq = TB - q0
                            bs = nextb("st")
                            pi = next_pt()
                            if kb < j:
                                lhs = Kr[0:96, rs, kt * 128:(kt + 1) * 128]
                                kkeys = [("Kr", rs)]
                                vkeys = [("Vr", rs)]
                                vap = Vr[:, rs, kt * 65:(kt + 1) * 65]
                            else:
                                lhs = KTv[0:96, h, kt * 128:(kt + 1) * 128]
                                kkeys = ["KT"]
                                vkeys = ["Vc"]
                                vap = Vc[:, kt, h, :]
                            A("pe", lambda e, lhs=lhs, q0=q0, nq=nq, bs=bs, h=h: e.matmul(
                                banks[bs][:, 0:nq], lhsT=lhs, rhs=QTv[0:96, h, q0:TB], start=True, stop=True),
                              reads=kkeys + ["QT"], writes=[PK(bs)], cost=0.25)
                            A("act", lambda e, bs=bs, pi=pi, nq=nq: e.activation(out=PT[:, pi, 0:nq], in_=banks[bs][:, 0:nq],
                                                                                 func=AF.Exp, scale=sc),
                              reads=[PK(bs)], writes=[("PT", pi)], cost=0.6)
                            if kb == j:
                                A("pool", lambda e, pi=pi: e.tensor_tensor(out=PT[:, pi, 0:128], in0=PT[:, pi, 0:128],
                                                                           in1=maskA[:, :], op=ALU.mult),
                                  reads=[("PT", pi), "maskA"], writes=[("PT", pi)])

                            if debug == 3 and j == 1 and h == 0 and kb == 0 and kt == 0:
                                S.dma("pool", lambda e, rs=rs: e.dma_start(out=dbg_d[:, 0:512], in_=Kr[:, rs, :]), reads=[("Kr", rs)], writes=["dbg1"])
                                S.dma("pool", lambda e, pi=pi: e.dma_start(out=dbg_d[:, 512:1024], in_=PT[:, pi, :]), reads=[("PT", pi)], writes=["dbg2"])
                                S.dma("pool", lambda e, rs=rs: e.dma_start(out=dbg_d[:, 1024:1024 + 260], in_=Vr[:, rs, :]), reads=[("Vr", rs)], writes=["dbg3"])
                                S.dma("pool", lambda e, rs=rs: e.dma_start(out=dbg_d[:, 2048:2048 + 512], in_=QTv[:, 0, :]), reads=["QT"], writes=["dbg4"])

                            A("pe", lambda e, pi=pi, q0=q0, nq=nq, vap=vap, bO=bO, st=(kb == 0 and kt == 0): e.matmul(
                                banks[bO][0:65, q0:TB], lhsT=vap, rhs=PT[:, pi, 0:nq], start=st, stop=True,
                                skip_group_check=True), reads=[("PT", pi)] + vkeys, writes=[PK(bO)], cost=0.33)
                    hr = slice((h % 2) * 64, (h % 2) * 64 + 64)
                    A("act", lambda e, bO=bO: e.copy(out=chs[64:65, :], in_=banks[bO][64:65, 0:TB]), reads=[PK(bO)], writes=["chs"])
                    bb2 = nextb("mm")
                    A("pe", lambda e, bb2=bb2: e.matmul(banks[bb2][0:64, 0:TB], lhsT=ones_f[64:65, 0:64], rhs=chs[64:65, :],
                                                        start=True, stop=True), reads=["chs", "ones_f"], writes=[PK(bb2)], cost=1.0)
                    A("act", lambda e, bb2=bb2: e.activation(out=cvb[0:64, :], in_=banks[bb2][0:64, 0:TB], func=AF.Ln),
                      reads=[PK(bb2)], writes=["cvb"])
                    A("act", lambda e: e.activation(out=cvb[0:64, :], in_=cvb[0:64, :], func=AF.Exp, scale=-1.0),
                      reads=["cvb"], writes=["cvb"])
                    A("dve", lambda e, bO=bO, hr=hr: e.tensor_tensor(out=otn[hr, :], in0=banks[bO][0:64, 0:TB], in1=cvb[0:64, :],
                                                                     op=ALU.mult), reads=[PK(bO), "cvb"], writes=["otn"])
                    if h % 2 == 1:
                        c = h // 2
                        A("dve", lambda e, c=c: e.tensor_tensor(out=catT[:, 6 + c, :], in0=otn[:, :], in1=gateM[:, c, :], op=ALU.mult),
                          reads=["otn", "gateM"], writes=[("catT", "mla")])

                if j + 1 < NB:
                    A("act", lambda e: e.activation(out=Vc[:].rearrange("p n h c -> p (n h c)"),
                                                    in_=Vc[:].rearrange("p n h c -> p (n h c)"), func=AF.Copy, scale=fcol),
                      reads=["Vc", "vecs"], writes=["Vc"], cost=1.3)
                    S.dma("pool", lambda e, tsl=tsl: e.dma_start(out=kscr_d[l, :, :, tsl].rearrange("h p t -> p h t"),
                                                                 in_=KTv[0:96, :, :]),
                          reads=["KT"], writes=[("kscr", l, j)])
                    for h in range(6):
                        S.dma("pool", lambda e, h=h: e.dma_start(
                            out=vscr_d[l, h, :, j * NT * 65:(j + 1) * NT * 65].rearrange("p (n c) -> p n c", n=NT),
                            in_=Vc[:, :, h, :]), reads=["Vc"], writes=[("vscr", l, j, h)])
                    A("pool", lambda e: e.memset(Vc[:, :, :, 64:65], 1.0), reads=[], writes=["Vc"])
                A("dve", lambda e: e.tensor_scalar(out=Sst[:, :, :], in0=Sst[:, :, :], scalar1=fcol[0:48, :], scalar2=None,
                                                   op0=ALU.mult), reads=[("Sst", l), "vecs"], writes=[("Sst", l)])
                A("act", lambda e: e.copy(out=Sb[:, :, :], in_=Sst[:, :, :]), reads=[("Sst", l)], writes=[("Sb", l)])

                for co in range(8):
                    s = load_slab(l, "wout%d" % co)
                    b = nextb("mm")

                    def fno(e, s=s, b=b):
                        ins = None
                        for k in range(9):
                            rows = 96 if k < 4 else 128
                            ins = e.matmul(banks[b][:, 0:TB], lhsT=slab(s, k, 0, 128, rows), rhs=catT[0:rows, k, :],
                                           start=(k == 0), stop=(k == 8))
                        return ins
                    A("pe", fno, reads=[("ring", s), ("catT", "gla"), ("catT", "conv"), ("catT", "mla")], writes=[PK(b)], cost=2.6)
                    A("dve", lambda e, co=co, b=b: e.tensor_tensor(out=hT[:, co, :], in0=hT[:, co, :], in1=banks[b][:, 0:TB],
                                                                   op=ALU.add), reads=[PK(b), HK], writes=[HK])

                if j == 0:
                    mem_kv(l)
                rmsnorm_feat(HK, hT, 8, lambda k: vcol(l, V_NX + k), hnT, "xnT", 1024)
                for h in range(4):
                    b = group_mm(l, "wq%d" % h, 128, lambda k: hnT[:, k, :], ["xnT"], perk=("xnT" if h == 0 else None))
                    A("act", lambda e, h=h, b=b: e.copy(out=qxT[:, h, :], in_=banks[b][:, 0:TB]), reads=[PK(b)], writes=["QT"])
                scx = 1.0 / math.sqrt(MEM_HD)
                for h in range(4):
                    pis = []
                    for mt in range(2):
                        bs = nextb("st")
                        pi = next_pt()
                        pis.append(pi)
                        A("pe", lambda e, h=h, mt=mt, bs=bs: e.matmul(banks[bs][:, 0:TB], lhsT=kmT[:, h, mt * 128:(mt + 1) * 128],
                                                                      rhs=qxT[:, h, :], start=True, stop=True),
                          reads=[("kmT", l), "QT"], writes=[PK(bs)])
                        A("act", lambda e, bs=bs, pi=pi: e.activation(out=PT[:, pi, 0:TB], in_=banks[bs][:, 0:TB], func=AF.Exp,
                                                                      scale=scx), reads=[PK(bs)], writes=[("PT", pi)])
                    b1, b2 = nextb("o"), nextb("mm")

                    def fnx(e, h=h, pis=tuple(pis), b1=b1, b2=b2):
                        ins = None
                        for mt in range(2):
                            e.matmul(banks[b1][:, 0:TB], lhsT=ones_b[:, :], rhs=PT[:, pis[mt], 0:TB],
                                     start=(mt == 0), stop=(mt == 1))
                        for mt in range(2):
                            ins = e.matmul(banks[b2][:, 0:TB], lhsT=vm[:, mt, h * 128:(h + 1) * 128], rhs=PT[:, pis[mt], 0:TB],
                                           start=(mt == 0), stop=(mt == 1))
                        return ins
                    A("pe", fnx, reads=[("PT", pis[0]), ("PT", pis[1]), ("vm", l), "ones_b"], writes=[PK(b1), PK(b2)], cost=1.16)
                    A("act", lambda e, b1=b1: e.activation(out=rsum[:, :], in_=banks[b1][:, 0:TB], func=AF.Ln),
                      reads=[PK(b1)], writes=["rsum"])
                    A("act", lambda e: e.activation(out=rsum[:, :], in_=rsum[:, :], func=AF.Exp, scale=-1.0),
                      reads=["rsum"], writes=["rsum"])
                    A("dve", lambda e, h=h, b2=b2: e.tensor_tensor(out=oxT[:, h, :], in0=banks[b2][:, 0:TB], in1=rsum[:, :],
                                                                   op=ALU.mult), reads=[PK(b2), "rsum"], writes=["KT"])
                for co in range(8):
                    s = load_slab(l, "wo%d" % co)
                    b = nextb("mm")

                    def fnw(e, s=s, b=b):
                        ins = None
                        for k in range(4):
                            ins = e.matmul(banks[b][:, 0:TB], lhsT=slab(s, k), rhs=oxT[:, k, :], start=(k == 0), stop=(k == 3))
                        return ins
                    A("pe", fnw, reads=[("ring", s), "KT"], writes=[PK(b)], cost=1.16)
                    A("dve", lambda e, co=co, b=b: e.tensor_tensor(out=hT[:, co, :], in0=hT[:, co, :], in1=banks[b][:, 0:TB],
                                                                   op=ALU.add), reads=[PK(b), HK], writes=[HK])


        fa_c, fb_c = vecs[:, VB + 10:VB + 11], vecs[:, VB + 11:VB + 12]
        ffin_c, omf_c = vecs[:, VB + 12:VB + 13], vecs[:, VB + 13:VB + 14]
        GROUPS = [[2 * p, 2 * p + 1] for p in range(ncores // 2)]

        def do_iter(j):
            hpar = j % 2
            hT = hTs[hpar]
            HK = ("hT", hpar)
            if j > 0:
                r0 = (j - 1) * 2 * 1024
                S.dma("sp", lambda e, r0=r0: e.dma_start(out=hT[:, :, :],
                                                         in_=xrecv_d[r0:r0 + 1024, :].rearrange("(k p) t -> p k t", p=128)),
                      reads=[("xrecv", j - 1)], writes=[HK], lat=8.0)
            for n in range(NT):
                for half in range(2):
                    t0 = j * TB + n * 128
                    hs = n * 2 + half
                    tk, th = (hs % 4) // 2, hs % 2
                    tkey = ("tok32b" if th else "tok32", tk)
                    tsrc = tok32[:, tk, th * 512:th * 512 + 512]
                    fs = slice(half * 512, half * 512 + 512)
                    S.dma("sp", lambda e, t0=t0, tsrc=tsrc, fs=fs: e.dma_start(out=tsrc, in_=x_d[t0:t0 + 128, fs]), writes=[tkey])
                    b = nextb("mm")

                    def fn(e, tsrc=tsrc, b=b):
                        ins = None
                        for kk in range(4):
                            ins = e.transpose(out=banks[b][:, kk * 128:(kk + 1) * 128],
                                              in_=tsrc[:, kk * 128:(kk + 1) * 128], identity=ident_f[:, :])
                        return ins
                    A("pe", fn, reads=[tkey, "ident_f"], writes=[PK(b)], cost=0.5)
                    hreg = hT[:, half * 4:half * 4 + 4, n * 128:(n + 1) * 128]
                    pv = banks[b][:, :].rearrange("p (k t) -> p k t", k=4)
                    if j == 0:
                        A("dve", lambda e, hreg=hreg, pv=pv: e.tensor_scalar(out=hreg, in0=pv, scalar1=fa_c, scalar2=None, op0=ALU.mult),
                          reads=[PK(b), "vecs"], writes=[HK])
                    else:
                        A("dve", lambda e, hreg=hreg: e.tensor_scalar(out=hreg, in0=hreg, scalar1=fb_c, scalar2=None, op0=ALU.mult),
                          reads=[HK, "vecs"], writes=[HK])
                        A("dve", lambda e, hreg=hreg, pv=pv: e.scalar_tensor_tensor(out=hreg, in0=pv, scalar=fa_c, in1=hreg,
                                                                                    op0=ALU.mult, op1=ALU.add),
                          reads=[PK(b), HK, "vecs"], writes=[HK])
            gen_rope(j)
            for l in range(depth):
                do_block(l, j)
            if j + 1 < NB:
                S.dma("act", lambda e: e.dma_start(out=xsend_d[j * 1024:(j + 1) * 1024, :].rearrange("(k p) t -> p k t", p=128),
                                                   in_=hT[:, :, :]), reads=[HK], writes=[("xsend", j)], lat=8.0)
                S.coll(lambda e: e.collective_compute("AllGather", ALU.bypass, replica_groups=GROUPS,
                                                      ins=[xsend_d[j * 1024:(j + 1) * 1024, :]],
                                                      outs=[xrecv_d[j * 2048:(j + 1) * 2048, :]]),
                       reads=[("xsend", j)], writes=[("xrecv", j)])
            b7 = 7
            for k in range(8):
                s3 = k % 3
                A("act", lambda e, k=k, s3=s3: e.activation(out=sqr[:, s3, :], in_=hT[:, k, :], func=AF.Square),
                  reads=[HK], writes=[("sqr", s3)])
                A("pe", lambda e, k=k, s3=s3: e.matmul(banks[b7][:, 0:TB], lhsT=ones_b[:, :], rhs=sqr[:, s3, :],
                                                      start=(k == 0), stop=(k == 7)),
                  reads=[("sqr", s3), "ones_b"], writes=[PK(b7)])
            A("act", lambda e: e.activation(out=rstd[:, :], in_=banks[b7][:, 0:TB], func=AF.Ln, scale=1.0 / 1024, bias=EPS),
              reads=[PK(b7)], writes=["rstd"])
            A("act", lambda e: e.activation(out=rstd[:, :], in_=rstd[:, :], func=AF.Exp, scale=-0.5),
              reads=["rstd"], writes=["rstd"])
            A("dve", lambda e: e.tensor_scalar(out=rstd[:, :], in0=rstd[:, :], scalar1=ffin_c, scalar2=omf_c,
                                               op0=ALU.mult, op1=ALU.add), reads=["rstd", "vecs"], writes=["rstd"])
            for k in range(8):
                A("dve", lambda e, k=k: e.scalar_tensor_tensor(out=hT[:, k, :], in0=hT[:, k, :], scalar=vecs[:, VB + k:VB + k + 1],
                                                               in1=rstd[:, :], op0=ALU.mult, op1=ALU.mult),
                  reads=[HK, "rstd", "vecs"], writes=[HK])
            for n in range(NT):
                t0 = j * TB + n * 128
                for half in range(2):
                    b = nextb("mm")
                    fs = slice(half * 512, half * 512 + 512)

                    def fnf(e, n=n, half=half, b=b):
                        ins = None
                        for kk in range(4):
                            k = half * 4 + kk
                            ins = e.transpose(out=banks[b][:, kk * 128:(kk + 1) * 128],
                                              in_=hT[:, k, n * 128:(n + 1) * 128], identity=ident_f[:, :])
                        return ins
                    A("pe", fnf, reads=[HK, "ident_f"], writes=[PK(b)], cost=0.5)
                    A("act", lambda e, half=half, b=b: e.copy(out=raw[:, half, :], in_=banks[b][:, :]),
                      reads=[PK(b)], writes=[("raw", half)])
                    S.dma("act", lambda e, t0=t0, half=half, fs=fs: e.dma_start(out=out_d[t0:t0 + 128, fs], in_=raw[:, half, :]),
                          reads=[("raw", half)], writes=[("out", j, n, half)])

        for l in range(depth):
            cast_weights(l)
        for l in range(depth):
            do_layer(l)
        for j in range(NB):
            do_iter(j)
        A("pool", None, reads=[("out", j, n, hh) for j in range(NB) for n in range(NT) for hh in range(2)])
        S.emit(ctx)
        build_program.stats = S.stats
    return nc


_T, _TB = 4096, 512


def run_cores(inputs, T, TB, npairs, debug=False):
    ncores = 2 * npairs
    NB = T // TB + 1
    T9 = NB * TB
    nc = build_program(T, 2, TB, debug, ncores=ncores)
    wr = [prep_weights(inputs, [0, 1], 0, NB), prep_weights(inputs, [2, 3], 1, NB)]
    in_maps = []
    for c in range(ncores):
        b, r = c // 2, c % 2
        m = dict(wr[r])
        x = np.zeros((T9, 1024), np.float32)
        pos = np.asarray(inputs["positions"][b], np.int32).reshape(T)
        if r == 0:
            x[:T] = np.asarray(inputs["x"][b], np.float32)
            p9 = np.concatenate([pos, pos[-TB:]])
        else:
            p9 = np.concatenate([pos[:TB], pos])
        m["x"] = x
        m["mem"] = np.ascontiguousarray(np.asarray(inputs["mem"][b], np.float32))
        m["pos"] = np.ascontiguousarray(p9.reshape(1, T9))
        in_maps.append(m)
    res = run_bass_kernel_spmd(nc, in_maps, core_ids=list(range(ncores)))
    if debug:
        run_cores.dbg = [r["dbg"] for r in res.results]
    run_cores.raw = res.results
    return [np.asarray(res.results[2 * p + 1]["out"], np.float32)[TB:T9] for p in range(npairs)]


def kernel(**inputs):
    B = inputs["x"].shape[0]
    outs = run_cores(inputs, _T, _TB, B)
    return np.stack(outs, axis=0)
```

```python
import math
from contextlib import ExitStack

import numpy as np
import concourse.bass as bass
import concourse.mybir as mybir
from concourse.bass_utils import run_bass_kernel_spmd

F32 = mybir.dt.float32
BF16 = mybir.dt.bfloat16
I32 = mybir.dt.int32
AF = mybir.ActivationFunctionType
ALU = mybir.AluOpType

D_MODEL = 1024
DEPTH = 4
MEM_LEN = 256
EPS = 1e-6
GLA_H, GLA_DV, GLA_DK, GLA_RANK, GLA_TAU = 4, 96, 48, 16, 16.0
CONV_W = 256
MLA_H, MLA_NOPE, MLA_ROPE, MLA_V = 6, 64, 32, 64
MEM_H, MEM_HD = 4, 128
O_GQ, O_GK, O_GV, O_GLR, O_GG = 0, 192, 384, 768, 784
O_CC, O_CB, O_CH, O_CG = 1168, 1424, 1680, 1936
O_CQ, O_CKV, O_KR, O_MG = 2192, 2448, 2704, 2736

COMPUTE = ("pe", "act", "dve", "pool")
ALL_ENG = ("pe", "act", "dve", "pool", "sp")
N_DMA_SEMS = 24


class Op:
    __slots__ = ("eng", "fn", "reads", "writes", "dma", "waits", "count", "marked", "slot", "pre_wait", "cost", "lat", "cc")

    def __init__(self, eng, fn, reads, writes, dma, cost, lat):
        self.eng, self.fn, self.reads, self.writes, self.dma = eng, fn, reads, writes, dma
        self.waits = []
        self.count = 0
        self.marked = False
        self.slot = None
        self.pre_wait = None
        self.cost = cost
        self.lat = lat
        self.cc = False


DEF_COST = {"pe": 0.3, "act": 0.5, "dve": 0.6, "pool": 0.7, "sp": 0.05}
WINDOW = {"pe": 256, "act": 192, "dve": 192, "pool": 64, "sp": 48}
REORDER = True
PRIO_Q = 0.5
STRICT_SAME_ENGINE = True


class Sched:
    def __init__(self, nc):
        self.nc = nc
        self.ops = []

    def add(self, eng, fn, reads=(), writes=(), cost=None):
        self.ops.append(Op(eng, fn, tuple(reads), tuple(writes), False, DEF_COST[eng] if cost is None else cost, 0.0))

    def dma(self, eng, fn, reads=(), writes=(), lat=3.0):
        self.ops.append(Op(eng, fn, tuple(reads), tuple(writes), True, 0.08 if eng in ("sp", "act") else 0.5, lat))

    def coll(self, fn, reads=(), writes=(), lat=40.0):
        op = Op("pool", fn, tuple(reads), tuple(writes), True, 1.0, lat)
        op.cc = True
        self.ops.append(op)

    def analyze(self):
        ops = self.ops
        n = len(ops)
        last_writer, readers, bank_last = {}, {}, {}
        deps_all = []
        for i, op in enumerate(ops):
            deps = set()
            for k in op.reads:
                j = last_writer.get(k)
                if j is not None:
                    deps.add(j)
            for k in op.writes:
                j = last_writer.get(k)
                if j is not None:
                    deps.add(j)
                for r in readers.get(k, ()):
                    deps.add(r)
            banks = set()
            for k in op.reads + op.writes:
                if isinstance(k, tuple) and k[0] == "ps":
                    banks.add(k[1])
            for b in banks:
                d = bank_last.setdefault(b, {})
                for e, j in d.items():
                    if e != op.eng:
                        deps.add(j)
                d[op.eng] = i
            for k in op.reads:
                readers.setdefault(k, []).append(i)
            for k in op.writes:
                last_writer[k] = i
                readers[k] = []
            deps.discard(i)
            deps_all.append(deps)
        per_eng = {e: [] for e in ALL_ENG}
        for i, op in enumerate(ops):
            per_eng[op.eng].append(i)
        order = {e: [] for e in ALL_ENG}
        if not REORDER:
            order = per_eng
        else:
            users = [[] for _ in range(n)]
            ndep = [0] * n
            for i in range(n):
                ndep[i] = len(deps_all[i])
                for j in deps_all[i]:
                    users[j].append(i)
            ready = [0.0] * n
            done = [None] * n
            tail = [0.0] * n
            for i in range(n - 1, -1, -1):
                t = 0.0
                for u in users[i]:
                    if tail[u] > t:
                        t = tail[u]
                tail[i] = t + ops[i].cost + ops[i].lat + 0.2
            pend = {e: list(per_eng[e]) for e in ALL_ENG}
            tfree = {e: 0.0 for e in ALL_ENG}
            remaining = n
            while remaining:
                best = None
                for e in ALL_ENG:
                    pl = pend[e]
                    if not pl:
                        continue
                    te = tfree[e]
                    w = WINDOW[e]
                    cand = None
                    for pos in range(min(w, len(pl))):
                        i = pl[pos]
                        if ndep[i]:
                            continue
                        st = ready[i] if ready[i] > te else te
                        key = (int(st / PRIO_Q), -tail[i])
                        if cand is None or key < cand[3]:
                            cand = (st, pos, i, key)
                    if cand is not None and (best is None or cand[0] < best[0] - 1e-9):
                        best = (cand[0], e, cand[1], cand[2])
                st, e, pos, i = best
                op = ops[i]
                pend[e].pop(pos)
                order[e].append(i)
                tfree[e] = st + op.cost
                done[i] = st + op.cost + op.lat + 0.2
                for u in users[i]:
                    ndep[u] -= 1
                    if done[i] > ready[u]:
                        ready[u] = done[i]
                remaining -= 1
            self.sim_time = max(d for d in done if d is not None)
        self.order = order
        pos_of = [0] * n
        for e in ALL_ENG:
            for p, i in enumerate(order[e]):
                pos_of[i] = p
        known = {e: {p: -1 for p in COMPUTE} for e in ALL_ENG}
        known_dma = {e: set() for e in ALL_ENG}
        for e in ALL_ENG:
            for i in order[e]:
                op = ops[i]
                best, dmas = {}, []
                for j in deps_all[i]:
                    pj = ops[j]
                    if pj.dma:
                        dmas.append(j)
                        continue
                    if pj.eng == op.eng and not op.dma:
                        if op.eng == "pe":
                            continue
                        if not STRICT_SAME_ENGINE and not any(k in pj.writes for k in op.reads):
                            continue
                    if pj.eng not in best or pos_of[best[pj.eng]] < pos_of[j]:
                        best[pj.eng] = j
                w = []
                for pe_, j in sorted(best.items()):
                    if known[e][pe_] >= pos_of[j]:
                        continue
                    known[e][pe_] = pos_of[j]
                    ops[j].marked = True
                    w.append(j)
                for j in sorted(dmas):
                    if j in known_dma[e]:
                        continue
                    known_dma[e].add(j)
                    w.append(j)
                op.waits = w
        cnt = {e: 0 for e in COMPUTE}
        dcnt = {e: 0 for e in ALL_ENG}
        ccn = [0]
        for e in ALL_ENG:
            for i in order[e]:
                op = ops[i]
                if op.cc:
                    ccn[0] += 1
                    op.count = ccn[0]
                elif op.dma:
                    d = dcnt[e]
                    dcnt[e] += 1
                    op.slot = d % N_DMA_SEMS
                    op.count = 16 * (d // N_DMA_SEMS + 1)
                    if d >= N_DMA_SEMS:
                        op.pre_wait = (op.slot, 16 * (d // N_DMA_SEMS))
                elif op.marked:
                    cnt[e] += 1
                    op.count = cnt[e]
        self.stats = dict(n_ops=len(ops), marked=dict(cnt), dmas=dict(dcnt),
                          waits=sum(len(o.waits) for o in ops), sim_us=getattr(self, "sim_time", None))

    def emit(self, ctx):
        nc = self.nc
        self.analyze()
        ops = self.ops
        order = self.order
        sems = {e: ctx.enter_context(nc.semaphore("s_" + e)) for e in COMPUTE}
        dma_engs = sorted({op.eng for op in ops if op.dma})
        dsems = {e: [ctx.enter_context(nc.semaphore("d_%s_%d" % (e, s))) for s in range(N_DMA_SEMS)]
                 for e in dma_engs}
        cc_sem = ctx.enter_context(nc.semaphore("cc_sem"))
        block = ctx.enter_context(nc.Block())

        def run(eng_name, eng):
            for i in order[eng_name]:
                op = ops[i]
                for j in op.waits:
                    pj = ops[j]
                    if pj.cc:
                        eng.wait_ge(cc_sem, pj.count)
                    elif pj.dma:
                        eng.wait_ge(dsems[pj.eng][pj.slot], pj.count)
                    else:
                        eng.wait_ge(sems[pj.eng], pj.count)
                if op.cc:
                    op.fn(eng).then_inc(cc_sem, 1)
                elif op.dma:
                    if op.pre_wait is not None:
                        eng.wait_ge(dsems[op.eng][op.pre_wait[0]], op.pre_wait[1])
                    op.fn(eng).then_inc(dsems[op.eng][op.slot], 16)
                elif op.fn is not None:
                    ins = op.fn(eng)
                    if op.marked:
                        ins.then_inc(sems[op.eng], 1)

        deco = {"pe": block.tensor, "act": block.scalar, "dve": block.vector,
                "pool": block.gpsimd, "sp": block.sync}
        for e in ALL_ENG:
            if order[e]:
                deco[e](lambda eng, e=e: run(e, eng))


def _slab_table():
    t = [("glrkr", 8), ("krrot", 8)]
    for i in range(5):
        t.append(("kvtok%d" % i, 8))
    for h in range(4):
        t.append(("gq%d" % h, 8))
    for h in range(4):
        t.append(("gk%d" % h, 8))
    for h in range(4):
        t.append(("gg%d" % h, 8))
    for i in range(2):
        for nm in ("ch", "cc", "cg", "cb"):
            t.append(("%s%d" % (nm, i), 8))
    for nm in ("cq", "ckv"):
        for i in range(2):
            t.append(("%s%d" % (nm, i), 8))
    for i in range(3):
        t.append(("mg%d" % i, 8))
    for i in range(8):
        t.append(("wout%d" % i, 9))
    for i in range(4):
        t.append(("wq%d" % i, 8))
    for i in range(8):
        t.append(("wo%d" % i, 4))
    for i in range(4):
        t.append(("wk%d" % i, 8))
    for i in range(4):
        t.append(("wv%d" % i, 8))
    return t


SLABS = _slab_table()
SLAB_OFF = {}
_o = 0
for _n, _k in SLABS:
    SLAB_OFF[_n] = (_o, _k)
    _o += _k
TOTK = _o

VC_PER = 35
V_NMIX, V_NX, V_NMEM, V_CONV, V_QN, V_KVN, V_GN = 0, 8, 16, 24, 30, 32, 34


def _in_cols():
    c = {}
    ar = np.arange
    for h in range(4):
        c["gq%d" % h] = ar(O_GQ + 48 * h, O_GQ + 48 * h + 48)
        c["gk%d" % h] = ar(O_GK + 48 * h, O_GK + 48 * h + 48)
        c["gg%d" % h] = ar(O_GG + 96 * h, O_GG + 96 * h + 96)
    c["glrkr"] = np.concatenate([ar(O_GLR, O_GLR + 16), ar(0, 48), ar(O_KR, O_KR + 32)])
    c["krrot"] = np.concatenate([ar(0, 64), ar(O_KR + 16, O_KR + 32), ar(O_KR, O_KR + 16)])
    for i in range(2):
        c["cc%d" % i] = ar(O_CC + 128 * i, O_CC + 128 * i + 128)
        c["cb%d" % i] = ar(O_CB + 128 * i, O_CB + 128 * i + 128)
        c["ch%d" % i] = ar(O_CH + 128 * i, O_CH + 128 * i + 128)
        c["cg%d" % i] = ar(O_CG + 128 * i, O_CG + 128 * i + 128)
        c["cq%d" % i] = ar(O_CQ + 128 * i, O_CQ + 128 * i + 128)
        c["ckv%d" % i] = ar(O_CKV + 128 * i, O_CKV + 128 * i + 128)
    for i in range(3):
        c["mg%d" % i] = ar(O_MG + 128 * i, O_MG + 128 * i + 128)
    kv = np.concatenate([ar(O_GK, O_GK + 192), ar(O_GV, O_GV + 384), ar(0, 64)])
    for i in range(5):
        c["kvtok%d" % i] = kv[128 * i:128 * i + 128]
    return c


def _pad128(a):
    n = a.shape[-1]
    if n == 128:
        return a
    pad = np.take(a, np.arange(128 - n) % n, axis=-1)
    return np.concatenate([a, pad], axis=-1)


def _kchunks(w, nk):
    return np.ascontiguousarray(w.reshape(nk, 128, 128).transpose(1, 0, 2))


def prep_weights(inp, layers, rank, NB):
    f = np.float32
    depth = len(layers)
    wsl = np.empty((depth, 128, TOTK, 128), f)
    cols = _in_cols()
    for l, gl in enumerate(layers):
        w_in = np.asarray(inp["w_in"][gl], f)
        for name, idx in cols.items():
            off, nk = SLAB_OFF[name]
            wsl[l, :, off:off + nk, :] = _kchunks(_pad128(w_in[:, idx]), 8)
        wk, wv, wq = (np.asarray(inp[k][gl], f) for k in ("mem_wk", "mem_wv", "mem_wq"))
        for i in range(4):
            for nm, w in (("wk", wk), ("wv", wv), ("wq", wq)):
                off, nk = SLAB_OFF["%s%d" % (nm, i)]
                wsl[l, :, off:off + nk, :] = _kchunks(w[:, 128 * i:128 * i + 128], 8)
        wo = np.asarray(inp["mem_wo"][gl], f)
        wout = np.asarray(inp["w_out"][gl], f)
        rows = []
        for h in range(4):
            r = np.arange(96 * h, 96 * h + 96)
            rows.append(np.concatenate([r, r[:32]]))
        for i in range(5):
            rows.append(np.arange(384 + 128 * i, 384 + 128 * i + 128))
        rows = np.concatenate(rows)
        woutr = wout[rows, :]
        for i in range(8):
            off, nk = SLAB_OFF["wout%d" % i]
            wsl[l, :, off:off + nk, :] = _kchunks(woutr[:, 128 * i:128 * i + 128], 9)
            off, nk = SLAB_OFF["wo%d" % i]
            wsl[l, :, off:off + nk, :] = _kchunks(wo[:, 128 * i:128 * i + 128], 4)
    wsl = wsl.reshape(depth, 128, TOTK * 128)
    wuq = np.empty((depth, 128, 2, 2, 576), f)
    wukv = np.empty((depth, 128, 2, 768), f)
    rot = np.concatenate([np.concatenate([np.arange(96 * h, 96 * h + 64), np.arange(96 * h + 80, 96 * h + 96),
                                          np.arange(96 * h + 64, 96 * h + 80)]) for h in range(6)])
    kvc = np.concatenate([np.concatenate([np.arange(128 * h, 128 * h + 64) for h in range(6)]),
                          np.concatenate([np.arange(128 * h + 64, 128 * h + 128) for h in range(6)])])
    wgate = np.empty((depth, 17, 192), f)
    for l, gl in enumerate(layers):
        u = np.asarray(inp["mla_w_uq"][gl], f)
        wuq[l, :, 0] = u.reshape(2, 128, 576).transpose(1, 0, 2)
        wuq[l, :, 1] = u[:, rot].reshape(2, 128, 576).transpose(1, 0, 2)
        wukv[l] = np.asarray(inp["mla_w_ukv"][gl], f)[:, kvc].reshape(2, 128, 768).transpose(1, 0, 2)
        wgate[l, :16] = np.asarray(inp["gla_w_gate"][gl], f)
        wgate[l, 16] = np.asarray(inp["gla_b_gate"][gl], f)
    nv = VC_PER * depth + 14 + NB
    vecs = np.zeros((128, nv), f)

    def colmaj(v, n):
        return np.asarray(v, f).reshape(n, 128).T

    for l, gl in enumerate(layers):
        b = VC_PER * l
        vecs[:, b + V_NMIX:b + V_NMIX + 8] = colmaj(inp["norm_mix"][gl], 8)
        vecs[:, b + V_NX:b + V_NX + 8] = colmaj(inp["norm_xattn"][gl], 8)
        vecs[:, b + V_NMEM:b + V_NMEM + 8] = colmaj(inp["norm_mem"][gl], 8)
        cw = np.asarray(inp["conv_w"][gl], f)
        for i in range(2):
            vecs[:, b + V_CONV + 3 * i:b + V_CONV + 3 * i + 3] = cw[:, 128 * i:128 * i + 128].T
        vecs[:, b + V_QN:b + V_QN + 2] = colmaj(inp["mla_q_norm"][gl], 2)
        vecs[:, b + V_KVN:b + V_KVN + 2] = colmaj(inp["mla_kv_norm"][gl], 2)
        vecs[:96, b + V_GN] = np.asarray(inp["gla_norm"][gl], f)
    b = VC_PER * depth
    last = (rank == 1)
    vecs[:, b:b + 8] = colmaj(inp["norm_final"], 8) if last else 1.0
    vecs[:, b + 10] = 0.0 if last else 1.0
    vecs[:, b + 11] = 1.0 if last else 0.0
    vecs[:, b + 12] = 1.0 if last else 0.0
    vecs[:, b + 13] = 0.0 if last else 1.0
    vecs[:, b + 14:b + 14 + NB] = 1.0
    if last:
        vecs[:, b + 14] = 0.0
    inv_freq = (1.0 / (10000.0 ** (np.arange(0, 32, 2, dtype=np.float32) / np.float32(32)))).astype(f)
    invf2 = np.concatenate([inv_freq, inv_freq])
    sgn = np.concatenate([-np.ones(16, f), np.ones(16, f)])
    vecs[64:96, b + 8] = invf2
    vecs[64:96, b + 9] = invf2 * sgn
    return dict(wsl=wsl, wuq=wuq.reshape(depth, 128, 2 * 2 * 576), wukv=wukv.reshape(depth, 128, 2 * 768),
                wgate=wgate, vecs=vecs)


def build_program(T, depth, TB=512, debug=False, ncores=8):
    NT = TB // 128
    NB = T // TB + 1
    T = NB * TB
    NTT = T // 128
    nc = bass.Bass("TRN2", target_bir_lowering=False)
    NV = VC_PER * depth + 14 + NB
    x_d = nc.dram_tensor("x", [T, 1024], F32, kind="ExternalInput").ap()
    mem_d = nc.dram_tensor("mem", [MEM_LEN, 1024], F32, kind="ExternalInput").ap()
    pos_d = nc.dram_tensor("pos", [1, T], I32, kind="ExternalInput").ap()
    wsl_d = nc.dram_tensor("wsl", [depth, 128, TOTK * 128], F32, kind="ExternalInput").ap()
    wuq_d = nc.dram_tensor("wuq", [depth, 128, 2 * 2 * 576], F32, kind="ExternalInput").ap()
    wukv_d = nc.dram_tensor("wukv", [depth, 128, 2 * 768], F32, kind="ExternalInput").ap()
    wgate_d = nc.dram_tensor("wgate", [depth, 17, 192], F32, kind="ExternalInput").ap()
    vecs_d = nc.dram_tensor("vecs", [128, NV], F32, kind="ExternalInput").ap()
    out_d = nc.dram_tensor("out", [T, 1024], F32, kind="ExternalOutput").ap()
    dbg_d = nc.dram_tensor("dbg", [128, 9 * TB], BF16, kind="ExternalOutput").ap() if debug else None
    wsc_d = nc.dram_tensor("wsc", [depth, 128, TOTK * 128], BF16).ap()
    xsend_d = nc.dram_tensor("xsend", [NB * 1024, TB], F32).ap()
    xrecv_d = nc.dram_tensor("xrecv", [NB * 2 * 1024, TB], F32).ap()
    kscr_d = nc.dram_tensor("kscr", [depth, 6, 96, T], BF16).ap()
    vscr_d = nc.dram_tensor("vscr", [depth, 6, 128, NTT * 65], BF16).ap()
    rope_d = nc.dram_tensor("ropetab", [2, 32, T], F32).ap()

    ctx = ExitStack()
    with ctx:
        def sb(name, shape, dt):
            return ctx.enter_context(nc.sbuf_tensor("sb_" + name, shape, dt))

        S = Sched(nc)
        A = S.add
        banks = [ctx.enter_context(nc.psum_tensor("bank%d" % i, [128, 512], F32)) for i in range(8)]

        def PK(b):
            return ("ps", b)

        rot = {"mm": [0, 1, 2], "st": [3, 4, 7], "o": [5, 6]}
        rot_i = {"mm": 0, "st": 0, "o": 0}

        def nextb(kind):
            b = rot[kind][rot_i[kind] % len(rot[kind])]
            rot_i[kind] += 1
            return b

        ident_f = sb("ident_f", [128, 128], F32)
        ident_b = sb("ident_b", [128, 128], BF16)
        ones_b = sb("ones_b", [128, 128], BF16)
        triSL = sb("triSL", [128, 128], F32)
        triU = sb("triU", [128, 128], F32)
        maskA = sb("maskA", [128, 128], BF16)
        vecs = sb("vecs", [128, NV], F32)
        A("pool", lambda e: e.memset(ident_f[:], 0.0), writes=["ident_f"])
        A("pool", lambda e: e.affine_select(out=ident_f[:], in_=ident_f[:], pattern=[[-1, 128]],
                                            compare_op=ALU.not_equal, fill=1.0, base=0, channel_multiplier=1),
          reads=["ident_f"], writes=["ident_f"])
        A("pool", lambda e: e.tensor_copy(out=ident_b[:], in_=ident_f[:]), reads=["ident_f"], writes=["ident_b"])
        A("pool", lambda e: e.memset(ones_b[:], 1.0), writes=["ones_b"])
        A("pool", lambda e: e.memset(ones_f[:], 1.0), writes=["ones_f"])
        A("pool", lambda e: e.memset(triU[:], 1.0), writes=["triU"])
        A("pool", lambda e: e.affine_select(out=triU[:], in_=triU[:], pattern=[[1, 128]], compare_op=ALU.is_ge,
                                            fill=0.0, base=0, channel_multiplier=-1),
          reads=["triU"], writes=["triU"])
        A("pool", lambda e: e.tensor_copy(out=maskA[:], in_=triU[:]), reads=["triU"], writes=["maskA"])
        A("pool", lambda e: e.memset(triSL[:], 1.0), writes=["triSL"])
        A("pool", lambda e: e.affine_select(out=triSL[:], in_=triSL[:], pattern=[[-1, 128]], compare_op=ALU.is_gt,
                                            fill=0.0, base=0, channel_multiplier=1),
          reads=["triSL"], writes=["triSL"])
        S.dma("sp", lambda e: e.dma_start(out=vecs[:], in_=vecs_d[:, :]), writes=["vecs"])

        def vcol(l, off, n=1, rows=slice(0, 128)):
            b = VC_PER * l + off
            return vecs[rows, b:b + n]

        VB = VC_PER * depth

        NRING = 6
        ring = sb("ring", [128, NRING, 9 * 128], BF16)
        ring_i = [0]

        def load_slab(l, name):
            off, nk = SLAB_OFF[name]
            s = ring_i[0] % NRING
            ring_i[0] += 1
            pcs = sorted({min(kk // CSTEP, NPC - 1) for kk in (off, off + nk - 1)})
            S.dma("sp", lambda e: e.dma_start(out=ring[:, s, 0:nk * 128],
                                              in_=wsc_d[l, :, off * 128:(off + nk) * 128]),
                  reads=[("wsc", l, i) for i in range(pcs[0], pcs[-1] + 1)], writes=[("ring", s)])
            return s

        def slab(s, k, m0=0, m1=128, rows=128):
            return ring[0:rows, s, k * 128 + m0:k * 128 + m1]

        NPC = 16
        CSTEP = TOTK // NPC

        def cast_weights(l):
            n = TOTK * 128
            step = CSTEP * 128
            for i in range(NPC):
                a, b = i * step, (n if i == NPC - 1 else (i + 1) * step)
                gi = l * NPC + i
                S.dma("pool", lambda e, a=a, b=b: e.dma_start(out=wsc_d[l, :, a:b], in_=wsl_d[l, :, a:b]),
                      reads=([("castchain", gi - 3)] if gi >= 3 else []), writes=[("wsc", l, i), ("castchain", gi)], lat=30.0)

        wuqs = [sb("wuq%d" % i, [128, 2, 2, 576], BF16) for i in range(depth)]
        wukvs = [sb("wukv%d" % i, [128, 2, 768], BF16) for i in range(depth)]
        wgates = [sb("wgate%d" % i, [17, 192], BF16) for i in range(depth)]
        hTs = [sb("hT0", [128, 8, TB], F32), sb("hT1", [128, 8, TB], F32)]
        xnT = sb("xnT", [128, 8, TB], BF16)
        hnT = xnT
        ones_f = sb("ones_f", [128, 64], F32)
        otn = sb("otn", [128, TB], BF16)
        sqr = sb("sqr", [128, 3, TB], BF16)
        rstd = sb("rstd", [128, TB], F32)
        catT = sb("catT", [128, 9, TB], BF16)
        tok32 = sb("tok32", [128, 2, 1024], F32)
        memnT = catT[:, 0:4, :].rearrange("p k t -> p (k t)")[:, 0:8 * MEM_LEN].rearrange("p (k t) -> p k t", k=8)
        kmTs = [sb("kmT%d" % i, [128, 4, MEM_LEN], BF16) for i in range(depth)]
        vms = [sb("vm%d" % i, [128, 2, 512], BF16) for i in range(depth)]
        ucar = sb("ucar", [128, depth, 2, 2], F32)
        small = sb("small", [128, 16], F32)
        glrT = sb("glrT", [17, TB], BF16)
        ktok = sb("ktok", [128, NT, 192], F32)
        vtok = sb("vtok", [128, NT, 384], BF16)
        g_e = sb("g_e", [128, 2, 192], F32)
        g_l = sb("g_l", [128, 2, 192], F32)
        g_ek = sb("g_ek", [128, 192], F32)
        kend = sb("kend", [128, NT, 192], BF16)
        e1 = sb("e1", [48, 4, TB], BF16)
        e2 = sb("e2", [48, 4, TB], BF16)
        dec = sb("dec", [48, NT, 4], F32)
        qdec = sb("qdec", [48, 4, TB], BF16)
        kinv = sb("kinv", [48, 4, TB], BF16)
        AT = sb("AT", [128, 2, 4, 128], BF16)
        Ssts = [sb("Sst%d" % i, [48, 4, 96], F32) for i in range(depth)]
        Sbs = [sb("Sb%d" % i, [48, 4, 96], BF16) for i in range(depth)]
        gateG = sb("gateG", [96, 4, TB], BF16)
        chs = sb("chs", [128, TB], F32)
        ubuf = sb("ubuf", [128, 2, TB + 2], F32)
        cva = sb("cva", [128, TB], F32)
        cvb = sb("cvb", [128, TB], F32)
        sg = sb("sg", [128, TB], F32)
        osq = otn[0:96, :].rearrange("p (h c) -> p h c", h=4)
        otmp = cva[0:96, :].rearrange("p (h c) -> p h c", h=4)
        orstd = sg[0:96, :].rearrange("p (h c) -> p h c", h=4)
        raw = sb("raw", [128, 2, TB], F32)
        cqn = sb("cqn", [128, 2, TB], BF16)
        ckvn = sb("ckvn", [128, 2, TB], BF16)
        QT = sb("QT", [128, 6 * TB], BF16)
        KT = sb("KT", [128, 6 * TB], BF16)
        QTv = QT[:, :].rearrange("p (h t) -> p h t", h=6)
        KTv = KT[:, :].rearrange("p (h t) -> p h t", h=6)
        qxT = QT[:, 0:4 * TB].rearrange("p (h t) -> p h t", h=4)
        oxT = KT[:, 0:4 * TB].rearrange("p (h t) -> p h t", h=4)
        Vc = sb("Vc", [128, NT, 6, 65], BF16)
        rtab = sb("rtab", [128, 2, TB], F32)
        rt1 = sb("rt1", [128, TB], F32)
        rt2 = sb("rt2", [128, TB], F32)
        gateM = sb("gateM", [128, 3, TB], BF16)
        NKR = 3
        Kr = sb("Kr", [128, NKR, TB], BF16)
        Vr = sb("Vr", [128, NKR, NT * 65], BF16)
        NPT = 4
        PT = sb("PT", [128, NPT, 512], BF16)
        rsum = sb("rsum", [128, TB], F32)
        posi = rsum[:, :].bitcast(I32)

        pt_i = [0]

        def next_pt():
            i = pt_i[0] % NPT
            pt_i[0] += 1
            return i

        TWO_PI = 2.0 * math.pi
        A("pool", lambda e: e.memset(Vc[:], 1.0), writes=["Vc"])
        A("pool", lambda e: e.memset(glrT[:], 1.0), writes=["glrT"])
        def gen_rope(j):
            tsl = slice(j * TB, (j + 1) * TB)
            R = slice(64, 96)
            S.dma("sp", lambda e, tsl=tsl: e.dma_start(out=posi[64:96, :], in_=pos_d[0:1, tsl].to_broadcast([32, TB])),
                  writes=["rsum"])
            A("dve", lambda e: e.tensor_copy(out=rt1[R, :], in_=posi[R, :]), reads=["rsum"], writes=["rt1"])
            for which in range(2):
                col = VB + 8 + which
                A("dve", lambda e, col=col, which=which: e.tensor_scalar(
                    out=rt2[R, :], in0=rt1[R, :], scalar1=vecs[R, col:col + 1],
                    scalar2=(math.pi / 2 if which == 0 else 0.0), op0=ALU.mult, op1=ALU.add),
                  reads=["rt1", "vecs"], writes=["rt2"])
                A("dve", lambda e: e.tensor_scalar(out=posi[R, :], in0=rt2[R, :], scalar1=1.0 / TWO_PI, scalar2=None,
                                                   op0=ALU.mult), reads=["rt2"], writes=["rsum"])
                A("dve", lambda e: e.tensor_copy(out=cva[R, :], in_=posi[R, :]), reads=["rsum"], writes=["cva"])
                A("dve", lambda e: e.scalar_tensor_tensor(out=rt2[R, :], in0=cva[R, :], scalar=-TWO_PI, in1=rt2[R, :],
                                                          op0=ALU.mult, op1=ALU.add),
                  reads=["cva", "rt2"], writes=["rt2"])
                for thr, sgn_ in ((math.pi, -TWO_PI), (-math.pi, TWO_PI)):
                    op = ALU.is_gt if thr > 0 else ALU.is_lt
                    A("dve", lambda e, thr=thr, sgn_=sgn_, op=op: e.tensor_scalar(
                        out=cva[R, :], in0=rt2[R, :], scalar1=thr, scalar2=sgn_, op0=op, op1=ALU.mult),
                      reads=["rt2"], writes=["cva"])
                    A("dve", lambda e: e.tensor_tensor(out=rt2[R, :], in0=rt2[R, :], in1=cva[R, :], op=ALU.add),
                      reads=["rt2", "cva"], writes=["rt2"])
                A("dve", lambda e: e.tensor_scalar(out=rt2[R, :], in0=rt2[R, :], scalar1=math.pi, scalar2=-math.pi,
                                                   op0=ALU.min, op1=ALU.max), reads=["rt2"], writes=["rt2"])
                A("act", lambda e, which=which: e.activation(out=rtab[R, which, :], in_=rt2[R, :], func=AF.Sin),
                  reads=["rt2"], writes=["rtab"])

        def rmsnorm_feat(src_key, src, nchunk, gcol, dst, dst_key, ndim, extra_reads=(), chunk_keys=False):
            b = 7
            for k in range(nchunk):
                s = k % 3
                A("act", lambda e, k=k, s=s: e.activation(out=sqr[:, s, :], in_=src[:, k, :], func=AF.Square),
                  reads=[(src_key, k) if chunk_keys else src_key] + list(extra_reads), writes=[("sqr", s)])
                A("pe", lambda e, k=k, s=s: e.matmul(banks[b][:, 0:TB], lhsT=ones_b[:, :], rhs=sqr[:, s, :],
                                                    start=(k == 0), stop=(k == nchunk - 1)),
                  reads=[("sqr", s), "ones_b"], writes=[PK(b)])
            A("act", lambda e: e.activation(out=rstd[:, :], in_=banks[b][:, 0:TB], func=AF.Ln, scale=1.0 / ndim,
                                            bias=EPS), reads=[PK(b)], writes=["rstd"])
            A("act", lambda e: e.activation(out=rstd[:, :], in_=rstd[:, :], func=AF.Exp, scale=-0.5),
              reads=["rstd"], writes=["rstd"])
            for k in range(nchunk):
                A("dve",
                  lambda e, k=k: e.scalar_tensor_tensor(out=dst[:, k, :], in0=src[:, k, :], scalar=gcol(k),
                                                        in1=rstd[:, :], op0=ALU.mult, op1=ALU.mult),
                  reads=[(src_key, k) if chunk_keys else src_key, "rstd", "vecs"], writes=[dst_key, (dst_key, "c", k)], cost=0.8)

        def group_mm(l, name, M, rhs_fn, rhs_keys, nk=8, rows=128, ncols=None, perk=None):
            s = load_slab(l, name)
            b = nextb("mm")
            ncols = TB if ncols is None else ncols

            def fn(e):
                ins = None
                for k in range(nk):
                    ins = e.matmul(banks[b][0:M, 0:ncols], lhsT=slab(s, k, 0, M, rows), rhs=rhs_fn(k),
                                   start=(k == 0), stop=(k == nk - 1))
                return ins
            if perk is not None:
                for k in range(nk):
                    A("pe", lambda e, k=k: e.matmul(banks[b][0:M, 0:ncols], lhsT=slab(s, k, 0, M, rows), rhs=rhs_fn(k),
                                                    start=(k == 0), stop=(k == nk - 1)),
                      reads=[("ring", s), (perk, "c", k)], writes=[PK(b)], cost=0.29)
                return b
            A("pe", fn, reads=[("ring", s)] + list(rhs_keys), writes=[PK(b)], cost=nk * (0.29 if ncols >= 512 else 0.17))
            return b

        def mem_kv(l):
            kmT, vm = kmTs[l], vms[l]
            for mt in range(2):
                S.dma("sp", lambda e, mt=mt: e.dma_start(out=tok32[:, mt, :], in_=mem_d[mt * 128:(mt + 1) * 128, :]),
                      writes=[("tok32", mt), ("tok32b", mt)])
                A("act", lambda e, mt=mt: e.activation(out=catT[:, 4:6, :].rearrange("p k t -> p (k t)")[:, 0:1024],
                                                       in_=tok32[:, mt, :], func=AF.Square,
                                                       accum_out=small[:, mt:mt + 1]),
                  reads=[("tok32", mt), ("tok32b", mt)], writes=[("small", mt), ("catT", "gla"), ("catT", "conv"), ("catT", "mla")])
                A("act", lambda e, mt=mt: e.activation(out=small[:, mt:mt + 1], in_=small[:, mt:mt + 1], func=AF.Sqrt,
                                                       scale=1.0 / 1024, bias=EPS),
                  reads=[("small", mt)], writes=[("small", mt)])
                A("dve", lambda e, mt=mt: e.reciprocal(out=small[:, mt:mt + 1], in_=small[:, mt:mt + 1]),
                  reads=[("small", mt)], writes=[("small", mt)])
                A("dve", lambda e, mt=mt: e.tensor_scalar(out=tok32[:, mt, :], in0=tok32[:, mt, :],
                                                          scalar1=small[:, mt:mt + 1], scalar2=None, op0=ALU.mult),
                  reads=[("tok32", mt), ("tok32b", mt), ("small", mt)], writes=[("tok32", mt), ("tok32b", mt)])
                for half in range(2):
                    b = nextb("mm")

                    def fn(e, mt=mt, half=half, b=b):
                        ins = None
                        for kk in range(4):
                            k = half * 4 + kk
                            ins = e.transpose(out=banks[b][:, kk * 128:(kk + 1) * 128],
                                              in_=tok32[:, mt, k * 128:(k + 1) * 128], identity=ident_f[:, :])
                        return ins
                    A("pe", fn, reads=[("tok32", mt), ("tok32b", mt), "ident_f"], writes=[PK(b)])
                    for kk in range(4):
                        k = half * 4 + kk
                        A("dve", lambda e, mt=mt, k=k, kk=kk, b=b: e.tensor_scalar(
                            out=memnT[:, k, mt * 128:(mt + 1) * 128], in0=banks[b][:, kk * 128:(kk + 1) * 128],
                            scalar1=vcol(l, V_NMEM + k), scalar2=None, op0=ALU.mult),
                          reads=[PK(b), "vecs"], writes=[("catT", "gla")])
            for h in range(4):
                b = group_mm(l, "wk%d" % h, 128, lambda k: memnT[:, k, :], [("catT", "gla")], ncols=MEM_LEN)
                A("act", lambda e, h=h, b=b: e.copy(out=kmT[:, h, :], in_=banks[b][:, 0:MEM_LEN]),
                  reads=[PK(b)], writes=[("kmT", l)])
            for i in range(4):
                s = load_slab(l, "wv%d" % i)
                b = nextb("mm")

                def fn(e, s=s, b=b):
                    ins = None
                    for mt in range(2):
                        for k in range(8):
                            ins = e.matmul(banks[b][:, mt * 128:(mt + 1) * 128],
                                           lhsT=memnT[:, k, mt * 128:(mt + 1) * 128], rhs=slab(s, k),
                                           start=(k == 0), stop=(k == 7))
                    return ins
                A("pe", fn, reads=[("ring", s), ("catT", "gla")], writes=[PK(b)])
                A("act", lambda e, i=i, b=b: e.copy(
                    out=vm[:, :, i * 128:(i + 1) * 128],
                    in_=banks[b][:, 0:256].rearrange("p (m c) -> p m c", m=2)),
                  reads=[PK(b)], writes=[("vm", l)])


        def do_layer(l):
            wuq, wukv, wgate, kmT, vm, Sst, Sb = wuqs[l], wukvs[l], wgates[l], kmTs[l], vms[l], Ssts[l], Sbs[l]
            S.dma("pool", lambda e: e.dma_start(out=wuq[:].rearrange("p a k c -> p (a k c)"), in_=wuq_d[l, :, :]),
                  writes=[("wuq", l)])
            S.dma("pool", lambda e: e.dma_start(out=wukv[:].rearrange("p k c -> p (k c)"), in_=wukv_d[l, :, :]),
                  writes=[("wukv", l)])
            S.dma("pool", lambda e: e.dma_start(out=wgate[:], in_=wgate_d[l, :, :]), writes=[("wgate", l)])

            A("pool", lambda e: e.memset(Sst[:], 0.0), writes=[("Sst", l)])
            A("pool", lambda e: e.memset(Sb[:], 0.0), writes=[("Sb", l)])
            A("pool", lambda e: e.memset(ucar[:, l, :, :], 0.0), writes=[("ucar", l)])

        if True:
            def do_block(l, j):
                wuq, wukv, wgate, kmT, vm, Sst, Sb = wuqs[l], wukvs[l], wgates[l], kmTs[l], vms[l], Ssts[l], Sbs[l]
                tsl = slice(j * TB, (j + 1) * TB)
                hpar = j % 2
                hT = hTs[hpar]
                HK = ("hT", hpar)
                fcol = vecs[:, VB + 14 + j:VB + 15 + j]

                rmsnorm_feat(HK, hT, 8, lambda k: vcol(l, V_NMIX + k), xnT, "xnT", 1024)

                def xk(k):
                    return xnT[:, k, :]

                b = group_mm(l, "glrkr", 96, xk, ["xnT"], perk="xnT")
                A("act", lambda e, b=b: e.copy(out=glrT[0:16, :], in_=banks[b][0:16, 0:TB]), reads=[PK(b)], writes=["glrT"])
                A("act", lambda e, b=b: e.copy(out=rt1[64:96, :], in_=banks[b][64:96, 0:TB]), reads=[PK(b)], writes=["rt1"])
                b = group_mm(l, "krrot", 96, xk, ["xnT"])
                A("dve", lambda e: e.tensor_tensor(out=rt1[64:96, :], in0=rt1[64:96, :], in1=rtab[64:96, 0, :], op=ALU.mult),
                  reads=["rt1", "rtab"], writes=["rt1"])
                A("dve", lambda e, b=b: e.tensor_tensor(out=rt2[64:96, :], in0=banks[b][64:96, 0:TB], in1=rtab[64:96, 1, :],
                                                        op=ALU.mult), reads=[PK(b), "rtab"], writes=["rt2"])
                A("dve", lambda e: e.tensor_tensor(out=rt1[64:96, :], in0=rt1[64:96, :], in1=rt2[64:96, :], op=ALU.add),
                  reads=["rt1", "rt2"], writes=["rt1"])
                for h in range(6):
                    A("pool", lambda e, h=h: e.tensor_copy(out=KTv[64:96, h, :], in_=rt1[64:96, :]),
                      reads=["rt1"], writes=["KT"])
                for i in range(5):
                    s = load_slab(l, "kvtok%d" % i)
                    b = nextb("mm")

                    def fn(e, s=s, b=b):
                        ins = None
                        for n in range(NT):
                            for k in range(8):
                                ins = e.matmul(banks[b][:, n * 128:(n + 1) * 128],
                                               lhsT=xnT[:, k, n * 128:(n + 1) * 128], rhs=slab(s, k),
                                               start=(k == 0), stop=(k == 7))
                        return ins
                    A("pe", fn, reads=[("ring", s), "xnT"], writes=[PK(b)], cost=NT * 8 * 0.1)
                    bv = banks[b][:, 0:NT * 128].rearrange("p (n c) -> p n c", n=NT)
                    if i == 0:
                        A("act", lambda e, bv=bv: e.copy(out=ktok[:, :, 0:128], in_=bv), reads=[PK(b)], writes=["ktok"])
                    elif i == 1:
                        A("act", lambda e, bv=bv: e.copy(out=ktok[:, :, 128:192], in_=bv[:, :, 0:64]),
                          reads=[PK(b)], writes=["ktok"])
                        A("dve", lambda e, bv=bv: e.tensor_copy(out=vtok[:, :, 0:64], in_=bv[:, :, 64:128]),
                          reads=[PK(b)], writes=["vtok"])
                    elif i < 4:
                        c0 = 64 + (i - 2) * 128
                        A("act", lambda e, bv=bv, c0=c0: e.copy(out=vtok[:, :, c0:c0 + 128], in_=bv),
                          reads=[PK(b)], writes=["vtok"])
                    else:
                        A("dve", lambda e, bv=bv: e.tensor_copy(out=vtok[:, :, 320:384], in_=bv[:, :, 0:64]),
                          reads=[PK(b)], writes=["vtok"])
                for n in range(NT):
                    nsl = slice(n * 128, (n + 1) * 128)
                    p = n % 2
                    bz = 7
                    A("pe", lambda e, nsl=nsl: e.matmul(banks[bz][:, 0:192], lhsT=glrT[0:17, nsl], rhs=wgate[0:17, :],
                                                        start=True, stop=True),
                      reads=["glrT", ("wgate", l)], writes=[PK(bz)])
                    A("act", lambda e, p=p: e.activation(out=g_e[:, p, :], in_=banks[bz][:, 0:192], func=AF.Exp, scale=-1.0),
                      reads=[PK(bz)], writes=[("g_e", p)])
                    A("act", lambda e, p=p: e.activation(out=g_l[:, p, :], in_=g_e[:, p, :], func=AF.Ln, bias=1.0),
                      reads=[("g_e", p)], writes=[("g_l", p)])
                    A("pe", lambda e, p=p: e.matmul(banks[bz][:, 256:448], lhsT=triSL[:, :], rhs=g_l[:, p, :],
                                                    start=True, stop=True),
                      reads=[("g_l", p), "triSL"], writes=[PK(bz)], cost=0.6)
                    A("act", lambda e: e.activation(out=g_ek[:, :], in_=banks[bz][:, 256:448], func=AF.Exp,
                                                    scale=-1.0 / GLA_TAU), reads=[PK(bz)], writes=["g_ek"])
                    A("dve", lambda e, n=n: e.tensor_tensor(out=kend[:, n, :], in0=ktok[:, n, :], in1=g_ek[:, :], op=ALU.mult),
                      reads=["ktok", "g_ek"], writes=["kend"])
                    bb = nextb("mm")

                    def fn(e, p=p, bb=bb):
                        ins = None
                        for h in range(4):
                            ins = e.matmul(banks[bb][0:48, h * 128:(h + 1) * 128], lhsT=g_l[:, p, h * 48:(h + 1) * 48],
                                           rhs=triU[:, :], start=True, stop=True)
                        return ins
                    A("pe", fn, reads=[("g_l", p), "triU"], writes=[PK(bb)], cost=1.6)
                    bbv = banks[bb][0:48, 0:512].rearrange("p (h t) -> p h t", h=4)
                    A("act", lambda e, bbv=bbv, nsl=nsl: e.activation(out=e1[:, :, nsl], in_=bbv, func=AF.Exp,
                                                                       scale=-1.0 / GLA_TAU),
                      reads=[PK(bb)], writes=["e1"])
                    A("act", lambda e, bbv=bbv, nsl=nsl: e.activation(out=e2[:, :, nsl], in_=bbv, func=AF.Exp,
                                                                       scale=1.0 / GLA_TAU),
                      reads=[PK(bb)], writes=["e2"])
                    A("act", lambda e, bbv=bbv, n=n: e.activation(out=dec[:, n, :], in_=bbv[:, :, 127], func=AF.Exp,
                                                                   scale=-1.0 / GLA_TAU),
                      reads=[PK(bb)], writes=["dec"])
                for h in range(4):
                    b = group_mm(l, "gq%d" % h, 48, xk, ["xnT"])
                    A("dve", lambda e, h=h, b=b: e.scalar_tensor_tensor(
                        out=qdec[:, h, :], in0=banks[b][0:48, 0:TB], scalar=GLA_DK ** -0.5, in1=e1[:, h, :],
                        op0=ALU.mult, op1=ALU.mult), reads=[PK(b), "e1"], writes=["qdec"])
                for h in range(4):
                    b = group_mm(l, "gk%d" % h, 48, xk, ["xnT"])
                    A("dve", lambda e, h=h, b=b: e.tensor_tensor(out=kinv[:, h, :], in0=banks[b][0:48, 0:TB], in1=e2[:, h, :],
                                                                 op=ALU.mult), reads=[PK(b), "e2"], writes=["kinv"])
                for h in range(4):
                    b = group_mm(l, "gg%d" % h, 96, xk, ["xnT"])
                    A("act", lambda e, h=h, b=b: e.activation(out=gateG[:, h, :], in_=banks[b][0:96, 0:TB], func=AF.Silu),
                      reads=[PK(b)], writes=["gateG"])

                for n in range(NT):
                    nsl = slice(n * 128, (n + 1) * 128)
                    p = n % 2
                    bA, bO, bU = 5, 6, 7

                    def fnA(e, nsl=nsl):
                        ins = None
                        for h in range(4):
                            ins = e.matmul(banks[bA][:, h * 128:(h + 1) * 128], lhsT=kinv[:, h, nsl], rhs=qdec[:, h, nsl],
                                           start=True, stop=True)
                        return ins
                    A("pe", fnA, reads=["kinv", "qdec"], writes=[PK(bA)], cost=0.45)
                    A("dve", lambda e, p=p: e.tensor_tensor(
                        out=AT[:, p, :, :], in0=banks[bA][:, :].rearrange("p (h c) -> p h c", h=4),
                        in1=maskA[:, :].unsqueeze(1).to_broadcast([128, 4, 128]), op=ALU.mult),
                      reads=[PK(bA), "maskA"], writes=[("AT", p)])

                    def fnO(e, n=n, nsl=nsl, p=p):
                        ins = None
                        for h in range(4):
                            e.matmul(banks[bO][0:96, h * 128:(h + 1) * 128], lhsT=vtok[:, n, h * 96:(h + 1) * 96],
                                     rhs=AT[:, p, h, :], start=True, stop=False)
                            ins = e.matmul(banks[bO][0:96, h * 128:(h + 1) * 128], lhsT=Sb[:, h, :], rhs=qdec[:, h, nsl],
                                           start=False, stop=True)
                        return ins
                    A("pe", fnO, reads=["vtok", ("AT", p), ("Sb", l), "qdec"], writes=[PK(bO)], cost=0.9)

                    def fnU(e, n=n):
                        ins = None
                        for h in range(4):
                            ins = e.matmul(banks[bU][0:48, h * 96:(h + 1) * 96], lhsT=kend[:, n, h * 48:(h + 1) * 48],
                                           rhs=vtok[:, n, h * 96:(h + 1) * 96], start=True, stop=True)
                        return ins
                    A("pe", fnU, reads=["kend", "vtok"], writes=[PK(bU)], cost=0.45)
                    A("dve", lambda e, n=n: e.tensor_tensor(out=Sst[:, :, :], in0=Sst[:, :, :],
                                                            in1=dec[:, n, :].unsqueeze(2).to_broadcast([48, 4, 96]),
                                                            op=ALU.mult), reads=[("Sst", l), "dec"], writes=[("Sst", l)])
                    A("dve", lambda e: e.tensor_tensor(out=Sst[:, :, :], in0=Sst[:, :, :],
                                                       in1=banks[bU][0:48, 0:384].rearrange("p (h c) -> p h c", h=4),
                                                       op=ALU.add), reads=[("Sst", l), PK(bU)], writes=[("Sst", l)])
                    A("act", lambda e: e.copy(out=Sb[:, :, :], in_=Sst[:, :, :]), reads=[("Sst", l)], writes=[("Sb", l)])
                    bOv = banks[bO][0:96, :].rearrange("p (h c) -> p h c", h=4)
                    A("act", lambda e, bOv=bOv: e.activation(out=osq[:, :, :], in_=bOv, func=AF.Square),
                      reads=[PK(bO)], writes=["otn"])
                    A("act", lambda e, bOv=bOv: e.copy(out=otmp[:, :, :], in_=bOv), reads=[PK(bO)], writes=["cva"])
                    A("pe", lambda e: e.matmul(banks[bA][0:96, :], lhsT=ones_b[0:96, 0:96],
                                               rhs=osq[:, :, :].rearrange("p h c -> p (h c)"), start=True, stop=True),
                      reads=["otn", "ones_b"], writes=[PK(bA)])
                    A("act", lambda e: e.activation(out=orstd[:, :, :].rearrange("p h c -> p (h c)"), in_=banks[bA][0:96, :],
                                                    func=AF.Ln, scale=1.0 / GLA_DV, bias=EPS),
                      reads=[PK(bA)], writes=["sg"])
                    A("act", lambda e: e.activation(out=orstd[:, :, :], in_=orstd[:, :, :], func=AF.Exp, scale=-0.5),
                      reads=["sg"], writes=["sg"])
                    A("dve", lambda e: e.scalar_tensor_tensor(out=otmp[:, :, :], in0=otmp[:, :, :],
                                                              scalar=vcol(l, V_GN, 1, slice(0, 96)), in1=orstd[:, :, :],
                                                              op0=ALU.mult, op1=ALU.mult),
                      reads=["cva", "sg", "vecs"], writes=["cva"])
                    A("dve", lambda e, nsl=nsl: e.tensor_tensor(out=catT[0:96, 0:4, nsl], in0=otmp[:, :, :],
                                                                in1=gateG[:, :, nsl], op=ALU.mult),
                      reads=["cva", "gateG"], writes=[("catT", "gla")])

                for i in range(2):
                    b = group_mm(l, "ch%d" % i, 128, xk, ["xnT"])
                    A("act", lambda e, b=b: e.copy(out=chs[:, :], in_=banks[b][:, 0:TB]), reads=[PK(b)], writes=["chs"])
                    A("pool", lambda e, i=i: e.tensor_copy(out=ubuf[:, i, 0:2], in_=ucar[:, l, i, :]),
                      reads=[("ucar", l)], writes=["ubuf"])
                    b = group_mm(l, "cc%d" % i, 128, xk, ["xnT"])
                    A("dve", lambda e, b=b, i=i: e.tensor_tensor(out=ubuf[:, i, 2:TB + 2], in0=banks[b][:, 0:TB], in1=chs[:, :],
                                                                 op=ALU.mult), reads=[PK(b), "chs"], writes=["ubuf"])
                    cw = V_CONV + 3 * i
                    A("dve", lambda e, i=i, cw=cw: e.tensor_scalar(out=cva[:, :], in0=ubuf[:, i, 2:TB + 2],
                                                                   scalar1=vcol(l, cw + 2), scalar2=None, op0=ALU.mult),
                      reads=["ubuf", "vecs"], writes=["cva"])
                    A("dve", lambda e, i=i, cw=cw: e.scalar_tensor_tensor(out=cva[:, :], in0=ubuf[:, i, 1:TB + 1],
                                                                          scalar=vcol(l, cw + 1), in1=cva[:, :],
                                                                          op0=ALU.mult, op1=ALU.add),
                      reads=["ubuf", "cva", "vecs"], writes=["cva"])
                    A("dve", lambda e, i=i, cw=cw: e.scalar_tensor_tensor(out=cva[:, :], in0=ubuf[:, i, 0:TB],
                                                                          scalar=vcol(l, cw + 0), in1=cva[:, :],
                                                                          op0=ALU.mult, op1=ALU.add),
                      reads=["ubuf", "cva", "vecs"], writes=["cva"])
                    A("dve", lambda e, i=i: e.tensor_scalar(out=ucar[:, l, i, :], in0=ubuf[:, i, TB:TB + 2], scalar1=fcol,
                                                            scalar2=None, op0=ALU.mult),
                      reads=["ubuf", "vecs"], writes=[("ucar", l)])
                    b = group_mm(l, "cg%d" % i, 128, xk, ["xnT"])
                    A("act", lambda e, b=b: e.activation(out=sg[:, :], in_=banks[b][:, 0:TB], func=AF.Silu),
                      reads=[PK(b)], writes=["sg"])
                    A("pool", lambda e: e.tensor_tensor(out=sg[:, :], in0=sg[:, :], in1=cva[:, :], op=ALU.mult),
                      reads=["sg", "cva"], writes=["sg"])
                    b = group_mm(l, "cb%d" % i, 128, xk, ["xnT"])
                    A("dve", lambda e, b=b, i=i: e.tensor_tensor(out=catT[:, 4 + i, :], in0=banks[b][:, 0:TB], in1=sg[:, :],
                                                                 op=ALU.mult), reads=[PK(b), "sg"], writes=[("catT", "conv")])

                for nm, dstn, dkey, vq in (("cq", cqn, "cqn", V_QN), ("ckv", ckvn, "ckvn", V_KVN)):
                    for i in range(2):
                        b = group_mm(l, "%s%d" % (nm, i), 128, xk, ["xnT"])
                        A("act", lambda e, b=b, i=i: e.copy(out=raw[:, i, :], in_=banks[b][:, 0:TB]), reads=[PK(b)], writes=[("raw", i)])
                    rmsnorm_feat("raw", raw, 2, lambda k, vq=vq: vcol(l, vq + k), dstn, dkey, 256, chunk_keys=True)
                for i in range(3):
                    b = group_mm(l, "mg%d" % i, 128, xk, ["xnT"])
                    A("act", lambda e, b=b, i=i: e.activation(out=gateM[:, i, :], in_=banks[b][:, 0:TB], func=AF.Silu),
                      reads=[PK(b)], writes=["gateM"])
                for h in range(6):
                    b1, b2 = nextb("mm"), nextb("mm")

                    def fnq(e, h=h, b1=b1, b2=b2):
                        ins = None
                        for a, b in ((0, b1), (1, b2)):
                            for k in range(2):
                                ins = e.matmul(banks[b][0:96, 0:TB], lhsT=wuq[:, a, k, h * 96:(h + 1) * 96], rhs=cqn[:, k, :],
                                               start=(k == 0), stop=(k == 1))
                        return ins
                    A("pe", fnq, reads=[("wuq", l), "cqn"], writes=[PK(b1), PK(b2)], cost=1.16)
                    A("act", lambda e, h=h, b1=b1: e.copy(out=QTv[0:64, h, :], in_=banks[b1][0:64, 0:TB]),
                      reads=[PK(b1)], writes=["QT"])
                    A("dve", lambda e, b1=b1: e.tensor_tensor(out=rt1[64:96, :], in0=banks[b1][64:96, 0:TB], in1=rtab[64:96, 0, :],
                                                              op=ALU.mult), reads=[PK(b1), "rtab"], writes=["rt1"])
                    A("dve", lambda e, b2=b2: e.tensor_tensor(out=rt2[64:96, :], in0=banks[b2][64:96, 0:TB], in1=rtab[64:96, 1, :],
                                                              op=ALU.mult), reads=[PK(b2), "rtab"], writes=["rt2"])
                    A("dve", lambda e, h=h: e.tensor_tensor(out=QTv[64:96, h, :], in0=rt1[64:96, :], in1=rt2[64:96, :],
                                                            op=ALU.add), reads=["rt1", "rt2"], writes=["QT"])
                for h in range(6):
                    b = nextb("mm")

                    def fnk(e, h=h, b=b):
                        ins = None
                        for k in range(2):
                            ins = e.matmul(banks[b][0:64, 0:TB], lhsT=wukv[:, k, h * 64:(h + 1) * 64], rhs=ckvn[:, k, :],
                                           start=(k == 0), stop=(k == 1))
                        return ins
                    A("pe", fnk, reads=[("wukv", l), "ckvn"], writes=[PK(b)], cost=0.58)
                    A("act", lambda e, h=h, b=b: e.copy(out=KTv[0:64, h, :], in_=banks[b][0:64, 0:TB]),
                      reads=[PK(b)], writes=["KT"])
                for n in range(NT):
                    b = nextb("mm")

                    def fnv(e, n=n, b=b):
                        ins = None
                        for k in range(2):
                            ins = e.matmul(banks[b][:, 0:384], lhsT=ckvn[:, k, n * 128:(n + 1) * 128], rhs=wukv[:, k, 384:768],
                                           start=(k == 0), stop=(k == 1))
                        return ins
                    A("pe", fnv, reads=[("wukv", l), "ckvn"], writes=[PK(b)], cost=0.45)
                    A("act", lambda e, n=n, b=b: e.copy(out=Vc[:, n, :, 0:64],
                                                        in_=banks[b][:, 0:384].rearrange("p (h c) -> p h c", h=6)),
                      reads=[PK(b)], writes=["Vc"])
                sc = 1.0 / math.sqrt(MLA_NOPE + MLA_ROPE)
                for h in range(6):
                    bO = nextb("o")
                    first = [True]
                    for kb in range(j + 1):
                        if kb < j:
                            rs = (h * (j + 1) + kb) % NKR
                            S.dma("sp", lambda e, h=h, kb=kb, rs=rs: e.dma_start(
                                out=Kr[0:96, rs, :], in_=kscr_d[l, h, :, kb * TB:(kb + 1) * TB]),
                                reads=[("kscr", l, kb)], writes=[("Kr", rs)])
                            S.dma("sp", lambda e, h=h, kb=kb, rs=rs: e.dma_start(
                                out=Vr[:, rs, :], in_=vscr_d[l, h, :, kb * NT * 65:(kb + 1) * NT * 65]),
                                reads=[("vscr", l, kb, h)], writes=[("Vr", rs)])
                        for kt in range(NT):
                            q0 = kt * 128 if kb == j else 0
                            nq = TB - q0
                            bs = nextb("st")
                            pi = next_pt()
                            if kb < j:
                                lhs = Kr[0:96, rs, kt * 128:(kt + 1) * 128]
                                kkeys = [("Kr", rs)]
                                vkeys = [("Vr", rs)]
                                vap = Vr[:, rs, kt * 65:(kt + 1) * 65]
                            else:
                                lhs = KTv[0:96, h, kt * 128:(kt + 1) * 128]
                                kkeys = ["KT"]
                                vkeys = ["Vc"]
                                vap = Vc[:, kt, h, :]
                            A("pe", lambda e, lhs=lhs, q0=q0, nq=nq, bs=bs, h=h: e.matmul(
                                banks[bs][:, 0:nq], lhsT=lhs, rhs=QTv[0:96, h, q0:TB], start=True, stop=True),
                              reads=kkeys + ["QT"], writes=[PK(bs)], cost=0.25)
                            A("act", lambda e, bs=bs, pi=pi, nq=nq: e.activation(out=PT[:, pi, 0:nq], in_=banks[bs][:, 0:nq],
                                                                                 func=AF.Exp, scale=sc),
                              reads=[PK(bs)], writes=[("PT", pi)], cost=0.6)
                            if kb == j:
                                A("pool", lambda e, pi=pi: e.tensor_tensor(out=PT[:, pi, 0:128], in0=PT[:, pi, 0:128],
                                                                           in1=maskA[:, :], op=ALU.mult),
                                  reads=[("PT", pi), "maskA"], writes=[("PT", pi)])

                            if debug == 3 and j == 1 and h == 0 and kb == 0 and kt == 0:
                                S.dma("pool", lambda e, rs=rs: e.dma_start(out=dbg_d[:, 0:512], in_=Kr[:, rs, :]), reads=[("Kr", rs)], writes=["dbg1"])
                                S.dma("pool", lambda e, pi=pi: e.dma_start(out=dbg_d[:, 512:1024], in_=PT[:, pi, :]), reads=[("PT", pi)], writes=["dbg2"])
                                S.dma("pool", lambda e, rs=rs: e.dma_start(out=dbg_d[:, 1024:1024 + 260], in_=Vr[:, rs, :]), reads=[("Vr", rs)], writes=["dbg3"])
                                S.dma("pool", lambda e, rs=rs: e.dma_start(out=dbg_d[:, 2048:2048 + 512], in_=QTv[:, 0, :]), reads=["QT"], writes=["dbg4"])

                            A("pe", lambda e, pi=pi, q0=q0, nq=nq, vap=vap, bO=bO, st=(kb == 0 and kt == 0): e.matmul(
                                banks[bO][0:65, q0:TB], lhsT=vap, rhs=PT[:, pi, 0:nq], start=st, stop=True,
                                skip_group_check=True), reads=[("PT", pi)] + vkeys, writes=[PK(bO)], cost=0.33)
                    hr = slice((h % 2) * 64, (h % 2) * 64 + 64)
                    A("act", lambda e, bO=bO: e.copy(out=chs[64:65, :], in_=banks[bO][64:65, 0:TB]), reads=[PK(bO)], writes=["chs"])
                    bb2 = nextb("mm")
                    A("pe", lambda e, bb2=bb2: e.matmul(banks[bb2][0:64, 0:TB], lhsT=ones_f[64:65, 0:64], rhs=chs[64:65, :],
                                                        start=True, stop=True), reads=["chs", "ones_f"], writes=[PK(bb2)], cost=1.0)
                    A("act", lambda e, bb2=bb2: e.activation(out=cvb[0:64, :], in_=banks[bb2][0:64, 0:TB], func=AF.Ln),
                      reads=[PK(bb2)], writes=["cvb"])
                    A("act", lambda e: e.activation(out=cvb[0:64, :], in_=cvb[0:64, :], func=AF.Exp, scale=-1.0),
                      reads=["cvb"], writes=["cvb"])
                    A("dve", lambda e, bO=bO, hr=hr: e.tensor_tensor(out=otn[hr, :], in0=banks[bO][0:64, 0:TB], in1=cvb[0:64, :],
                                                                     op=ALU.mult), reads=[PK(bO), "cvb"], writes=["otn"])
                    if h % 2 == 1:
                        c = h // 2
                        A("dve", lambda e, c=c: e.tensor_tensor(out=catT[:, 6 + c, :], in0=otn[:, :], in1=gateM[:, c, :], op=ALU.mult),
                          reads=["otn", "gateM"], writes=[("catT", "mla")])

                if j + 1 < NB:
                    A("act", lambda e: e.activation(out=Vc[:].rearrange("p n h c -> p (n h c)"),
                                                    in_=Vc[:].rearrange("p n h c -> p (n h c)"), func=AF.Copy, scale=fcol),
                      reads=["Vc", "vecs"], writes=["Vc"], cost=1.3)
                    S.dma("pool", lambda e, tsl=tsl: e.dma_start(out=kscr_d[l, :, :, tsl].rearrange("h p t -> p h t"),
                                                                 in_=KTv[0:96, :, :]),
                          reads=["KT"], writes=[("kscr", l, j)])
                    for h in range(6):
                        S.dma("pool", lambda e, h=h: e.dma_start(
                            out=vscr_d[l, h, :, j * NT * 65:(j + 1) * NT * 65].rearrange("p (n c) -> p n c", n=NT),
                            in_=Vc[:, :, h, :]), reads=["Vc"], writes=[("vscr", l, j, h)])
                    A("pool", lambda e: e.memset(Vc[:, :, :, 64:65], 1.0), reads=[], writes=["Vc"])
                A("dve", lambda e: e.tensor_scalar(out=Sst[:, :, :], in0=Sst[:, :, :], scalar1=fcol[0:48, :], scalar2=None,
                                                   op0=ALU.mult), reads=[("Sst", l), "vecs"], writes=[("Sst", l)])
                A("act", lambda e: e.copy(out=Sb[:, :, :], in_=Sst[:, :, :]), reads=[("Sst", l)], writes=[("Sb", l)])

                for co in range(8):
                    s = load_slab(l, "wout%d" % co)
                    b = nextb("mm")

                    def fno(e, s=s, b=b):
                        ins = None
                        for k in range(9):
                            rows = 96 if k < 4 else 128
                            ins = e.matmul(banks[b][:, 0:TB], lhsT=slab(s, k, 0, 128, rows), rhs=catT[0:rows, k, :],
                                           start=(k == 0), stop=(k == 8))
                        return ins
                    A("pe", fno, reads=[("ring", s), ("catT", "gla"), ("catT", "conv"), ("catT", "mla")], writes=[PK(b)], cost=2.6)
                    A("dve", lambda e, co=co, b=b: e.tensor_tensor(out=hT[:, co, :], in0=hT[:, co, :], in1=banks[b][:, 0:TB],
                                                                   op=ALU.add), reads=[PK(b), HK], writes=[HK])

                if j == 0:
                    mem_kv(l)
                rmsnorm_feat(HK, hT, 8, lambda k: vcol(l, V_NX + k), hnT, "xnT", 1024)
                for h in range(4):
                    b = group_mm(l, "wq%d" % h, 128, lambda k: hnT[:, k, :], ["xnT"], perk=("xnT" if h == 0 else None))
                    A("act", lambda e, h=h, b=b: e.copy(out=qxT[:, h, :], in_=banks[b][:, 0:TB]), reads=[PK(b)], writes=["QT"])
                scx = 1.0 / math.sqrt(MEM_HD)
                for h in range(4):
                    pis = []
                    for mt in range(2):
                        bs = nextb("st")
                        pi = next_pt()
                        pis.append(pi)
                        A("pe", lambda e, h=h, mt=mt, bs=bs: e.matmul(banks[bs][:, 0:TB], lhsT=kmT[:, h, mt * 128:(mt + 1) * 128],
                                                                      rhs=qxT[:, h, :], start=True, stop=True),
                          reads=[("kmT", l), "QT"], writes=[PK(bs)])
                        A("act", lambda e, bs=bs, pi=pi: e.activation(out=PT[:, pi, 0:TB], in_=banks[bs][:, 0:TB], func=AF.Exp,
                                                                      scale=scx), reads=[PK(bs)], writes=[("PT", pi)])
                    b1, b2 = nextb("o"), nextb("mm")

                    def fnx(e, h=h, pis=tuple(pis), b1=b1, b2=b2):
                        ins = None
                        for mt in range(2):
                            e.matmul(banks[b1][:, 0:TB], lhsT=ones_b[:, :], rhs=PT[:, pis[mt], 0:TB],
                                     start=(mt == 0), stop=(mt == 1))
                        for mt in range(2):
                            ins = e.matmul(banks[b2][:, 0:TB], lhsT=vm[:, mt, h * 128:(h + 1) * 128], rhs=PT[:, pis[mt], 0:TB],
                                           start=(mt == 0), stop=(mt == 1))
                        return ins
                    A("pe", fnx, reads=[("PT", pis[0]), ("PT", pis[1]), ("vm", l), "ones_b"], writes=[PK(b1), PK(b2)], cost=1.16)
                    A("act", lambda e, b1=b1: e.activation(out=rsum[:, :], in_=banks[b1][:, 0:TB], func=AF.Ln),
                      reads=[PK(b1)], writes=["rsum"])
                    A("act", lambda e: e.activation(out=rsum[:, :], in_=rsum[:, :], func=AF.Exp, scale=-1.0),
                      reads=["rsum"], writes=["rsum"])
                    A("dve", lambda e, h=h, b2=b2: e.tensor_tensor(out=oxT[:, h, :], in0=banks[b2][:, 0:TB], in1=rsum[:, :],
                                                                   op=ALU.mult), reads=[PK(b2), "rsum"], writes=["KT"])
                for co in range(8):
                    s = load_slab(l, "wo%d" % co)
                    b = nextb("mm")

                    def fnw(e, s=s, b=b):
                        ins = None
                        for k in range(4):
                            ins = e.matmul(banks[b][:, 0:TB], lhsT=slab(s, k), rhs=oxT[:, k, :], start=(k == 0), stop=(k == 3))
                        return ins
                    A("pe", fnw, reads=[("ring", s), "KT"], writes=[PK(b)], cost=1.16)
                    A("dve", lambda e, co=co, b=b: e.tensor_tensor(out=hT[:, co, :], in0=hT[:, co, :], in1=banks[b][:, 0:TB],
                                                                   op=ALU.add), reads=[PK(b), HK], writes=[HK])


        fa_c, fb_c = vecs[:, VB + 10:VB + 11], vecs[:, VB + 11:VB + 12]
        ffin_c, omf_c = vecs[:, VB + 12:VB + 13], vecs[:, VB + 13:VB + 14]
        GROUPS = [[2 * p, 2 * p + 1] for p in range(ncores // 2)]

        def do_iter(j):
            hpar = j % 2
            hT = hTs[hpar]
            HK = ("hT", hpar)
            if j > 0:
                r0 = (j - 1) * 2 * 1024
                S.dma("sp", lambda e, r0=r0: e.dma_start(out=hT[:, :, :],
                                                         in_=xrecv_d[r0:r0 + 1024, :].rearrange("(k p) t -> p k t", p=128)),
                      reads=[("xrecv", j - 1)], writes=[HK], lat=8.0)
            for n in range(NT):
                for half in range(2):
                    t0 = j * TB + n * 128
                    hs = n * 2 + half
                    tk, th = (hs % 4) // 2, hs % 2
                    tkey = ("tok32b" if th else "tok32", tk)
                    tsrc = tok32[:, tk, th * 512:th * 512 + 512]
                    fs = slice(half * 512, half * 512 + 512)
                    S.dma("sp", lambda e, t0=t0, tsrc=tsrc, fs=fs: e.dma_start(out=tsrc, in_=x_d[t0:t0 + 128, fs]), writes=[tkey])
                    b = nextb("mm")

                    def fn(e, tsrc=tsrc, b=b):
                        ins = None
                        for kk in range(4):
                            ins = e.transpose(out=banks[b][:, kk * 128:(kk + 1) * 128],
                                              in_=tsrc[:, kk * 128:(kk + 1) * 128], identity=ident_f[:, :])
                        return ins
                    A("pe", fn, reads=[tkey, "ident_f"], writes=[PK(b)], cost=0.5)
                    hreg = hT[:, half * 4:half * 4 + 4, n * 128:(n + 1) * 128]
                    pv = banks[b][:, :].rearrange("p (k t) -> p k t", k=4)
                    if j == 0:
                        A("dve", lambda e, hreg=hreg, pv=pv: e.tensor_scalar(out=hreg, in0=pv, scalar1=fa_c, scalar2=None, op0=ALU.mult),
                          reads=[PK(b), "vecs"], writes=[HK])
                    else:
                        A("dve", lambda e, hreg=hreg: e.tensor_scalar(out=hreg, in0=hreg, scalar1=fb_c, scalar2=None, op0=ALU.mult),
                          reads=[HK, "vecs"], writes=[HK])
                        A("dve", lambda e, hreg=hreg, pv=pv: e.scalar_tensor_tensor(out=hreg, in0=pv, scalar=fa_c, in1=hreg,
                                                                                    op0=ALU.mult, op1=ALU.add),
                          reads=[PK(b), HK, "vecs"], writes=[HK])
            gen_rope(j)
            for l in range(depth):
                do_block(l, j)
            if j + 1 < NB:
                S.dma("act", lambda e: e.dma_start(out=xsend_d[j * 1024:(j + 1) * 1024, :].rearrange("(k p) t -> p k t", p=128),
                                                   in_=hT[:, :, :]), reads=[HK], writes=[("xsend", j)], lat=8.0)
                S.coll(lambda e: e.collective_compute("AllGather", ALU.bypass, replica_groups=GROUPS,
                                                      ins=[xsend_d[j * 1024:(j + 1) * 1024, :]],
                                                      outs=[xrecv_d[j * 2048:(j + 1) * 2048, :]]),
                       reads=[("xsend", j)], writes=[("xrecv", j)])
            b7 = 7
            for k in range(8):
                s3 = k % 3
                A("act", lambda e, k=k, s3=s3: e.activation(out=sqr[:, s3, :], in_=hT[:, k, :], func=AF.Square),
                  reads=[HK], writes=[("sqr", s3)])
                A("pe", lambda e, k=k, s3=s3: e.matmul(banks[b7][:, 0:TB], lhsT=ones_b[:, :], rhs=sqr[:, s3, :],
                                                      start=(k == 0), stop=(k == 7)),
                  reads=[("sqr", s3), "ones_b"], writes=[PK(b7)])
            A("act", lambda e: e.activation(out=rstd[:, :], in_=banks[b7][:, 0:TB], func=AF.Ln, scale=1.0 / 1024, bias=EPS),
              reads=[PK(b7)], writes=["rstd"])
            A("act", lambda e: e.activation(out=rstd[:, :], in_=rstd[:, :], func=AF.Exp, scale=-0.5),
              reads=["rstd"], writes=["rstd"])
            A("dve", lambda e: e.tensor_scalar(out=rstd[:, :], in0=rstd[:, :], scalar1=ffin_c, scalar2=omf_c,
                                               op0=ALU.mult, op1=ALU.add), reads=["rstd", "vecs"], writes=["rstd"])
            for k in range(8):
                A("dve", lambda e, k=k: e.scalar_tensor_tensor(out=hT[:, k, :], in0=hT[:, k, :], scalar=vecs[:, VB + k:VB + k + 1],
                                                               in1=rstd[:, :], op0=ALU.mult, op1=ALU.mult),
                  reads=[HK, "rstd", "vecs"], writes=[HK])
            for n in range(NT):
                t0 = j * TB + n * 128
                for half in range(2):
                    b = nextb("mm")
                    fs = slice(half * 512, half * 512 + 512)

                    def fnf(e, n=n, half=half, b=b):
                        ins = None
                        for kk in range(4):
                            k = half * 4 + kk
                            ins = e.transpose(out=banks[b][:, kk * 128:(kk + 1) * 128],
                                              in_=hT[:, k, n * 128:(n + 1) * 128], identity=ident_f[:, :])
                        return ins
                    A("pe", fnf, reads=[HK, "ident_f"], writes=[PK(b)], cost=0.5)
                    A("act", lambda e, half=half, b=b: e.copy(out=raw[:, half, :], in_=banks[b][:, :]),
                      reads=[PK(b)], writes=[("raw", half)])
                    S.dma("act", lambda e, t0=t0, half=half, fs=fs: e.dma_start(out=out_d[t0:t0 + 128, fs], in_=raw[:, half, :]),
                          reads=[("raw", half)], writes=[("out", j, n, half)])

        for l in range(depth):
            cast_weights(l)
        for l in range(depth):
            do_layer(l)
        for j in range(NB):
            do_iter(j)
        A("pool", None, reads=[("out", j, n, hh) for j in range(NB) for n in range(NT) for hh in range(2)])
        S.emit(ctx)
        build_program.stats = S.stats
    return nc


_T, _TB = 4096, 512


def run_cores(inputs, T, TB, npairs, debug=False):
    ncores = 2 * npairs
    NB = T // TB + 1
    T9 = NB * TB
    nc = build_program(T, 2, TB, debug, ncores=ncores)
    wr = [prep_weights(inputs, [0, 1], 0, NB), prep_weights(inputs, [2, 3], 1, NB)]
    in_maps = []
    for c in range(ncores):
        b, r = c // 2, c % 2
        m = dict(wr[r])
        x = np.zeros((T9, 1024), np.float32)
        pos = np.asarray(inputs["positions"][b], np.int32).reshape(T)
        if r == 0:
            x[:T] = np.asarray(inputs["x"][b], np.float32)
            p9 = np.concatenate([pos, pos[-TB:]])
        else:
            p9 = np.concatenate([pos[:TB], pos])
        m["x"] = x
        m["mem"] = np.ascontiguousarray(np.asarray(inputs["mem"][b], np.float32))
        m["pos"] = np.ascontiguousarray(p9.reshape(1, T9))
        in_maps.append(m)
    res = run_bass_kernel_spmd(nc, in_maps, core_ids=list(range(ncores)))
    if debug:
        run_cores.dbg = [r["dbg"] for r in res.results]
    run_cores.raw = res.results
    return [np.asarray(res.results[2 * p + 1]["out"], np.float32)[TB:T9] for p in range(npairs)]


def kernel(**inputs):
    B = inputs["x"].shape[0]
    outs = run_cores(inputs, _T, _TB, B)
    return np.stack(outs, axis=0)
```

```python
import math
from contextlib import ExitStack

import numpy as np
import concourse.bass as bass
import concourse.mybir as mybir
from concourse.bass_utils import run_bass_kernel_spmd

F32 = mybir.dt.float32
BF16 = mybir.dt.bfloat16
I32 = mybir.dt.int32
AF = mybir.ActivationFunctionType
ALU = mybir.AluOpType

D_MODEL = 1024
DEPTH = 4
MEM_LEN = 256
EPS = 1e-6
GLA_H, GLA_DV, GLA_DK, GLA_RANK, GLA_TAU = 4, 96, 48, 16, 16.0
CONV_W = 256
MLA_H, MLA_NOPE, MLA_ROPE, MLA_V = 6, 64, 32, 64
MEM_H, MEM_HD = 4, 128
O_GQ, O_GK, O_GV, O_GLR, O_GG = 0, 192, 384, 768, 784
O_CC, O_CB, O_CH, O_CG = 1168, 1424, 1680, 1936
O_CQ, O_CKV, O_KR, O_MG = 2192, 2448, 2704, 2736

COMPUTE = ("pe", "act", "dve", "pool")
ALL_ENG = ("pe", "act", "dve", "pool", "sp")
N_DMA_SEMS = 24


class Op:
    __slots__ = ("eng", "fn", "reads", "writes", "dma", "waits", "count", "marked", "slot", "pre_wait", "cost", "lat", "cc")

    def __init__(self, eng, fn, reads, writes, dma, cost, lat):
        self.eng, self.fn, self.reads, self.writes, self.dma = eng, fn, reads, writes, dma
        self.waits = []
        self.count = 0
        self.marked = False
        self.slot = None
        self.pre_wait = None
        self.cost = cost
        self.lat = lat
        self.cc = False


DEF_COST = {"pe": 0.3, "act": 0.5, "dve": 0.6, "pool": 0.7, "sp": 0.05}
WINDOW = {"pe": 256, "act": 192, "dve": 192, "pool": 64, "sp": 48}
REORDER = True
PRIO_Q = 0.5
STRICT_SAME_ENGINE = True


class Sched:
    def __init__(self, nc):
        self.nc = nc
        self.ops = []

    def add(self, eng, fn, reads=(), writes=(), cost=None):
        self.ops.append(Op(eng, fn, tuple(reads), tuple(writes), False, DEF_COST[eng] if cost is None else cost, 0.0))

    def dma(self, eng, fn, reads=(), writes=(), lat=3.0):
        self.ops.append(Op(eng, fn, tuple(reads), tuple(writes), True, 0.08 if eng in ("sp", "act") else 0.5, lat))

    def coll(self, fn, reads=(), writes=(), lat=40.0):
        op = Op("pool", fn, tuple(reads), tuple(writes), True, 1.0, lat)
        op.cc = True
        self.ops.append(op)

    def analyze(self):
        ops = self.ops
        n = len(ops)
        last_writer, readers, bank_last = {}, {}, {}
        deps_all = []
        for i, op in enumerate(ops):
            deps = set()
            for k in op.reads:
                j = last_writer.get(k)
                if j is not None:
                    deps.add(j)
            for k in op.writes:
                j = last_writer.get(k)
                if j is not None:
                    deps.add(j)
                for r in readers.get(k, ()):
                    deps.add(r)
            banks = set()
            for k in op.reads + op.writes:
                if isinstance(k, tuple) and k[0] == "ps":
                    banks.add(k[1])
            for b in banks:
                d = bank_last.setdefault(b, {})
                for e, j in d.items():
                    if e != op.eng:
                        deps.add(j)
                d[op.eng] = i
            for k in op.reads:
                readers.setdefault(k, []).append(i)
            for k in op.writes:
                last_writer[k] = i
                readers[k] = []
            deps.discard(i)
            deps_all.append(deps)
        per_eng = {e: [] for e in ALL_ENG}
        for i, op in enumerate(ops):
            per_eng[op.eng].append(i)
        order = {e: [] for e in ALL_ENG}
        if not REORDER:
            order = per_eng
        else:
            users = [[] for _ in range(n)]
            ndep = [0] * n
            for i in range(n):
                ndep[i] = len(deps_all[i])
                for j in deps_all[i]:
                    users[j].append(i)
            ready = [0.0] * n
            done = [None] * n
            tail = [0.0] * n
            for i in range(n - 1, -1, -1):
                t = 0.0
                for u in users[i]:
                    if tail[u] > t:
                        t = tail[u]
                tail[i] = t + ops[i].cost + ops[i].lat + 0.2
            pend = {e: list(per_eng[e]) for e in ALL_ENG}
            tfree = {e: 0.0 for e in ALL_ENG}
            remaining = n
            while remaining:
                best = None
                for e in ALL_ENG:
                    pl = pend[e]
                    if not pl:
                        continue
                    te = tfree[e]
                    w = WINDOW[e]
                    cand = None
                    for pos in range(min(w, len(pl))):
                        i = pl[pos]
                        if ndep[i]:
                            continue
                        st = ready[i] if ready[i] > te else te
                        key = (int(st / PRIO_Q), -tail[i])
                        if cand is None or key < cand[3]:
                            cand = (st, pos, i, key)
                    if cand is not None and (best is None or cand[0] < best[0] - 1e-9):
                        best = (cand[0], e, cand[1], cand[2])
                st, e, pos, i = best
                op = ops[i]
                pend[e].pop(pos)
                order[e].append(i)
                tfree[e] = st + op.cost
                done[i] = st + op.cost + op.lat + 0.2
                for u in users[i]:
                    ndep[u] -= 1
                    if done[i] > ready[u]:
                        ready[u] = done[i]
                remaining -= 1
            self.sim_time = max(d for d in done if d is not None)
        self.order = order
        pos_of = [0] * n
        for e in ALL_ENG:
            for p, i in enumerate(order[e]):
                pos_of[i] = p
        known = {e: {p: -1 for p in COMPUTE} for e in ALL_ENG}
        known_dma = {e: set() for e in ALL_ENG}
        for e in ALL_ENG:
            for i in order[e]:
                op = ops[i]
                best, dmas = {}, []
                for j in deps_all[i]:
                    pj = ops[j]
                    if pj.dma:
                        dmas.append(j)
                        continue
                    if pj.eng == op.eng and not op.dma:
                        if op.eng == "pe":
                            continue
                        if not STRICT_SAME_ENGINE and not any(k in pj.writes for k in op.reads):
                            continue
                    if pj.eng not in best or pos_of[best[pj.eng]] < pos_of[j]:
                        best[pj.eng] = j
                w = []
                for pe_, j in sorted(best.items()):
                    if known[e][pe_] >= pos_of[j]:
                        continue
                    known[e][pe_] = pos_of[j]
                    ops[j].marked = True
                    w.append(j)
                for j in sorted(dmas):
                    if j in known_dma[e]:
                        continue
                    known_dma[e].add(j)
                    w.append(j)
                op.waits = w
        cnt = {e: 0 for e in COMPUTE}
        dcnt = {e: 0 for e in ALL_ENG}
        ccn = [0]
        for e in ALL_ENG:
            for i in order[e]:
                op = ops[i]
                if op.cc:
                    ccn[0] += 1
                    op.count = ccn[0]
                elif op.dma:
                    d = dcnt[e]
                    dcnt[e] += 1
                    op.slot = d % N_DMA_SEMS
                    op.count = 16 * (d // N_DMA_SEMS + 1)
                    if d >= N_DMA_SEMS:
                        op.pre_wait = (op.slot, 16 * (d // N_DMA_SEMS))
                elif op.marked:
                    cnt[e] += 1
                    op.count = cnt[e]
        self.stats = dict(n_ops=len(ops), marked=dict(cnt), dmas=dict(dcnt),
                          waits=sum(len(o.waits) for o in ops), sim_us=getattr(self, "sim_time", None))

    def emit(self, ctx):
        nc = self.nc
        self.analyze()
        ops = self.ops
        order = self.order
        sems = {e: ctx.enter_context(nc.semaphore("s_" + e)) for e in COMPUTE}
        dma_engs = sorted({op.eng for op in ops if op.dma})
        dsems = {e: [ctx.enter_context(nc.semaphore("d_%s_%d" % (e, s))) for s in range(N_DMA_SEMS)]
                 for e in dma_engs}
        cc_sem = ctx.enter_context(nc.semaphore("cc_sem"))
        block = ctx.enter_context(nc.Block())

        def run(eng_name, eng):
            for i in order[eng_name]:
                op = ops[i]
                for j in op.waits:
                    pj = ops[j]
                    if pj.cc:
                        eng.wait_ge(cc_sem, pj.count)
                    elif pj.dma:
                        eng.wait_ge(dsems[pj.eng][pj.slot], pj.count)
                    else:
                        eng.wait_ge(sems[pj.eng], pj.count)
                if op.cc:
                    op.fn(eng).then_inc(cc_sem, 1)
                elif op.dma:
                    if op.pre_wait is not None:
                        eng.wait_ge(dsems[op.eng][op.pre_wait[0]], op.pre_wait[1])
                    op.fn(eng).then_inc(dsems[op.eng][op.slot], 16)
                elif op.fn is not None:
                    ins = op.fn(eng)
                    if op.marked:
                        ins.then_inc(sems[op.eng], 1)

        deco = {"pe": block.tensor, "act": block.scalar, "dve": block.vector,
                "pool": block.gpsimd, "sp": block.sync}
        for e in ALL_ENG:
            if order[e]:
                deco[e](lambda eng, e=e: run(e, eng))


def _slab_table():
    t = [("glrkr", 8), ("krrot", 8)]
    for i in range(5):
        t.append(("kvtok%d" % i, 8))
    for p in range(2):
        t.append(("gq%d" % p, 8))
    for p in range(2):
        t.append(("gk%d" % p, 8))
    for h in range(4):
        t.append(("gg%d" % h, 8))
    for i in range(2):
        for nm in ("ch", "cc", "cg", "cb"):
            t.append(("%s%d" % (nm, i), 8))
    for nm in ("cq", "ckv"):
        for i in range(2):
            t.append(("%s%d" % (nm, i), 8))
    for i in range(3):
        t.append(("mg%d" % i, 8))
    for i in range(8):
        t.append(("wout%d" % i, 9))
    for i in range(4):
        t.append(("wq%d" % i, 8))
    for i in range(8):
        t.append(("wo%d" % i, 4))
    for i in range(4):
        t.append(("wk%d" % i, 8))
    for i in range(4):
        t.append(("wv%d" % i, 8))
    return t


SLABS = _slab_table()
SLAB_OFF = {}
_o = 0
for _n, _k in SLABS:
    SLAB_OFF[_n] = (_o, _k)
    _o += _k
TOTK = _o

VC_PER = 35
V_NMIX, V_NX, V_NMEM, V_CONV, V_QN, V_KVN, V_GN = 0, 8, 16, 24, 30, 32, 34


def _in_cols():
    c = {}
    ar = np.arange
    for h in range(4):
        c["gg%d" % h] = ar(O_GG + 96 * h, O_GG + 96 * h + 96)
    for p in range(2):
        for nm, o in (("gq", O_GQ), ("gk", O_GK)):
            c["%s%d" % (nm, p)] = np.concatenate([ar(o + 96 * p, o + 96 * p + 48), ar(0, 16), ar(o + 96 * p + 48, o + 96 * p + 96)])
    c["glrkr"] = np.concatenate([ar(O_GLR, O_GLR + 16), ar(0, 48), ar(O_KR, O_KR + 32)])
    c["krrot"] = np.concatenate([ar(0, 64), ar(O_KR + 16, O_KR + 32), ar(O_KR, O_KR + 16)])
    for i in range(2):
        c["cc%d" % i] = ar(O_CC + 128 * i, O_CC + 128 * i + 128)
        c["cb%d" % i] = ar(O_CB + 128 * i, O_CB + 128 * i + 128)
        c["ch%d" % i] = ar(O_CH + 128 * i, O_CH + 128 * i + 128)
        c["cg%d" % i] = ar(O_CG + 128 * i, O_CG + 128 * i + 128)
        c["cq%d" % i] = ar(O_CQ + 128 * i, O_CQ + 128 * i + 128)
        c["ckv%d" % i] = ar(O_CKV + 128 * i, O_CKV + 128 * i + 128)
    for i in range(3):
        c["mg%d" % i] = ar(O_MG + 128 * i, O_MG + 128 * i + 128)
    kv = np.concatenate([ar(O_GK, O_GK + 192), ar(O_GV, O_GV + 384), ar(0, 64)])
    for i in range(5):
        c["kvtok%d" % i] = kv[128 * i:128 * i + 128]
    return c


def _pad128(a):
    n = a.shape[-1]
    if n == 128:
        return a
    pad = np.take(a, np.arange(128 - n) % n, axis=-1)
    return np.concatenate([a, pad], axis=-1)


def _kchunks(w, nk):
    return np.ascontiguousarray(w.reshape(nk, 128, 128).transpose(1, 0, 2))


def prep_weights(inp, layers, rank, NB):
    f = np.float32
    depth = len(layers)
    wsl = np.empty((depth, 128, TOTK, 128), f)
    cols = _in_cols()
    for l, gl in enumerate(layers):
        w_in = np.asarray(inp["w_in"][gl], f)
        for name, idx in cols.items():
            off, nk = SLAB_OFF[name]
            wsl[l, :, off:off + nk, :] = _kchunks(_pad128(w_in[:, idx]), 8)
        wk, wv, wq = (np.asarray(inp[k][gl], f) for k in ("mem_wk", "mem_wv", "mem_wq"))
        for i in range(4):
            for nm, w in (("wk", wk), ("wv", wv), ("wq", wq)):
                off, nk = SLAB_OFF["%s%d" % (nm, i)]
                wsl[l, :, off:off + nk, :] = _kchunks(w[:, 128 * i:128 * i + 128], 8)
        wo = np.asarray(inp["mem_wo"][gl], f)
        wout = np.asarray(inp["w_out"][gl], f)
        rows = []
        for h in range(4):
            r = np.arange(96 * h, 96 * h + 96)
            rows.append(np.concatenate([r, r[:32]]))
        for i in range(5):
            rows.append(np.arange(384 + 128 * i, 384 + 128 * i + 128))
        rows = np.concatenate(rows)
        woutr = wout[rows, :]
        for i in range(8):
            off, nk = SLAB_OFF["wout%d" % i]
            wsl[l, :, off:off + nk, :] = _kchunks(woutr[:, 128 * i:128 * i + 128], 9)
            off, nk = SLAB_OFF["wo%d" % i]
            wsl[l, :, off:off + nk, :] = _kchunks(wo[:, 128 * i:128 * i + 128], 4)
    wsl = wsl.reshape(depth, 128, TOTK * 128)
    wuq = np.empty((depth, 128, 2, 2, 576), f)
    wukv = np.empty((depth, 128, 2, 768), f)
    rot = np.concatenate([np.concatenate([np.arange(96 * h, 96 * h + 64), np.arange(96 * h + 80, 96 * h + 96),
                                          np.arange(96 * h + 64, 96 * h + 80)]) for h in range(6)])
    kvc = np.concatenate([np.concatenate([np.arange(128 * h, 128 * h + 64) for h in range(6)]),
                          np.concatenate([np.arange(128 * h + 64, 128 * h + 128) for h in range(6)])])
    wgate = np.empty((depth, 17, 192), f)
    for l, gl in enumerate(layers):
        u = np.asarray(inp["mla_w_uq"][gl], f)
        wuq[l, :, 0] = u.reshape(2, 128, 576).transpose(1, 0, 2)
        wuq[l, :, 1] = u[:, rot].reshape(2, 128, 576).transpose(1, 0, 2)
        wukv[l] = np.asarray(inp["mla_w_ukv"][gl], f)[:, kvc].reshape(2, 128, 768).transpose(1, 0, 2)
        wgate[l, :16] = np.asarray(inp["gla_w_gate"][gl], f)
        wgate[l, 16] = np.asarray(inp["gla_b_gate"][gl], f)
    nv = VC_PER * depth + 14 + NB
    vecs = np.zeros((128, nv), f)

    def colmaj(v, n):
        return np.asarray(v, f).reshape(n, 128).T

    for l, gl in enumerate(layers):
        b = VC_PER * l
        vecs[:, b + V_NMIX:b + V_NMIX + 8] = colmaj(inp["norm_mix"][gl], 8)
        vecs[:, b + V_NX:b + V_NX + 8] = colmaj(inp["norm_xattn"][gl], 8)
        vecs[:, b + V_NMEM:b + V_NMEM + 8] = colmaj(inp["norm_mem"][gl], 8)
        cw = np.asarray(inp["conv_w"][gl], f)
        for i in range(2):
            vecs[:, b + V_CONV + 3 * i:b + V_CONV + 3 * i + 3] = cw[:, 128 * i:128 * i + 128].T
        vecs[:, b + V_QN:b + V_QN + 2] = colmaj(inp["mla_q_norm"][gl], 2)
        vecs[:, b + V_KVN:b + V_KVN + 2] = colmaj(inp["mla_kv_norm"][gl], 2)
        vecs[:96, b + V_GN] = np.asarray(inp["gla_norm"][gl], f)
    b = VC_PER * depth
    last = (rank == 1)
    vecs[:, b:b + 8] = colmaj(inp["norm_final"], 8) if last else 1.0
    vecs[:, b + 10] = 0.0 if last else 1.0
    vecs[:, b + 11] = 1.0 if last else 0.0
    vecs[:, b + 12] = 1.0 if last else 0.0
    vecs[:, b + 13] = 0.0 if last else 1.0
    vecs[:, b + 14:b + 14 + NB] = 1.0
    if last:
        vecs[:, b + 14] = 0.0
    inv_freq = (1.0 / (10000.0 ** (np.arange(0, 32, 2, dtype=np.float32) / np.float32(32)))).astype(f)
    invf2 = np.concatenate([inv_freq, inv_freq])
    sgn = np.concatenate([-np.ones(16, f), np.ones(16, f)])
    vecs[64:96, b + 8] = invf2
    vecs[64:96, b + 9] = invf2 * sgn
    return dict(wsl=wsl, wuq=wuq.reshape(depth, 128, 2 * 2 * 576), wukv=wukv.reshape(depth, 128, 2 * 768),
                wgate=wgate, vecs=vecs)


def build_program(T, depth, TB=512, debug=False, ncores=8):
    NT = TB // 128
    NB = T // TB + 1
    T = NB * TB
    NTT = T // 128
    nc = bass.Bass("TRN2", target_bir_lowering=False)
    NV = VC_PER * depth + 14 + NB
    x_d = nc.dram_tensor("x", [T, 1024], F32, kind="ExternalInput").ap()
    mem_d = nc.dram_tensor("mem", [MEM_LEN, 1024], F32, kind="ExternalInput").ap()
    pos_d = nc.dram_tensor("pos", [1, T], I32, kind="ExternalInput").ap()
    wsl_d = nc.dram_tensor("wsl", [depth, 128, TOTK * 128], F32, kind="ExternalInput").ap()
    wuq_d = nc.dram_tensor("wuq", [depth, 128, 2 * 2 * 576], F32, kind="ExternalInput").ap()
    wukv_d = nc.dram_tensor("wukv", [depth, 128, 2 * 768], F32, kind="ExternalInput").ap()
    wgate_d = nc.dram_tensor("wgate", [depth, 17, 192], F32, kind="ExternalInput").ap()
    vecs_d = nc.dram_tensor("vecs", [128, NV], F32, kind="ExternalInput").ap()
    out_d = nc.dram_tensor("out", [T, 1024], F32, kind="ExternalOutput").ap()
    dbg_d = nc.dram_tensor("dbg", [128, 9 * TB], BF16, kind="ExternalOutput").ap() if debug else None
    wsc_d = nc.dram_tensor("wsc", [depth, 128, TOTK * 128], BF16).ap()
    xsend_d = nc.dram_tensor("xsend", [NB * 1024, TB], F32).ap()
    xrecv_d = nc.dram_tensor("xrecv", [NB * 2 * 1024, TB], F32).ap()
    kscr_d = nc.dram_tensor("kscr", [depth, 6, 96, T], BF16).ap()
    vscr_d = nc.dram_tensor("vscr", [depth, 6, 128, NTT * 65], BF16).ap()
    rope_d = nc.dram_tensor("ropetab", [2, 32, T], F32).ap()

    ctx = ExitStack()
    with ctx:
        def sb(name, shape, dt):
            return ctx.enter_context(nc.sbuf_tensor("sb_" + name, shape, dt))

        S = Sched(nc)
        A = S.add
        banks = [ctx.enter_context(nc.psum_tensor("bank%d" % i, [128, 512], F32)) for i in range(8)]

        def PK(b):
            return ("ps", b)

        rot = {"mm": [0, 1, 2], "st": [3, 4, 7], "o": [5, 6]}
        rot_i = {"mm": 0, "st": 0, "o": 0}

        def nextb(kind):
            b = rot[kind][rot_i[kind] % len(rot[kind])]
            rot_i[kind] += 1
            return b

        ident_f = sb("ident_f", [128, 128], F32)
        ident_b = sb("ident_b", [128, 128], BF16)
        ones_b = sb("ones_b", [128, 128], BF16)
        triSL = sb("triSL", [128, 128], F32)
        triU = sb("triU", [128, 128], F32)
        maskA = sb("maskA", [128, 128], BF16)
        vecs = sb("vecs", [128, NV], F32)
        A("pool", lambda e: e.memset(ident_f[:], 0.0), writes=["ident_f"])
        A("pool", lambda e: e.affine_select(out=ident_f[:], in_=ident_f[:], pattern=[[-1, 128]],
                                            compare_op=ALU.not_equal, fill=1.0, base=0, channel_multiplier=1),
          reads=["ident_f"], writes=["ident_f"])
        A("pool", lambda e: e.tensor_copy(out=ident_b[:], in_=ident_f[:]), reads=["ident_f"], writes=["ident_b"])
        A("pool", lambda e: e.memset(ones_b[:], 1.0), writes=["ones_b"])
        A("pool", lambda e: e.memset(ones_f[:], 1.0), writes=["ones_f"])
        A("pool", lambda e: e.memset(triU[:], 1.0), writes=["triU"])
        A("pool", lambda e: e.affine_select(out=triU[:], in_=triU[:], pattern=[[1, 128]], compare_op=ALU.is_ge,
                                            fill=0.0, base=0, channel_multiplier=-1),
          reads=["triU"], writes=["triU"])
        A("pool", lambda e: e.tensor_copy(out=maskA[:], in_=triU[:]), reads=["triU"], writes=["maskA"])
        A("pool", lambda e: e.memset(triSL[:], 1.0), writes=["triSL"])
        A("pool", lambda e: e.affine_select(out=triSL[:], in_=triSL[:], pattern=[[-1, 128]], compare_op=ALU.is_gt,
                                            fill=0.0, base=0, channel_multiplier=1),
          reads=["triSL"], writes=["triSL"])
        S.dma("sp", lambda e: e.dma_start(out=vecs[:], in_=vecs_d[:, :]), writes=["vecs"])

        def vcol(l, off, n=1, rows=slice(0, 128)):
            b = VC_PER * l + off
            return vecs[rows, b:b + n]

        VB = VC_PER * depth

        NRING = 6
        ring = sb("ring", [128, NRING, 9 * 128], BF16)
        ring_i = [0]

        def load_slab(l, name):
            off, nk = SLAB_OFF[name]
            s = ring_i[0] % NRING
            ring_i[0] += 1
            pcs = sorted({min(kk // CSTEP, NPC - 1) for kk in (off, off + nk - 1)})
            S.dma("sp", lambda e: e.dma_start(out=ring[:, s, 0:nk * 128],
                                              in_=wsc_d[l, :, off * 128:(off + nk) * 128]),
                  reads=[("wsc", l, i) for i in range(pcs[0], pcs[-1] + 1)], writes=[("ring", s)])
            return s

        def slab(s, k, m0=0, m1=128, rows=128):
            return ring[0:rows, s, k * 128 + m0:k * 128 + m1]

        NPC = 16
        CSTEP = TOTK // NPC

        def cast_weights(l):
            n = TOTK * 128
            step = CSTEP * 128
            for i in range(NPC):
                a, b = i * step, (n if i == NPC - 1 else (i + 1) * step)
                gi = l * NPC + i
                S.dma("pool", lambda e, a=a, b=b: e.dma_start(out=wsc_d[l, :, a:b], in_=wsl_d[l, :, a:b]),
                      reads=([("castchain", gi - 3)] if gi >= 3 else []), writes=[("wsc", l, i), ("castchain", gi)], lat=30.0)

        wuqs = [sb("wuq%d" % i, [128, 2, 2, 576], BF16) for i in range(depth)]
        wukvs = [sb("wukv%d" % i, [128, 2, 768], BF16) for i in range(depth)]
        wgates = [sb("wgate%d" % i, [17, 192], BF16) for i in range(depth)]
        hTs = [sb("hT0", [128, 8, TB], F32), sb("hT1", [128, 8, TB], F32)]
        xnT = sb("xnT", [128, 8, TB], BF16)
        hnT = xnT
        ones_f = sb("ones_f", [128, 64], F32)
        otn = sb("otn", [128, TB], BF16)
        sqr = sb("sqr", [128, 3, TB], BF16)
        rstd = sb("rstd", [128, TB], F32)
        catT = sb("catT", [128, 9, TB], BF16)
        tok32 = sb("tok32", [128, 2, 1024], F32)
        memnT = catT[:, 0:4, :].rearrange("p k t -> p (k t)")[:, 0:8 * MEM_LEN].rearrange("p (k t) -> p k t", k=8)
        kmTs = [sb("kmT%d" % i, [128, 4, MEM_LEN], BF16) for i in range(depth)]
        vms = [sb("vm%d" % i, [128, 2, 512], BF16) for i in range(depth)]
        ucar = sb("ucar", [128, depth, 2, 2], F32)
        small = sb("small", [128, 16], F32)
        glrT = sb("glrT", [17, TB], BF16)
        ktok = sb("ktok", [128, NT, 192], F32)
        vtok = sb("vtok", [128, NT, 384], BF16)
        g_e = sb("g_e", [128, 2, 192], F32)
        g_l = sb("g_l", [128, 2, 192], F32)
        g_ek = sb("g_ek", [128, 192], F32)
        kend = sb("kend", [128, NT, 192], BF16)
        e1 = sb("e1", [112, 2, TB], BF16)
        e2 = sb("e2", [112, 2, TB], BF16)
        dec = sb("dec", [48, NT, 4], F32)
        qdec = sb("qdec", [48, 4, TB], BF16)
        kinv = sb("kinv", [48, 4, TB], BF16)
        AT = sb("AT", [128, 2, 4, 128], BF16)
        Ssts = [sb("Sst%d" % i, [48, 4, 96], F32) for i in range(depth)]
        Sbs = [sb("Sb%d" % i, [48, 4, 96], BF16) for i in range(depth)]
        gateG = sb("gateG", [96, 4, TB], BF16)
        chs = sb("chs", [128, TB], F32)
        ubuf = sb("ubuf", [128, 2, TB + 2], F32)
        cva = sb("cva", [128, TB], F32)
        cvb = sb("cvb", [128, TB], F32)
        sg = sb("sg", [128, TB], F32)
        osq = otn[0:96, :].rearrange("p (h c) -> p h c", h=4)
        otmp = cva[0:96, :].rearrange("p (h c) -> p h c", h=4)
        orstd = sg[0:96, :].rearrange("p (h c) -> p h c", h=4)
        raw = sb("raw", [128, 2, TB], F32)
        cqn = sb("cqn", [128, 2, TB], BF16)
        ckvn = sb("ckvn", [128, 2, TB], BF16)
        QT = sb("QT", [128, 6 * TB], BF16)
        KT = sb("KT", [128, 6 * TB], BF16)
        QTv = QT[:, :].rearrange("p (h t) -> p h t", h=6)
        KTv = KT[:, :].rearrange("p (h t) -> p h t", h=6)
        qxT = QT[:, 0:4 * TB].rearrange("p (h t) -> p h t", h=4)
        oxT = KT[:, 0:4 * TB].rearrange("p (h t) -> p h t", h=4)
        Vc = sb("Vc", [128, NT, 6, 65], BF16)
        rtab = sb("rtab", [128, 2, TB], F32)
        rt1 = sb("rt1", [128, TB], F32)
        rt2 = sb("rt2", [128, TB], F32)
        gateM = sb("gateM", [128, 3, TB], BF16)
        NKR = 3
        Kr = sb("Kr", [128, NKR, TB], BF16)
        Vr = sb("Vr", [128, NKR, NT * 65], BF16)
        NPT = 4
        PT = sb("PT", [128, NPT, 512], BF16)
        rsum = sb("rsum", [128, TB], F32)
        posi = rsum[:, :].bitcast(I32)

        pt_i = [0]

        def next_pt():
            i = pt_i[0] % NPT
            pt_i[0] += 1
            return i

        TWO_PI = 2.0 * math.pi
        A("pool", lambda e: e.memset(Vc[:], 1.0), writes=["Vc"])
        A("pool", lambda e: e.memset(glrT[:], 1.0), writes=["glrT"])
        def gen_rope(j):
            tsl = slice(j * TB, (j + 1) * TB)
            R = slice(64, 96)
            S.dma("sp", lambda e, tsl=tsl: e.dma_start(out=posi[64:96, :], in_=pos_d[0:1, tsl].to_broadcast([32, TB])),
                  writes=["rsum"])
            A("dve", lambda e: e.tensor_copy(out=rt1[R, :], in_=posi[R, :]), reads=["rsum"], writes=["rt1"])
            for which in range(2):
                col = VB + 8 + which
                A("dve", lambda e, col=col, which=which: e.tensor_scalar(
                    out=rt2[R, :], in0=rt1[R, :], scalar1=vecs[R, col:col + 1],
                    scalar2=(math.pi / 2 if which == 0 else 0.0), op0=ALU.mult, op1=ALU.add),
                  reads=["rt1", "vecs"], writes=["rt2"])
                A("dve", lambda e: e.tensor_scalar(out=posi[R, :], in0=rt2[R, :], scalar1=1.0 / TWO_PI, scalar2=None,
                                                   op0=ALU.mult), reads=["rt2"], writes=["rsum"])
                A("dve", lambda e: e.tensor_copy(out=cva[R, :], in_=posi[R, :]), reads=["rsum"], writes=["cva"])
                A("dve", lambda e: e.scalar_tensor_tensor(out=rt2[R, :], in0=cva[R, :], scalar=-TWO_PI, in1=rt2[R, :],
                                                          op0=ALU.mult, op1=ALU.add),
                  reads=["cva", "rt2"], writes=["rt2"])
                for thr, sgn_ in ((math.pi, -TWO_PI), (-math.pi, TWO_PI)):
                    op = ALU.is_gt if thr > 0 else ALU.is_lt
                    A("dve", lambda e, thr=thr, sgn_=sgn_, op=op: e.tensor_scalar(
                        out=cva[R, :], in0=rt2[R, :], scalar1=thr, scalar2=sgn_, op0=op, op1=ALU.mult),
                      reads=["rt2"], writes=["cva"])
                    A("dve", lambda e: e.tensor_tensor(out=rt2[R, :], in0=rt2[R, :], in1=cva[R, :], op=ALU.add),
                      reads=["rt2", "cva"], writes=["rt2"])
                A("dve", lambda e: e.tensor_scalar(out=rt2[R, :], in0=rt2[R, :], scalar1=math.pi, scalar2=-math.pi,
                                                   op0=ALU.min, op1=ALU.max), reads=["rt2"], writes=["rt2"])
                A("act", lambda e, which=which: e.activation(out=rtab[R, which, :], in_=rt2[R, :], func=AF.Sin),
                  reads=["rt2"], writes=["rtab"])

        def rmsnorm_feat(src_key, src, nchunk, gcol, dst, dst_key, ndim, extra_reads=(), chunk_keys=False):
            b = 7
            for k in range(nchunk):
                s = k % 3
                A("act", lambda e, k=k, s=s: e.activation(out=sqr[:, s, :], in_=src[:, k, :], func=AF.Square),
                  reads=[(src_key, k) if chunk_keys else src_key] + list(extra_reads), writes=[("sqr", s)])
                A("pe", lambda e, k=k, s=s: e.matmul(banks[b][:, 0:TB], lhsT=ones_b[:, :], rhs=sqr[:, s, :],
                                                    start=(k == 0), stop=(k == nchunk - 1)),
                  reads=[("sqr", s), "ones_b"], writes=[PK(b)])
            A("act", lambda e: e.activation(out=rstd[:, :], in_=banks[b][:, 0:TB], func=AF.Ln, scale=1.0 / ndim,
                                            bias=EPS), reads=[PK(b)], writes=["rstd"])
            A("act", lambda e: e.activation(out=rstd[:, :], in_=rstd[:, :], func=AF.Exp, scale=-0.5),
              reads=["rstd"], writes=["rstd"])
            for k in range(nchunk):
                A("dve",
                  lambda e, k=k: e.scalar_tensor_tensor(out=dst[:, k, :], in0=src[:, k, :], scalar=gcol(k),
                                                        in1=rstd[:, :], op0=ALU.mult, op1=ALU.mult),
                  reads=[(src_key, k) if chunk_keys else src_key, "rstd", "vecs"], writes=[dst_key, (dst_key, "c", k)], cost=0.8)

        def group_mm(l, name, M, rhs_fn, rhs_keys, nk=8, rows=128, ncols=None, perk=None):
            s = load_slab(l, name)
            b = nextb("mm")
            ncols = TB if ncols is None else ncols

            def fn(e):
                ins = None
                for k in range(nk):
                    ins = e.matmul(banks[b][0:M, 0:ncols], lhsT=slab(s, k, 0, M, rows), rhs=rhs_fn(k),
                                   start=(k == 0), stop=(k == nk - 1))
                return ins
            if perk is not None:
                for k in range(nk):
                    A("pe", lambda e, k=k: e.matmul(banks[b][0:M, 0:ncols], lhsT=slab(s, k, 0, M, rows), rhs=rhs_fn(k),
                                                    start=(k == 0), stop=(k == nk - 1)),
                      reads=[("ring", s), (perk, "c", k)], writes=[PK(b)], cost=0.29)
                return b
            A("pe", fn, reads=[("ring", s)] + list(rhs_keys), writes=[PK(b)], cost=nk * (0.29 if ncols >= 512 else 0.17))
            return b

        def mem_kv(l):
            kmT, vm = kmTs[l], vms[l]
            for mt in range(2):
                S.dma("sp", lambda e, mt=mt: e.dma_start(out=tok32[:, mt, :], in_=mem_d[mt * 128:(mt + 1) * 128, :]),
                      writes=[("tok32", mt), ("tok32b", mt)])
                A("act", lambda e, mt=mt: e.activation(out=catT[:, 4:6, :].rearrange("p k t -> p (k t)")[:, 0:1024],
                                                       in_=tok32[:, mt, :], func=AF.Square,
                                                       accum_out=small[:, mt:mt + 1]),
                  reads=[("tok32", mt), ("tok32b", mt)], writes=[("small", mt), ("catT", "gla"), ("catT", "conv"), ("catT", "mla")])
                A("act", lambda e, mt=mt: e.activation(out=small[:, mt:mt + 1], in_=small[:, mt:mt + 1], func=AF.Sqrt,
                                                       scale=1.0 / 1024, bias=EPS),
                  reads=[("small", mt)], writes=[("small", mt)])
                A("dve", lambda e, mt=mt: e.reciprocal(out=small[:, mt:mt + 1], in_=small[:, mt:mt + 1]),
                  reads=[("small", mt)], writes=[("small", mt)])
                A("dve", lambda e, mt=mt: e.tensor_scalar(out=tok32[:, mt, :], in0=tok32[:, mt, :],
                                                          scalar1=small[:, mt:mt + 1], scalar2=None, op0=ALU.mult),
                  reads=[("tok32", mt), ("tok32b", mt), ("small", mt)], writes=[("tok32", mt), ("tok32b", mt)])
                for half in range(2):
                    b = nextb("mm")

                    def fn(e, mt=mt, half=half, b=b):
                        ins = None
                        for kk in range(4):
                            k = half * 4 + kk
                            ins = e.transpose(out=banks[b][:, kk * 128:(kk + 1) * 128],
                                              in_=tok32[:, mt, k * 128:(k + 1) * 128], identity=ident_f[:, :])
                        return ins
                    A("pe", fn, reads=[("tok32", mt), ("tok32b", mt), "ident_f"], writes=[PK(b)])
                    for kk in range(4):
                        k = half * 4 + kk
                        A("dve", lambda e, mt=mt, k=k, kk=kk, b=b: e.tensor_scalar(
                            out=memnT[:, k, mt * 128:(mt + 1) * 128], in0=banks[b][:, kk * 128:(kk + 1) * 128],
                            scalar1=vcol(l, V_NMEM + k), scalar2=None, op0=ALU.mult),
                          reads=[PK(b), "vecs"], writes=[("catT", "gla")])
            for h in range(4):
                b = group_mm(l, "wk%d" % h, 128, lambda k: memnT[:, k, :], [("catT", "gla")], ncols=MEM_LEN)
                A("act", lambda e, h=h, b=b: e.copy(out=kmT[:, h, :], in_=banks[b][:, 0:MEM_LEN]),
                  reads=[PK(b)], writes=[("kmT", l)])
            for i in range(4):
                s = load_slab(l, "wv%d" % i)
                b = nextb("mm")

                def fn(e, s=s, b=b):
                    ins = None
                    for mt in range(2):
                        for k in range(8):
                            ins = e.matmul(banks[b][:, mt * 128:(mt + 1) * 128],
                                           lhsT=memnT[:, k, mt * 128:(mt + 1) * 128], rhs=slab(s, k),
                                           start=(k == 0), stop=(k == 7))
                    return ins
                A("pe", fn, reads=[("ring", s), ("catT", "gla")], writes=[PK(b)])
                A("act", lambda e, i=i, b=b: e.copy(
                    out=vm[:, :, i * 128:(i + 1) * 128],
                    in_=banks[b][:, 0:256].rearrange("p (m c) -> p m c", m=2)),
                  reads=[PK(b)], writes=[("vm", l)])


        def do_layer(l):
            wuq, wukv, wgate, kmT, vm, Sst, Sb = wuqs[l], wukvs[l], wgates[l], kmTs[l], vms[l], Ssts[l], Sbs[l]
            S.dma("pool", lambda e: e.dma_start(out=wuq[:].rearrange("p a k c -> p (a k c)"), in_=wuq_d[l, :, :]),
                  writes=[("wuq", l)])
            S.dma("pool", lambda e: e.dma_start(out=wukv[:].rearrange("p k c -> p (k c)"), in_=wukv_d[l, :, :]),
                  writes=[("wukv", l)])
            S.dma("pool", lambda e: e.dma_start(out=wgate[:], in_=wgate_d[l, :, :]), writes=[("wgate", l)])

            A("pool", lambda e: e.memset(Sst[:], 0.0), writes=[("Sst", l)])
            A("pool", lambda e: e.memset(Sb[:], 0.0), writes=[("Sb", l)])
            A("pool", lambda e: e.memset(ucar[:, l, :, :], 0.0), writes=[("ucar", l)])

        if True:
            def do_block(l, j):
                wuq, wukv, wgate, kmT, vm, Sst, Sb = wuqs[l], wukvs[l], wgates[l], kmTs[l], vms[l], Ssts[l], Sbs[l]
                tsl = slice(j * TB, (j + 1) * TB)
                hpar = j % 2
                hT = hTs[hpar]
                HK = ("hT", hpar)
                fcol = vecs[:, VB + 14 + j:VB + 15 + j]

                rmsnorm_feat(HK, hT, 8, lambda k: vcol(l, V_NMIX + k), xnT, "xnT", 1024)

                def xk(k):
                    return xnT[:, k, :]

                b = group_mm(l, "glrkr", 96, xk, ["xnT"], perk="xnT")
                A("act", lambda e, b=b: e.copy(out=glrT[0:16, :], in_=banks[b][0:16, 0:TB]), reads=[PK(b)], writes=["glrT"])
                A("act", lambda e, b=b: e.copy(out=rt1[64:96, :], in_=banks[b][64:96, 0:TB]), reads=[PK(b)], writes=["rt1"])
                b = group_mm(l, "krrot", 96, xk, ["xnT"])
                A("dve", lambda e: e.tensor_tensor(out=rt1[64:96, :], in0=rt1[64:96, :], in1=rtab[64:96, 0, :], op=ALU.mult),
                  reads=["rt1", "rtab"], writes=["rt1"])
                A("dve", lambda e, b=b: e.tensor_tensor(out=rt2[64:96, :], in0=banks[b][64:96, 0:TB], in1=rtab[64:96, 1, :],
                                                        op=ALU.mult), reads=[PK(b), "rtab"], writes=["rt2"])
                A("dve", lambda e: e.tensor_tensor(out=rt1[64:96, :], in0=rt1[64:96, :], in1=rt2[64:96, :], op=ALU.add),
                  reads=["rt1", "rt2"], writes=["rt1"])
                for h in range(6):
                    A("pool", lambda e, h=h: e.tensor_copy(out=KTv[64:96, h, :], in_=rt1[64:96, :]),
                      reads=["rt1"], writes=["KT"])
                for i in range(5):
                    s = load_slab(l, "kvtok%d" % i)
                    b = nextb("mm")

                    def fn(e, s=s, b=b):
                        ins = None
                        for n in range(NT):
                            for k in range(8):
                                ins = e.matmul(banks[b][:, n * 128:(n + 1) * 128],
                                               lhsT=xnT[:, k, n * 128:(n + 1) * 128], rhs=slab(s, k),
                                               start=(k == 0), stop=(k == 7))
                        return ins
                    A("pe", fn, reads=[("ring", s), "xnT"], writes=[PK(b)], cost=NT * 8 * 0.1)
                    bv = banks[b][:, 0:NT * 128].rearrange("p (n c) -> p n c", n=NT)
                    if i == 0:
                        A("act", lambda e, bv=bv: e.copy(out=ktok[:, :, 0:128], in_=bv), reads=[PK(b)], writes=["ktok"])
                    elif i == 1:
                        A("act", lambda e, bv=bv: e.copy(out=ktok[:, :, 128:192], in_=bv[:, :, 0:64]),
                          reads=[PK(b)], writes=["ktok"])
                        A("dve", lambda e, bv=bv: e.tensor_copy(out=vtok[:, :, 0:64], in_=bv[:, :, 64:128]),
                          reads=[PK(b)], writes=["vtok"])
                    elif i < 4:
                        c0 = 64 + (i - 2) * 128
                        A("act", lambda e, bv=bv, c0=c0: e.copy(out=vtok[:, :, c0:c0 + 128], in_=bv),
                          reads=[PK(b)], writes=["vtok"])
                    else:
                        A("dve", lambda e, bv=bv: e.tensor_copy(out=vtok[:, :, 320:384], in_=bv[:, :, 0:64]),
                          reads=[PK(b)], writes=["vtok"])
                for n in range(NT):
                    nsl = slice(n * 128, (n + 1) * 128)
                    p = n % 2
                    bz = 7
                    A("pe", lambda e, nsl=nsl: e.matmul(banks[bz][:, 0:192], lhsT=glrT[0:17, nsl], rhs=wgate[0:17, :],
                                                        start=True, stop=True),
                      reads=["glrT", ("wgate", l)], writes=[PK(bz)])
                    A("act", lambda e, p=p: e.activation(out=g_e[:, p, :], in_=banks[bz][:, 0:192], func=AF.Exp, scale=-1.0),
                      reads=[PK(bz)], writes=[("g_e", p)])
                    A("act", lambda e, p=p: e.activation(out=g_l[:, p, :], in_=g_e[:, p, :], func=AF.Ln, bias=1.0),
                      reads=[("g_e", p)], writes=[("g_l", p)])
                    A("pe", lambda e, p=p: e.matmul(banks[bz][:, 256:448], lhsT=triSL[:, :], rhs=g_l[:, p, :],
                                                    start=True, stop=True),
                      reads=[("g_l", p), "triSL"], writes=[PK(bz)], cost=0.6)
                    A("act", lambda e: e.activation(out=g_ek[:, :], in_=banks[bz][:, 256:448], func=AF.Exp,
                                                    scale=-1.0 / GLA_TAU), reads=[PK(bz)], writes=["g_ek"])
                    A("dve", lambda e, n=n: e.tensor_tensor(out=kend[:, n, :], in0=ktok[:, n, :], in1=g_ek[:, :], op=ALU.mult),
                      reads=["ktok", "g_ek"], writes=["kend"])
                    bb = nextb("mm")

                    def fn(e, p=p, bb=bb):
                        ins = None
                        for h in range(4):
                            ins = e.matmul(banks[bb][0:48, h * 128:(h + 1) * 128], lhsT=g_l[:, p, h * 48:(h + 1) * 48],
                                           rhs=triU[:, :], start=True, stop=True)
                        return ins
                    A("pe", fn, reads=[("g_l", p), "triU"], writes=[PK(bb)], cost=1.6)
                    bbv = banks[bb][0:48, 0:512].rearrange("p (h t) -> p h t", h=4)
                    bb4 = banks[bb][0:48, 0:512].rearrange("p (q o t) -> p q o t", q=2, o=2)
                    for od in range(2):
                        rws = slice(64 * od, 64 * od + 48)
                        A("act", lambda e, bb4=bb4, nsl=nsl, od=od, rws=rws: e.activation(
                            out=e1[rws, :, nsl], in_=bb4[:, :, od, :], func=AF.Exp, scale=-1.0 / GLA_TAU),
                          reads=[PK(bb)], writes=["e1"], cost=0.35)
                        A("act", lambda e, bb4=bb4, nsl=nsl, od=od, rws=rws: e.activation(
                            out=e2[rws, :, nsl], in_=bb4[:, :, od, :], func=AF.Exp, scale=1.0 / GLA_TAU),
                          reads=[PK(bb)], writes=["e2"], cost=0.35)
                    A("act", lambda e, bbv=bbv, n=n: e.activation(out=dec[:, n, :], in_=bbv[:, :, 127], func=AF.Exp,
                                                                   scale=-1.0 / GLA_TAU),
                      reads=[PK(bb)], writes=["dec"])
                for p2 in range(2):
                    b = group_mm(l, "gq%d" % p2, 112, xk, ["xnT"])
                    for od in range(2):
                        rws = slice(64 * od, 64 * od + 48)
                        A("dve", lambda e, p2=p2, od=od, rws=rws, b=b: e.scalar_tensor_tensor(
                            out=qdec[:, 2 * p2 + od, :], in0=banks[b][rws, 0:TB], scalar=GLA_DK ** -0.5, in1=e1[rws, p2, :],
                            op0=ALU.mult, op1=ALU.mult), reads=[PK(b), "e1"], writes=["qdec"])
                for p2 in range(2):
                    b = group_mm(l, "gk%d" % p2, 112, xk, ["xnT"])
                    for od in range(2):
                        rws = slice(64 * od, 64 * od + 48)
                        A("dve", lambda e, p2=p2, od=od, rws=rws, b=b: e.tensor_tensor(
                            out=kinv[:, 2 * p2 + od, :], in0=banks[b][rws, 0:TB], in1=e2[rws, p2, :], op=ALU.mult),
                          reads=[PK(b), "e2"], writes=["kinv"])
                for h in range(4):
                    b = group_mm(l, "gg%d" % h, 96, xk, ["xnT"])
                    A("act", lambda e, h=h, b=b: e.activation(out=gateG[:, h, :], in_=banks[b][0:96, 0:TB], func=AF.Silu),
                      reads=[PK(b)], writes=["gateG"])

                for n in range(NT):
                    nsl = slice(n * 128, (n + 1) * 128)
                    p = n % 2
                    bA, bO, bU = 5, 6, 7

                    def fnA(e, nsl=nsl):
                        ins = None
                        for h in range(4):
                            ins = e.matmul(banks[bA][:, h * 128:(h + 1) * 128], lhsT=kinv[:, h, nsl], rhs=qdec[:, h, nsl],
                                           start=True, stop=True)
                        return ins
                    A("pe", fnA, reads=["kinv", "qdec"], writes=[PK(bA)], cost=0.45)
                    A("dve", lambda e, p=p: e.tensor_tensor(
                        out=AT[:, p, :, :], in0=banks[bA][:, :].rearrange("p (h c) -> p h c", h=4),
                        in1=maskA[:, :].unsqueeze(1).to_broadcast([128, 4, 128]), op=ALU.mult),
                      reads=[PK(bA), "maskA"], writes=[("AT", p)])

                    def fnO(e, n=n, nsl=nsl, p=p):
                        ins = None
                        for h in range(4):
                            e.matmul(banks[bO][0:96, h * 128:(h + 1) * 128], lhsT=vtok[:, n, h * 96:(h + 1) * 96],
                                     rhs=AT[:, p, h, :], start=True, stop=False)
                            ins = e.matmul(banks[bO][0:96, h * 128:(h + 1) * 128], lhsT=Sb[:, h, :], rhs=qdec[:, h, nsl],
                                           start=False, stop=True)
                        return ins
                    A("pe", fnO, reads=["vtok", ("AT", p), ("Sb", l), "qdec"], writes=[PK(bO)], cost=0.9)

                    def fnU(e, n=n):
                        ins = None
                        for h in range(4):
                            ins = e.matmul(banks[bU][0:48, h * 96:(h + 1) * 96], lhsT=kend[:, n, h * 48:(h + 1) * 48],
                                           rhs=vtok[:, n, h * 96:(h + 1) * 96], start=True, stop=True)
                        return ins
                    A("pe", fnU, reads=["kend", "vtok"], writes=[PK(bU)], cost=0.45)
                    A("dve", lambda e, n=n: e.tensor_tensor(out=Sst[:, :, :], in0=Sst[:, :, :],
                                                            in1=dec[:, n, :].unsqueeze(2).to_broadcast([48, 4, 96]),
                                                            op=ALU.mult), reads=[("Sst", l), "dec"], writes=[("Sst", l)])
                    A("dve", lambda e: e.tensor_tensor(out=Sst[:, :, :], in0=Sst[:, :, :],
                                                       in1=banks[bU][0:48, 0:384].rearrange("p (h c) -> p h c", h=4),
                                                       op=ALU.add), reads=[("Sst", l), PK(bU)], writes=[("Sst", l)])
                    A("act", lambda e: e.copy(out=Sb[:, :, :], in_=Sst[:, :, :]), reads=[("Sst", l)], writes=[("Sb", l)])
                    bOv = banks[bO][0:96, :].rearrange("p (h c) -> p h c", h=4)
                    A("act", lambda e, bOv=bOv: e.activation(out=osq[:, :, :], in_=bOv, func=AF.Square),
                      reads=[PK(bO)], writes=["otn"])
                    A("act", lambda e, bOv=bOv: e.copy(out=otmp[:, :, :], in_=bOv), reads=[PK(bO)], writes=["cva"])
                    A("pe", lambda e: e.matmul(banks[bA][0:96, :], lhsT=ones_b[0:96, 0:96],
                                               rhs=osq[:, :, :].rearrange("p h c -> p (h c)"), start=True, stop=True),
                      reads=["otn", "ones_b"], writes=[PK(bA)])
                    A("act", lambda e: e.activation(out=orstd[:, :, :].rearrange("p h c -> p (h c)"), in_=banks[bA][0:96, :],
                                                    func=AF.Ln, scale=1.0 / GLA_DV, bias=EPS),
                      reads=[PK(bA)], writes=["sg"])
                    A("act", lambda e: e.activation(out=orstd[:, :, :], in_=orstd[:, :, :], func=AF.Exp, scale=-0.5),
                      reads=["sg"], writes=["sg"])
                    A("dve", lambda e: e.scalar_tensor_tensor(out=otmp[:, :, :], in0=otmp[:, :, :],
                                                              scalar=vcol(l, V_GN, 1, slice(0, 96)), in1=orstd[:, :, :],
                                                              op0=ALU.mult, op1=ALU.mult),
                      reads=["cva", "sg", "vecs"], writes=["cva"])
                    A("dve", lambda e, nsl=nsl: e.tensor_tensor(out=catT[0:96, 0:4, nsl], in0=otmp[:, :, :],
                                                                in1=gateG[:, :, nsl], op=ALU.mult),
                      reads=["cva", "gateG"], writes=[("catT", "gla")])

                for i in range(2):
                    b = group_mm(l, "ch%d" % i, 128, xk, ["xnT"])
                    A("act", lambda e, b=b: e.copy(out=chs[:, :], in_=banks[b][:, 0:TB]), reads=[PK(b)], writes=["chs"])
                    A("pool", lambda e, i=i: e.tensor_copy(out=ubuf[:, i, 0:2], in_=ucar[:, l, i, :]),
                      reads=[("ucar", l)], writes=["ubuf"])
                    b = group_mm(l, "cc%d" % i, 128, xk, ["xnT"])
                    A("dve", lambda e, b=b, i=i: e.tensor_tensor(out=ubuf[:, i, 2:TB + 2], in0=banks[b][:, 0:TB], in1=chs[:, :],
                                                                 op=ALU.mult), reads=[PK(b), "chs"], writes=["ubuf"])
                    cw = V_CONV + 3 * i
                    A("dve", lambda e, i=i, cw=cw: e.tensor_scalar(out=cva[:, :], in0=ubuf[:, i, 2:TB + 2],
                                                                   scalar1=vcol(l, cw + 2), scalar2=None, op0=ALU.mult),
                      reads=["ubuf", "vecs"], writes=["cva"])
                    A("dve", lambda e, i=i, cw=cw: e.scalar_tensor_tensor(out=cva[:, :], in0=ubuf[:, i, 1:TB + 1],
                                                                          scalar=vcol(l, cw + 1), in1=cva[:, :],
                                                                          op0=ALU.mult, op1=ALU.add),
                      reads=["ubuf", "cva", "vecs"], writes=["cva"])
                    A("dve", lambda e, i=i, cw=cw: e.scalar_tensor_tensor(out=cva[:, :], in0=ubuf[:, i, 0:TB],
                                                                          scalar=vcol(l, cw + 0), in1=cva[:, :],
                                                                          op0=ALU.mult, op1=ALU.add),
                      reads=["ubuf", "cva", "vecs"], writes=["cva"])
                    A("dve", lambda e, i=i: e.tensor_scalar(out=ucar[:, l, i, :], in0=ubuf[:, i, TB:TB + 2], scalar1=fcol,
                                                            scalar2=None, op0=ALU.mult),
                      reads=["ubuf", "vecs"], writes=[("ucar", l)])
                    b = group_mm(l, "cg%d" % i, 128, xk, ["xnT"])
                    A("act", lambda e, b=b: e.activation(out=sg[:, :], in_=banks[b][:, 0:TB], func=AF.Silu),
                      reads=[PK(b)], writes=["sg"])
                    A("pool", lambda e: e.tensor_tensor(out=sg[:, :], in0=sg[:, :], in1=cva[:, :], op=ALU.mult),
                      reads=["sg", "cva"], writes=["sg"])
                    b = group_mm(l, "cb%d" % i, 128, xk, ["xnT"])
                    A("dve", lambda e, b=b, i=i: e.tensor_tensor(out=catT[:, 4 + i, :], in0=banks[b][:, 0:TB], in1=sg[:, :],
                                                                 op=ALU.mult), reads=[PK(b), "sg"], writes=[("catT", "conv")])

                for nm, dstn, dkey, vq in (("cq", cqn, "cqn", V_QN), ("ckv", ckvn, "ckvn", V_KVN)):
                    for i in range(2):
                        b = group_mm(l, "%s%d" % (nm, i), 128, xk, ["xnT"])
                        A("act", lambda e, b=b, i=i: e.copy(out=raw[:, i, :], in_=banks[b][:, 0:TB]), reads=[PK(b)], writes=[("raw", i)])
                    rmsnorm_feat("raw", raw, 2, lambda k, vq=vq: vcol(l, vq + k), dstn, dkey, 256, chunk_keys=True)
                for i in range(3):
                    b = group_mm(l, "mg%d" % i, 128, xk, ["xnT"])
                    A("act", lambda e, b=b, i=i: e.activation(out=gateM[:, i, :], in_=banks[b][:, 0:TB], func=AF.Silu),
                      reads=[PK(b)], writes=["gateM"])
                for h in range(6):
                    b1, b2 = nextb("mm"), nextb("mm")

                    def fnq(e, h=h, b1=b1, b2=b2):
                        ins = None
                        for a, b in ((0, b1), (1, b2)):
                            for k in range(2):
                                ins = e.matmul(banks[b][0:96, 0:TB], lhsT=wuq[:, a, k, h * 96:(h + 1) * 96], rhs=cqn[:, k, :],
                                               start=(k == 0), stop=(k == 1))
                        return ins
                    A("pe", fnq, reads=[("wuq", l), "cqn"], writes=[PK(b1), PK(b2)], cost=1.16)
                    A("act", lambda e, h=h, b1=b1: e.copy(out=QTv[0:64, h, :], in_=banks[b1][0:64, 0:TB]),
                      reads=[PK(b1)], writes=["QT"])
                    A("dve", lambda e, b1=b1: e.tensor_tensor(out=rt1[64:96, :], in0=banks[b1][64:96, 0:TB], in1=rtab[64:96, 0, :],
                                                              op=ALU.mult), reads=[PK(b1), "rtab"], writes=["rt1"])
                    A("dve", lambda e, b2=b2: e.tensor_tensor(out=rt2[64:96, :], in0=banks[b2][64:96, 0:TB], in1=rtab[64:96, 1, :],
                                                              op=ALU.mult), reads=[PK(b2), "rtab"], writes=["rt2"])
                    A("dve", lambda e, h=h: e.tensor_tensor(out=QTv[64:96, h, :], in0=rt1[64:96, :], in1=rt2[64:96, :],
                                                            op=ALU.add), reads=["rt1", "rt2"], writes=["QT"])
                for h in range(6):
                    b = nextb("mm")

                    def fnk(e, h=h, b=b):
                        ins = None
                        for k in range(2):
                            ins = e.matmul(banks[b][0:64, 0:TB], lhsT=wukv[:, k, h * 64:(h + 1) * 64], rhs=ckvn[:, k, :],
                                           start=(k == 0), stop=(k == 1))
                        return ins
                    A("pe", fnk, reads=[("wukv", l), "ckvn"], writes=[PK(b)], cost=0.58)
                    A("act", lambda e, h=h, b=b: e.copy(out=KTv[0:64, h, :], in_=banks[b][0:64, 0:TB]),
                      reads=[PK(b)], writes=["KT"])
                for n in range(NT):
                    b = nextb("mm")

                    def fnv(e, n=n, b=b):
                        ins = None
                        for k in range(2):
                            ins = e.matmul(banks[b][:, 0:384], lhsT=ckvn[:, k, n * 128:(n + 1) * 128], rhs=wukv[:, k, 384:768],
                                           start=(k == 0), stop=(k == 1))
                        return ins
                    A("pe", fnv, reads=[("wukv", l), "ckvn"], writes=[PK(b)], cost=0.45)
                    A("act", lambda e, n=n, b=b: e.copy(out=Vc[:, n, :, 0:64],
                                                        in_=banks[b][:, 0:384].rearrange("p (h c) -> p h c", h=6)),
                      reads=[PK(b)], writes=["Vc"])
                sc = 1.0 / math.sqrt(MLA_NOPE + MLA_ROPE)
                for h in range(6):
                    bO = nextb("o")
                    first = [True]
                    for kb in range(j + 1):
                        if kb < j:
                            rs = (h * (j + 1) + kb) % NKR
                            S.dma("sp", lambda e, h=h, kb=kb, rs=rs: e.dma_start(
                                out=Kr[0:96, rs, :], in_=kscr_d[l, h, :, kb * TB:(kb + 1) * TB]),
                                reads=[("kscr", l, kb)], writes=[("Kr", rs)])
                            S.dma("sp", lambda e, h=h, kb=kb, rs=rs: e.dma_start(
                                out=Vr[:, rs, :], in_=vscr_d[l, h, :, kb * NT * 65:(kb + 1) * NT * 65]),
                                reads=[("vscr", l, kb, h)], writes=[("Vr", rs)])
                        for kt in range(NT):
                            q0 = kt * 128 if kb == j else 0
                            nq = TB - q0
                            bs = nextb("st")
                            pi = next_pt()
                            if kb < j:
                                lhs = Kr[0:96, rs, kt * 128:(kt + 1) * 128]
                                kkeys = [("Kr", rs)]
                                vkeys = [("Vr", rs)]
                                vap = Vr[:, rs, kt * 65:(kt + 1) * 65]
                            else:
                                lhs = KTv[0:96, h, kt * 128:(kt + 1) * 128]
                                kkeys = ["KT"]
                                vkeys = ["Vc"]
                                vap = Vc[:, kt, h, :]
                            A("pe", lambda e, lhs=lhs, q0=q0, nq=nq, bs=bs, h=h: e.matmul(
                                banks[bs][:, 0:nq], lhsT=lhs, rhs=QTv[0:96, h, q0:TB], start=True, stop=True),
                              reads=kkeys + ["QT"], writes=[PK(bs)], cost=0.25)
                            A("act", lambda e, bs=bs, pi=pi, nq=nq: e.activation(out=PT[:, pi, 0:nq], in_=banks[bs][:, 0:nq],
                                                                                 func=AF.Exp, scale=sc),
                              reads=[PK(bs)], writes=[("PT", pi)], cost=0.6)
                            if kb == j:
                                A("pool", lambda e, pi=pi: e.tensor_tensor(out=PT[:, pi, 0:128], in0=PT[:, pi, 0:128],
                                                                           in1=maskA[:, :], op=ALU.mult),
                                  reads=[("PT", pi), "maskA"], writes=[("PT", pi)])

                            if debug == 3 and j == 1 and h == 0 and kb == 0 and kt == 0:
                                S.dma("pool", lambda e, rs=rs: e.dma_start(out=dbg_d[:, 0:512], in_=Kr[:, rs, :]), reads=[("Kr", rs)], writes=["dbg1"])
                                S.dma("pool", lambda e, pi=pi: e.dma_start(out=dbg_d[:, 512:1024], in_=PT[:, pi, :]), reads=[("PT", pi)], writes=["dbg2"])
                                S.dma("pool", lambda e, rs=rs: e.dma_start(out=dbg_d[:, 1024:1024 + 260], in_=Vr[:, rs, :]), reads=[("Vr", rs)], writes=["dbg3"])
                                S.dma("pool", lambda e, rs=rs: e.dma_start(out=dbg_d[:, 2048:2048 + 512], in_=QTv[:, 0, :]), reads=["QT"], writes=["dbg4"])

                            A("pe", lambda e, pi=pi, q0=q0, nq=nq, vap=vap, bO=bO, st=(kb == 0 and kt == 0): e.matmul(
                                banks[bO][0:65, q0:TB], lhsT=vap, rhs=PT[:, pi, 0:nq], start=st, stop=True,
                                skip_group_check=True), reads=[("PT", pi)] + vkeys, writes=[PK(bO)], cost=0.33)
                    hr = slice((h % 2) * 64, (h % 2) * 64 + 64)
                    A("act", lambda e, bO=bO: e.copy(out=chs[64:65, :], in_=banks[bO][64:65, 0:TB]), reads=[PK(bO)], writes=["chs"])
                    bb2 = nextb("mm")
                    A("pe", lambda e, bb2=bb2: e.matmul(banks[bb2][0:64, 0:TB], lhsT=ones_f[64:65, 0:64], rhs=chs[64:65, :],
                                                        start=True, stop=True), reads=["chs", "ones_f"], writes=[PK(bb2)], cost=1.0)
                    A("act", lambda e, bb2=bb2: e.activation(out=cvb[0:64, :], in_=banks[bb2][0:64, 0:TB], func=AF.Ln),
                      reads=[PK(bb2)], writes=["cvb"])
                    A("act", lambda e: e.activation(out=cvb[0:64, :], in_=cvb[0:64, :], func=AF.Exp, scale=-1.0),
                      reads=["cvb"], writes=["cvb"])
                    A("dve", lambda e, bO=bO, hr=hr: e.tensor_tensor(out=otn[hr, :], in0=banks[bO][0:64, 0:TB], in1=cvb[0:64, :],
                                                                     op=ALU.mult), reads=[PK(bO), "cvb"], writes=["otn"])
                    if h % 2 == 1:
                        c = h // 2
                        A("dve", lambda e, c=c: e.tensor_tensor(out=catT[:, 6 + c, :], in0=otn[:, :], in1=gateM[:, c, :], op=ALU.mult),
                          reads=["otn", "gateM"], writes=[("catT", "mla")])

                if j + 1 < NB:
                    A("act", lambda e: e.activation(out=Vc[:].rearrange("p n h c -> p (n h c)"),
                                                    in_=Vc[:].rearrange("p n h c -> p (n h c)"), func=AF.Copy, scale=fcol),
                      reads=["Vc", "vecs"], writes=["Vc"], cost=1.3)
                    S.dma("pool", lambda e, tsl=tsl: e.dma_start(out=kscr_d[l, :, :, tsl].rearrange("h p t -> p h t"),
                                                                 in_=KTv[0:96, :, :]),
                          reads=["KT"], writes=[("kscr", l, j)])
                    for h in range(6):
                        S.dma("pool", lambda e, h=h: e.dma_start(
                            out=vscr_d[l, h, :, j * NT * 65:(j + 1) * NT * 65].rearrange("p (n c) -> p n c", n=NT),
                            in_=Vc[:, :, h, :]), reads=["Vc"], writes=[("vscr", l, j, h)])
                    A("pool", lambda e: e.memset(Vc[:, :, :, 64:65], 1.0), reads=[], writes=["Vc"])
                A("dve", lambda e: e.tensor_scalar(out=Sst[:, :, :], in0=Sst[:, :, :], scalar1=fcol[0:48, :], scalar2=None,
                                                   op0=ALU.mult), reads=[("Sst", l), "vecs"], writes=[("Sst", l)])
                A("act", lambda e: e.copy(out=Sb[:, :, :], in_=Sst[:, :, :]), reads=[("Sst", l)], writes=[("Sb", l)])

                for co in range(8):
                    s = load_slab(l, "wout%d" % co)
                    b = nextb("mm")

                    def fno(e, s=s, b=b):
                        ins = None
                        for k in range(9):
                            rows = 96 if k < 4 else 128
                            ins = e.matmul(banks[b][:, 0:TB], lhsT=slab(s, k, 0, 128, rows), rhs=catT[0:rows, k, :],
                                           start=(k == 0), stop=(k == 8))
                        return ins
                    A("pe", fno, reads=[("ring", s), ("catT", "gla"), ("catT", "conv"), ("catT", "mla")], writes=[PK(b)], cost=2.6)
                    A("dve", lambda e, co=co, b=b: e.tensor_tensor(out=hT[:, co, :], in0=hT[:, co, :], in1=banks[b][:, 0:TB],
                                                                   op=ALU.add), reads=[PK(b), HK], writes=[HK])

                if j == 0:
                    mem_kv(l)
                rmsnorm_feat(HK, hT, 8, lambda k: vcol(l, V_NX + k), hnT, "xnT", 1024)
                for h in range(4):
                    b = group_mm(l, "wq%d" % h, 128, lambda k: hnT[:, k, :], ["xnT"], perk=("xnT" if h == 0 else None))
                    A("act", lambda e, h=h, b=b: e.copy(out=qxT[:, h, :], in_=banks[b][:, 0:TB]), reads=[PK(b)], writes=["QT"])
                scx = 1.0 / math.sqrt(MEM_HD)
                for h in range(4):
                    pis = []
                    for mt in range(2):
                        bs = nextb("st")
                        pi = next_pt()
                        pis.append(pi)
                        A("pe", lambda e, h=h, mt=mt, bs=bs: e.matmul(banks[bs][:, 0:TB], lhsT=kmT[:, h, mt * 128:(mt + 1) * 128],
                                                                      rhs=qxT[:, h, :], start=True, stop=True),
                          reads=[("kmT", l), "QT"], writes=[PK(bs)])
                        A("act", lambda e, bs=bs, pi=pi: e.activation(out=PT[:, pi, 0:TB], in_=banks[bs][:, 0:TB], func=AF.Exp,
                                                                      scale=scx), reads=[PK(bs)], writes=[("PT", pi)])
                    b1, b2 = nextb("o"), nextb("mm")

                    def fnx(e, h=h, pis=tuple(pis), b1=b1, b2=b2):
                        ins = None
                        for mt in range(2):
                            e.matmul(banks[b1][:, 0:TB], lhsT=ones_b[:, :], rhs=PT[:, pis[mt], 0:TB],
                                     start=(mt == 0), stop=(mt == 1))
                        for mt in range(2):
                            ins = e.matmul(banks[b2][:, 0:TB], lhsT=vm[:, mt, h * 128:(h + 1) * 128], rhs=PT[:, pis[mt], 0:TB],
                                           start=(mt == 0), stop=(mt == 1))
                        return ins
                    A("pe", fnx, reads=[("PT", pis[0]), ("PT", pis[1]), ("vm", l), "ones_b"], writes=[PK(b1), PK(b2)], cost=1.16)
                    A("act", lambda e, b1=b1: e.activation(out=rsum[:, :], in_=banks[b1][:, 0:TB], func=AF.Ln),
                      reads=[PK(b1)], writes=["rsum"])
                    A("act", lambda e: e.activation(out=rsum[:, :], in_=rsum[:, :], func=AF.Exp, scale=-1.0),
                      reads=["rsum"], writes=["rsum"])
                    A("dve", lambda e, h=h, b2=b2: e.tensor_tensor(out=oxT[:, h, :], in0=banks[b2][:, 0:TB], in1=rsum[:, :],
                                                                   op=ALU.mult), reads=[PK(b2), "rsum"], writes=["KT"])
                for co in range(8):
                    s = load_slab(l, "wo%d" % co)
                    b = nextb("mm")

                    def fnw(e, s=s, b=b):
                        ins = None
                        for k in range(4):
                            ins = e.matmul(banks[b][:, 0:TB], lhsT=slab(s, k), rhs=oxT[:, k, :], start=(k == 0), stop=(k == 3))
                        return ins
                    A("pe", fnw, reads=[("ring", s), "KT"], writes=[PK(b)], cost=1.16)
                    A("dve", lambda e, co=co, b=b: e.tensor_tensor(out=hT[:, co, :], in0=hT[:, co, :], in1=banks[b][:, 0:TB],
                                                                   op=ALU.add), reads=[PK(b), HK], writes=[HK])


        fa_c, fb_c = vecs[:, VB + 10:VB + 11], vecs[:, VB + 11:VB + 12]
        ffin_c, omf_c = vecs[:, VB + 12:VB + 13], vecs[:, VB + 13:VB + 14]
        GROUPS = [[2 * p, 2 * p + 1] for p in range(ncores // 2)]

        def do_iter(j):
            hpar = j % 2
            hT = hTs[hpar]
            HK = ("hT", hpar)
            if j > 0:
                r0 = (j - 1) * 2 * 1024
                S.dma("sp", lambda e, r0=r0: e.dma_start(out=hT[:, :, :],
                                                         in_=xrecv_d[r0:r0 + 1024, :].rearrange("(k p) t -> p k t", p=128)),
                      reads=[("xrecv", j - 1)], writes=[HK], lat=8.0)
            for n in range(NT):
                for half in range(2):
                    t0 = j * TB + n * 128
                    hs = n * 2 + half
                    tk, th = (hs % 4) // 2, hs % 2
                    tkey = ("tok32b" if th else "tok32", tk)
                    tsrc = tok32[:, tk, th * 512:th * 512 + 512]
                    fs = slice(half * 512, half * 512 + 512)
                    S.dma("sp", lambda e, t0=t0, tsrc=tsrc, fs=fs: e.dma_start(out=tsrc, in_=x_d[t0:t0 + 128, fs]), writes=[tkey])
                    b = nextb("mm")

                    def fn(e, tsrc=tsrc, b=b):
                        ins = None
                        for kk in range(4):
                            ins = e.transpose(out=banks[b][:, kk * 128:(kk + 1) * 128],
                                              in_=tsrc[:, kk * 128:(kk + 1) * 128], identity=ident_f[:, :])
                        return ins
                    A("pe", fn, reads=[tkey, "ident_f"], writes=[PK(b)], cost=0.5)
                    hreg = hT[:, half * 4:half * 4 + 4, n * 128:(n + 1) * 128]
                    pv = banks[b][:, :].rearrange("p (k t) -> p k t", k=4)
                    if j == 0:
                        A("dve", lambda e, hreg=hreg, pv=pv: e.tensor_scalar(out=hreg, in0=pv, scalar1=fa_c, scalar2=None, op0=ALU.mult),
                          reads=[PK(b), "vecs"], writes=[HK])
                    else:
                        A("dve", lambda e, hreg=hreg: e.tensor_scalar(out=hreg, in0=hreg, scalar1=fb_c, scalar2=None, op0=ALU.mult),
                          reads=[HK, "vecs"], writes=[HK])
                        A("dve", lambda e, hreg=hreg, pv=pv: e.scalar_tensor_tensor(out=hreg, in0=pv, scalar=fa_c, in1=hreg,
                                                                                    op0=ALU.mult, op1=ALU.add),
                          reads=[PK(b), HK, "vecs"], writes=[HK])
            gen_rope(j)
            for l in range(depth):
                do_block(l, j)
            if j + 1 < NB:
                S.dma("act", lambda e: e.dma_start(out=xsend_d[j * 1024:(j + 1) * 1024, :].rearrange("(k p) t -> p k t", p=128),
                                                   in_=hT[:, :, :]), reads=[HK], writes=[("xsend", j)], lat=8.0)
                S.coll(lambda e: e.collective_compute("AllGather", ALU.bypass, replica_groups=GROUPS,
                                                      ins=[xsend_d[j * 1024:(j + 1) * 1024, :]],
                                                      outs=[xrecv_d[j * 2048:(j + 1) * 2048, :]]),
                       reads=[("xsend", j)], writes=[("xrecv", j)])
            b7 = 7
            for k in range(8):
                s3 = k % 3
                A("act", lambda e, k=k, s3=s3: e.activation(out=sqr[:, s3, :], in_=hT[:, k, :], func=AF.Square),
                  reads=[HK], writes=[("sqr", s3)])
                A("pe", lambda e, k=k, s3=s3: e.matmul(banks[b7][:, 0:TB], lhsT=ones_b[:, :], rhs=sqr[:, s3, :],
                                                      start=(k == 0), stop=(k == 7)),
                  reads=[("sqr", s3), "ones_b"], writes=[PK(b7)])
            A("act", lambda e: e.activation(out=rstd[:, :], in_=banks[b7][:, 0:TB], func=AF.Ln, scale=1.0 / 1024, bias=EPS),
              reads=[PK(b7)], writes=["rstd"])
            A("act", lambda e: e.activation(out=rstd[:, :], in_=rstd[:, :], func=AF.Exp, scale=-0.5),
              reads=["rstd"], writes=["rstd"])
            A("dve", lambda e: e.tensor_scalar(out=rstd[:, :], in0=rstd[:, :], scalar1=ffin_c, scalar2=omf_c,
                                               op0=ALU.mult, op1=ALU.add), reads=["rstd", "vecs"], writes=["rstd"])
            for k in range(8):
                A("dve", lambda e, k=k: e.scalar_tensor_tensor(out=hT[:, k, :], in0=hT[:, k, :], scalar=vecs[:, VB + k:VB + k + 1],
                                                               in1=rstd[:, :], op0=ALU.mult, op1=ALU.mult),
                  reads=[HK, "rstd", "vecs"], writes=[HK])
            for n in range(NT):
                t0 = j * TB + n * 128
                for half in range(2):
                    b = nextb("mm")
                    fs = slice(half * 512, half * 512 + 512)

                    def fnf(e, n=n, half=half, b=b):
                        ins = None
                        for kk in range(4):
                            k = half * 4 + kk
                            ins = e.transpose(out=banks[b][:, kk * 128:(kk + 1) * 128],
                                              in_=hT[:, k, n * 128:(n + 1) * 128], identity=ident_f[:, :])
                        return ins
                    A("pe", fnf, reads=[HK, "ident_f"], writes=[PK(b)], cost=0.5)
                    A("act", lambda e, half=half, b=b: e.copy(out=raw[:, half, :], in_=banks[b][:, :]),
                      reads=[PK(b)], writes=[("raw", half)])
                    S.dma("act", lambda e, t0=t0, half=half, fs=fs: e.dma_start(out=out_d[t0:t0 + 128, fs], in_=raw[:, half, :]),
                          reads=[("raw", half)], writes=[("out", j, n, half)])

        for l in range(depth):
            cast_weights(l)
        for l in range(depth):
            do_layer(l)
        for j in range(NB):
            do_iter(j)
        A("pool", None, reads=[("out", j, n, hh) for j in range(NB) for n in range(NT) for hh in range(2)])
        S.emit(ctx)
        build_program.stats = S.stats
    return nc


_T, _TB = 4096, 512


def run_cores(inputs, T, TB, npairs, debug=False):
    ncores = 2 * npairs
    NB = T // TB + 1
    T9 = NB * TB
    nc = build_program(T, 2, TB, debug, ncores=ncores)
    wr = [prep_weights(inputs, [0, 1], 0, NB), prep_weights(inputs, [2, 3], 1, NB)]
    in_maps = []
    for c in range(ncores):
        b, r = c // 2, c % 2
        m = dict(wr[r])
        x = np.zeros((T9, 1024), np.float32)
        pos = np.asarray(inputs["positions"][b], np.int32).reshape(T)
        if r == 0:
            x[:T] = np.asarray(inputs["x"][b], np.float32)
            p9 = np.concatenate([pos, pos[-TB:]])
        else:
            p9 = np.concatenate([pos[:TB], pos])
        m["x"] = x
        m["mem"] = np.ascontiguousarray(np.asarray(inputs["mem"][b], np.float32))
        m["pos"] = np.ascontiguousarray(p9.reshape(1, T9))
        in_maps.append(m)
    res = run_bass_kernel_spmd(nc, in_maps, core_ids=list(range(ncores)))
    if debug:
        run_cores.dbg = [r["dbg"] for r in res.results]
    run_cores.raw = res.results
    return [np.asarray(res.results[2 * p + 1]["out"], np.float32)[TB:T9] for p in range(npairs)]


def kernel(**inputs):
    B = inputs["x"].shape[0]
    outs = run_cores(inputs, _T, _TB, B)
    return np.stack(outs, axis=0)
```
